# Optimizing a Trainium2 kernel written in Bass

```python
import math
import jax
import jax.numpy as jnp
from jax import lax
import numpy as np

D_MODEL = 1024
BATCH = 4
SEQ = 4096
DEPTH = 4
DEC_BATCH = 128
DEC_SEQ = 8
PAST_LEN = 2048
PAGE_SIZE = 128

D_MIX = D_MODEL
HEAD_DIM = 64
A_WIDTH = D_MIX // 4
A_CONV = 3
GDN_WIDTH = D_MIX // 4
GDN_HEADS = GDN_WIDTH // HEAD_DIM
GDN_DK = HEAD_DIM
GDN_DV = HEAD_DIM
GDN_QK = GDN_HEADS * GDN_DK
GDN_CONV = 4
GDN_CONV_CH = 2 * GDN_QK + GDN_WIDTH
GDN_CHUNK = 64
NSA_WIDTH = D_MIX // 2
NSA_HEADS = NSA_WIDTH // HEAD_DIM
NSA_KV_HEADS = 2
NSA_GROUP = NSA_HEADS // NSA_KV_HEADS
NSA_HD = HEAD_DIM
NSA_KV_DIM = NSA_KV_HEADS * NSA_HD
CMP_STRIDE = 16
CMP_BLOCK = 2 * CMP_STRIDE
SEL_BLOCK = 64
SEL_TOPN = 8
WINDOW = 512
Q_BLOCK = 128
D_FF = 2816
FFN_CONV = 3
ALPHA = (2.0 * DEPTH) ** 0.25
BETA_INIT = (8.0 * DEPTH) ** -0.25
LN_EPS = 1e-5
IN_SIZES = (A_WIDTH, A_WIDTH, A_WIDTH, GDN_QK, GDN_QK, GDN_WIDTH, GDN_WIDTH, GDN_HEADS, GDN_HEADS, NSA_WIDTH, 4 * NSA_KV_DIM, 2 * NSA_KV_DIM, 3 * NSA_HEADS)
N_IN = sum(IN_SIZES)

kernel_name = 'hybrid_conv_gdn_nsa_decoder_step'


def layer_norm(x, g, b):
    xf = x.astype(jnp.float32)
    mu = jnp.mean(xf, axis=-1, keepdims=True)
    var = jnp.mean(jnp.square(xf - mu), axis=-1, keepdims=True)
    y = (xf - mu) * lax.rsqrt(var + LN_EPS) * g.astype(jnp.float32) + b.astype(jnp.float32)
    return y.astype(x.dtype)


def l2_normalize(x):
    return x * lax.rsqrt(jnp.sum(jnp.square(x), axis=-1, keepdims=True) + 1e-6)


def gated_rms_norm(o, gain, gate):
    of = o.astype(jnp.float32)
    of = of * lax.rsqrt(jnp.mean(jnp.square(of), axis=-1, keepdims=True) + 1e-6)
    return (of * gain.astype(jnp.float32) * jax.nn.silu(gate.astype(jnp.float32))).astype(gate.dtype)


def masked_softmax(s, mask):
    s = jnp.where(mask, s.astype(jnp.float32), -jnp.inf)
    m = jnp.max(s, axis=-1, keepdims=True)
    m = jnp.where(jnp.isfinite(m), m, 0.0)
    p = jnp.exp(s - m)
    return p / jnp.maximum(jnp.sum(p, axis=-1, keepdims=True), 1e-30)


def causal_dwconv(x, buf, w):
    k = w.shape[0]
    t = x.shape[1]
    xp = jnp.concatenate([buf.astype(x.dtype), x], axis=1)
    y = sum(xp[:, i:i + t] * w[i] for i in range(k))
    return y, xp[:, t:]


def split_in(proj):
    return jnp.split(proj, np.cumsum(IN_SIZES)[:-1].tolist(), axis=-1)


def gated_delta_rule(q, k, v, a_in, b_in, s0, a_log, dt_bias):
    f32 = jnp.float32
    b, t, h, dk = q.shape
    dv = v.shape[-1]
    q = l2_normalize(q.astype(f32)) * (dk ** -0.5)
    k = l2_normalize(k.astype(f32))
    v = v.astype(f32)
    g = -jnp.exp(a_log.astype(f32)) * jax.nn.softplus(a_in.astype(f32) + dt_bias.astype(f32))
    beta = jax.nn.sigmoid(b_in.astype(f32))
    c = GDN_CHUNK if t % GDN_CHUNK == 0 else t
    n = t // c

    def to_chunks(z):
        return jnp.moveaxis(z.reshape((b, n, c) + z.shape[2:]), 3, 2)

    qc, kc, vc, gc, bc = (to_chunks(z) for z in (q, k, v, g, beta))
    gc = jnp.cumsum(gc, axis=-1)
    incl = jnp.tril(jnp.ones((c, c), dtype=bool))
    strict = jnp.tril(jnp.ones((c, c), dtype=bool), -1)
    diff = gc[..., :, None] - gc[..., None, :]
    decay = jnp.where(incl, jnp.exp(jnp.where(incl, diff, 0.0)), 0.0)
    kb = kc * bc[..., None]
    m = jnp.eye(c, dtype=f32) + jnp.where(strict, jnp.einsum('bnhik,bnhjk->bnhij', kb, kc) * decay, 0.0)
    rhs = jnp.concatenate([vc * bc[..., None], kb * jnp.exp(gc)[..., None]], axis=-1)
    sol = lax.linalg.triangular_solve(m, rhs, left_side=True, lower=True, unit_diagonal=True)
    u, w = sol[..., :dv], sol[..., dv:]
    a_qk = jnp.einsum('bnhik,bnhjk->bnhij', qc, kc) * decay
    q_dec = qc * jnp.exp(gc)[..., None]
    k_dec = kc * jnp.exp(gc[..., -1:] - gc)[..., None]
    g_last = jnp.exp(gc[..., -1])

    def step(state, xs):
        u_n, w_n, q_n, k_n, a_n, gl_n = xs
        v_new = u_n - jnp.einsum('bhik,bhkv->bhiv', w_n, state)
        o_n = jnp.einsum('bhik,bhkv->bhiv', q_n, state) + jnp.einsum('bhij,bhjv->bhiv', a_n, v_new)
        state = state * gl_n[..., None, None] + jnp.einsum('bhik,bhiv->bhkv', k_n, v_new)
        return state, o_n

    xs = tuple(jnp.moveaxis(z, 1, 0) for z in (u, w, q_dec, k_dec, a_qk, g_last))
    s_final, o = lax.scan(step, s0.astype(f32), xs)
    o = jnp.moveaxis(jnp.moveaxis(o, 0, 1), 2, 3).reshape(b, t, h, dv)
    return o, s_final


def compress_blocks(rows, pe, w1, b1, w2):
    b, t_pad, kvh, hd = rows.shape
    r = rows.reshape(b, t_pad // CMP_STRIDE, CMP_STRIDE, kvh, hd)
    blocks = jnp.concatenate([r[:, :-1], r[:, 1:]], axis=2) + pe[:, None, :]
    hid = jax.nn.gelu(jnp.einsum('bclhd,lde->bche', blocks, w1) + b1)
    return jnp.einsum('bche,ed->bchd', hid, w2)


def select_attend(qg, idx, q_pos, ks, vs):
    b, kvh = ks.shape[:2]
    tq = qg.shape[1]
    n = idx.shape[-1]
    b_ix = jnp.arange(b)[:, None, None, None]
    h_ix = jnp.arange(kvh)[None, :, None, None]
    kg = ks[b_ix, h_ix, idx].reshape(b, kvh, tq, n * SEL_BLOCK, NSA_HD)
    vg = vs[b_ix, h_ix, idx].reshape(b, kvh, tq, n * SEL_BLOCK, NSA_HD)
    k_pos = (idx[..., None] * SEL_BLOCK + jnp.arange(SEL_BLOCK)).reshape(b, kvh, 1, tq, n * SEL_BLOCK)
    s = jnp.einsum('bqhgd,bhqkd->bhgqk', qg, kg) * (NSA_HD ** -0.5)
    p = masked_softmax(s, k_pos <= q_pos[:, None])
    return jnp.einsum('bhgqk,bhqkd->bqhgd', p.astype(vg.dtype), vg)


def nsa_compressed_selected(qg, kv, q_pos, cmp_pe, cmp_w1, cmp_b1, cmp_w2):
    b, t = kv.shape[:2]
    tq = qg.shape[1]
    t_pad = -(-t // SEL_BLOCK) * SEL_BLOCK
    kv = jnp.pad(kv, ((0, 0), (0, t_pad - t), (0, 0), (0, 0), (0, 0)))
    kc = compress_blocks(kv[:, :, 0], cmp_pe[0], cmp_w1[0], cmp_b1[0], cmp_w2[0])
    vc = compress_blocks(kv[:, :, 1], cmp_pe[1], cmp_w1[1], cmp_b1[1], cmp_w2[1])
    nc = kc.shape[1]
    cmp_start = jnp.arange(nc) * CMP_STRIDE
    s = jnp.einsum('bqhgd,bchd->bhgqc', qg, kc) * (NSA_HD ** -0.5)
    p = masked_softmax(s, cmp_start[None, :] + (CMP_BLOCK - 1) <= q_pos[:, None])
    o_cmp = jnp.einsum('bhgqc,bchd->bqhgd', p.astype(vc.dtype), vc)
    ns = t_pad // SEL_BLOCK
    sel_start = jnp.arange(ns) * SEL_BLOCK
    overlap = ((cmp_start[:, None] < sel_start[None, :] + SEL_BLOCK) & (cmp_start[:, None] + CMP_BLOCK > sel_start[None, :])).astype(jnp.float32)
    imp = jnp.einsum('bhgqc,cn->bhqn', p, overlap)
    blk = jnp.arange(ns)[None, :]
    forced = (blk == 0) | (blk == q_pos[:, None] // SEL_BLOCK)
    valid = sel_start[None, :] <= q_pos[:, None]
    imp = jnp.where(forced, jnp.inf, jnp.where(valid, imp, -jnp.inf))
    n_top = min(SEL_TOPN, ns)
    _, idx = lax.top_k(imp, n_top)
    ks = jnp.moveaxis(kv[:, :, 2].reshape(b, ns, SEL_BLOCK, NSA_KV_HEADS, NSA_HD), 3, 1)
    vs = jnp.moveaxis(kv[:, :, 3].reshape(b, ns, SEL_BLOCK, NSA_KV_HEADS, NSA_HD), 3, 1)
    if tq % Q_BLOCK == 0:
        nb = tq // Q_BLOCK
        qb = jnp.moveaxis(qg.reshape(b, nb, Q_BLOCK, NSA_KV_HEADS, NSA_GROUP, NSA_HD), 1, 0)
        ib = jnp.moveaxis(idx.reshape(b, NSA_KV_HEADS, nb, Q_BLOCK, n_top), 2, 0)
        pb = q_pos.reshape(nb, Q_BLOCK)
        ob = lax.map(lambda a: select_attend(a[0], a[1], a[2], ks, vs), (qb, ib, pb))
        o_slc = jnp.moveaxis(ob, 0, 1).reshape(b, tq, NSA_KV_HEADS, NSA_GROUP, NSA_HD)
    else:
        o_slc = select_attend(qg, idx, q_pos, ks, vs)
    return o_cmp, o_slc


def window_banded(qg, kvw):
    b, t = kvw.shape[:2]
    nb = t // Q_BLOCK
    nw = WINDOW // Q_BLOCK
    kp = jnp.pad(kvw, ((0, 0), (WINDOW, 0), (0, 0), (0, 0), (0, 0))).reshape(b, nb + nw, Q_BLOCK, 2, NSA_KV_HEADS, NSA_HD)
    band = jnp.concatenate([kp[:, i:i + nb] for i in range(nw + 1)], axis=2)
    qb = qg.reshape(b, nb, Q_BLOCK, NSA_KV_HEADS, NSA_GROUP, NSA_HD)
    start = jnp.arange(nb)[:, None] * Q_BLOCK
    qpos = start + jnp.arange(Q_BLOCK)
    kpos = start - WINDOW + jnp.arange((nw + 1) * Q_BLOCK)
    qp, kp_ = qpos[:, :, None], kpos[:, None, :]
    mask = (kp_ <= qp) & (kp_ > qp - WINDOW) & (kp_ >= 0)
    s = jnp.einsum('bnqhgd,bnkhd->bnhgqk', qb, band[:, :, :, 0]) * (NSA_HD ** -0.5)
    p = masked_softmax(s, mask[None, :, None, None])
    o = jnp.einsum('bnhgqk,bnkhd->bnqhgd', p.astype(band.dtype), band[:, :, :, 1])
    return o.reshape(b, t, NSA_KV_HEADS, NSA_GROUP, NSA_HD)


def window_dense(qg, kvw_all, q_pos, k_pos):
    s = jnp.einsum('bqhgd,bkhd->bhgqk', qg, kvw_all[:, :, 0]) * (NSA_HD ** -0.5)
    mask = (k_pos[None, :] <= q_pos[:, None]) & (k_pos[None, :] > q_pos[:, None] - WINDOW)
    p = masked_softmax(s, mask)
    return jnp.einsum('bhgqk,bkhd->bqhgd', p.astype(kvw_all.dtype), kvw_all[:, :, 1])


def layer(x, q_pos, past, w_in, conv_a_w, gdn_conv_w, gdn_a_log, gdn_dt_bias, gdn_norm_g,
          cmp_pe, cmp_w1, cmp_b1, cmp_w2, w_out, ln1_g, ln1_b, w_up, ffn_conv_w, w_down, ln2_g, ln2_b):
    b, t, _ = x.shape
    if past is None:
        conv_a_buf = jnp.zeros((b, A_CONV - 1, A_WIDTH), x.dtype)
        gdn_conv_buf = jnp.zeros((b, GDN_CONV - 1, GDN_CONV_CH), x.dtype)
        s0 = jnp.zeros((b, GDN_HEADS, GDN_DK, GDN_DV), jnp.float32)
        ffn_buf = jnp.zeros((b, FFN_CONV - 1, D_FF), x.dtype)
        kv_past, win_buf = None, None
    else:
        conv_a_buf, gdn_conv_buf, s0, kv_past, win_buf, ffn_buf = past
    (a_b, a_c, a_h, g_q, g_k, g_v, g_gate, g_a, g_b, n_q, n_kv, n_win, n_gate) = split_in(x @ w_in)
    z, conv_a_new = causal_dwconv(a_c * a_h, conv_a_buf, conv_a_w)
    y_a = a_b * z
    qkv, gdn_conv_new = causal_dwconv(jnp.concatenate([g_q, g_k, g_v], axis=-1), gdn_conv_buf, gdn_conv_w)
    qkv = jax.nn.silu(qkv)
    q_b, k_b, v_b = jnp.split(qkv, [GDN_QK, 2 * GDN_QK], axis=-1)
    o_b, s_new = gated_delta_rule(q_b.reshape(b, t, GDN_HEADS, GDN_DK), k_b.reshape(b, t, GDN_HEADS, GDN_DK),
                                  v_b.reshape(b, t, GDN_HEADS, GDN_DV), g_a, g_b, s0, gdn_a_log, gdn_dt_bias)
    y_b = gated_rms_norm(o_b, gdn_norm_g, g_gate.reshape(b, t, GDN_HEADS, GDN_DV)).reshape(b, t, GDN_WIDTH)
    kv_new = n_kv.reshape(b, t, 4, NSA_KV_HEADS, NSA_HD)
    win_new = n_win.reshape(b, t, 2, NSA_KV_HEADS, NSA_HD)
    qg = n_q.reshape(b, t, NSA_KV_HEADS, NSA_GROUP, NSA_HD)
    kv_all = kv_new if kv_past is None else jnp.concatenate([kv_past, kv_new], axis=1)
    o_cmp, o_slc = nsa_compressed_selected(qg, kv_all, q_pos, cmp_pe, cmp_w1, cmp_b1, cmp_w2)
    if win_buf is None:
        o_win = window_banded(qg, win_new)
        win_state = win_new[:, t - min(WINDOW, t):]
    else:
        lw = win_buf.shape[1]
        win_all = jnp.concatenate([win_buf, win_new], axis=1)
        k_pos = (q_pos[0] - lw) + jnp.arange(lw + t)
        o_win = window_dense(qg, win_all, q_pos, k_pos)
        win_state = win_all[:, t:]
    gates = jax.nn.sigmoid(n_gate.reshape(b, t, 3, NSA_KV_HEADS, NSA_GROUP))[..., None]
    y_c = (gates[:, :, 0] * o_cmp + gates[:, :, 1] * o_slc + gates[:, :, 2] * o_win).reshape(b, t, NSA_WIDTH)
    mix = jnp.concatenate([y_a, y_b, y_c], axis=-1) @ w_out
    x = layer_norm(ALPHA * x + mix, ln1_g, ln1_b)
    gate, val = jnp.split(x @ w_up, 2, axis=-1)
    gate, ffn_new = causal_dwconv(gate, ffn_buf, ffn_conv_w)
    x = layer_norm(ALPHA * x + (jax.nn.silu(gate) * val) @ w_down, ln2_g, ln2_b)
    return x, (kv_new, win_state, conv_a_new, gdn_conv_new, s_new.astype(x.dtype), ffn_new)


def stack_layers(states, i):
    return jnp.stack([s[i] for s in states], axis=0)


def setup_inputs(seed: int = 0) -> dict:
    key = jax.random.key(seed)
    ks = jax.random.split(key, 32)
    f32 = jnp.float32
    n_pages = PAST_LEN // PAGE_SIZE
    n_used = DEC_BATCH * n_pages
    n_pool = n_used + max(1, n_used // 4)
    win_len = min(WINDOW, PAST_LEN)

    def nrm(k, shape, scale):
        return jax.random.normal(k, shape, f32) * scale

    dt = jnp.exp(jax.random.uniform(ks[14], (DEPTH, GDN_HEADS), f32, math.log(1e-3), math.log(1e-1)))
    return {
        'x_prompt': nrm(ks[0], (BATCH, SEQ, D_MODEL), 1.0),
        'x_sample': nrm(ks[1], (DEC_BATCH, DEC_SEQ, D_MODEL), 1.0),
        'cache_nsa_kv': nrm(ks[2], (DEPTH, n_pool, PAGE_SIZE, 4, NSA_KV_HEADS, NSA_HD), 1.0),
        'cache_nsa_win': nrm(ks[3], (DEPTH, DEC_BATCH, win_len, 2, NSA_KV_HEADS, NSA_HD), 1.0),
        'state_conv_a': nrm(ks[4], (DEPTH, DEC_BATCH, A_CONV - 1, A_WIDTH), 1.0),
        'state_gdn_conv': nrm(ks[5], (DEPTH, DEC_BATCH, GDN_CONV - 1, GDN_CONV_CH), 1.0),
        'state_gdn': nrm(ks[6], (DEPTH, DEC_BATCH, GDN_HEADS, GDN_DK, GDN_DV), 0.3),
        'state_ffn_conv': nrm(ks[7], (DEPTH, DEC_BATCH, FFN_CONV - 1, D_FF), 1.0),
        'page_table': jax.random.permutation(ks[8], n_pool)[:n_used].reshape(DEC_BATCH, n_pages).astype(jnp.int32),
        'ln_emb_g': 1.0 + nrm(ks[9], (D_MODEL,), 0.02),
        'ln_emb_b': nrm(ks[10], (D_MODEL,), 0.02),
        'w_in': nrm(ks[11], (DEPTH, D_MODEL, N_IN), D_MODEL ** -0.5),
        'conv_a_w': nrm(ks[12], (DEPTH, A_CONV, A_WIDTH), A_CONV ** -0.5),
        'gdn_conv_w': nrm(ks[13], (DEPTH, GDN_CONV, GDN_CONV_CH), GDN_CONV ** -0.5),
        'gdn_a_log': jnp.log(jax.random.uniform(ks[15], (DEPTH, GDN_HEADS), f32, 1.0, 16.0)),
        'gdn_dt_bias': jnp.log(jnp.expm1(dt)),
        'gdn_norm_g': 1.0 + nrm(ks[16], (DEPTH, GDN_DV), 0.02),
        'cmp_pe': nrm(ks[17], (DEPTH, 2, CMP_BLOCK, NSA_HD), 0.02),
        'cmp_w1': nrm(ks[18], (DEPTH, 2, CMP_BLOCK, NSA_HD, NSA_HD), (CMP_BLOCK * NSA_HD) ** -0.5),
        'cmp_b1': nrm(ks[19], (DEPTH, 2, NSA_HD), 0.02),
        'cmp_w2': nrm(ks[20], (DEPTH, 2, NSA_HD, NSA_HD), NSA_HD ** -0.5),
        'w_out': nrm(ks[21], (DEPTH, D_MIX, D_MODEL), BETA_INIT * D_MIX ** -0.5),
        'ln1_g': 1.0 + nrm(ks[22], (DEPTH, D_MODEL), 0.02),
        'ln1_b': nrm(ks[23], (DEPTH, D_MODEL), 0.02),
        'w_up': nrm(ks[24], (DEPTH, D_MODEL, 2 * D_FF), D_MODEL ** -0.5),
        'ffn_conv_w': nrm(ks[25], (DEPTH, FFN_CONV, D_FF), FFN_CONV ** -0.5),
        'w_down': nrm(ks[26], (DEPTH, D_FF, D_MODEL), BETA_INIT * D_FF ** -0.5),
        'ln2_g': 1.0 + nrm(ks[27], (DEPTH, D_MODEL), 0.02),
        'ln2_b': nrm(ks[28], (DEPTH, D_MODEL), 0.02),
    }


def reference(x_prompt, x_sample, cache_nsa_kv, cache_nsa_win, state_conv_a, state_gdn_conv, state_gdn,
              state_ffn_conv, page_table, ln_emb_g, ln_emb_b, w_in, conv_a_w, gdn_conv_w, gdn_a_log,
              gdn_dt_bias, gdn_norm_g, cmp_pe, cmp_w1, cmp_b1, cmp_w2, w_out, ln1_g, ln1_b, w_up,
              ffn_conv_w, w_down, ln2_g, ln2_b):
    dec_b, dec_t = x_sample.shape[:2]
    past_len = page_table.shape[1] * cache_nsa_kv.shape[2]
    pos_p = jnp.arange(x_prompt.shape[1], dtype=jnp.int32)
    pos_s = past_len + jnp.arange(dec_t, dtype=jnp.int32)
    xp = layer_norm(x_prompt, ln_emb_g, ln_emb_b)
    xs = layer_norm(x_sample, ln_emb_g, ln_emb_b)
    st_p, st_s = [], []
    for l in range(DEPTH):
        wl = (w_in[l], conv_a_w[l], gdn_conv_w[l], gdn_a_log[l], gdn_dt_bias[l], gdn_norm_g[l],
              cmp_pe[l], cmp_w1[l], cmp_b1[l], cmp_w2[l], w_out[l], ln1_g[l], ln1_b[l],
              w_up[l], ffn_conv_w[l], w_down[l], ln2_g[l], ln2_b[l])
        kv_past = cache_nsa_kv[l, page_table].reshape(dec_b, past_len, 4, NSA_KV_HEADS, NSA_HD)
        past = (state_conv_a[l], state_gdn_conv[l], state_gdn[l], kv_past, cache_nsa_win[l], state_ffn_conv[l])
        xp, sp = layer(xp, pos_p, None, *wl)
        xs, ss = layer(xs, pos_s, past, *wl)
        st_p.append(sp)
        st_s.append(ss)
    return (xp, xs,
            stack_layers(st_p, 0), stack_layers(st_s, 0),
            stack_layers(st_p, 1), stack_layers(st_s, 1),
            stack_layers(st_p, 2), stack_layers(st_s, 2),
            stack_layers(st_p, 3), stack_layers(st_s, 3),
            stack_layers(st_p, 4), stack_layers(st_s, 4),
            stack_layers(st_p, 5), stack_layers(st_s, 5))
```

```python
import numpy as np
import os
from contextlib import ExitStack
import concourse.bass as bass
import concourse.mybir as mybir
from concourse.bass_utils import run_bass_kernel_spmd

F32 = mybir.dt.float32
BF16 = mybir.dt.bfloat16
I32 = mybir.dt.int32
AF = mybir.ActivationFunctionType
ALU = mybir.AluOpType
AX = mybir.AxisListType

D = 1024
DFF = 2816
NIN = 3104
ALPHA = (2.0 * 4) ** 0.25
EPS = 1e-5
O_AB, O_AC, O_AH, O_GQ, O_GK, O_GV, O_GG, O_GA, O_GB, O_NQ, O_NKV, O_NWIN, O_NG = (
    0, 256, 512, 768, 1024, 1280, 1536, 1792, 1796, 1800, 2312, 2824, 3080)


class Cfg:
    def __init__(self, T=4096, L=4, NS=16, TS=8, PAST=2048, NPOOL=2560, mix_b=True, mix_c=True):
        self.T, self.L, self.NS, self.TS, self.PAST, self.NPOOL = T, L, NS, TS, PAST, NPOOL
        self.NT = T // 128
        self.mix_b, self.mix_c = mix_b, mix_c


class Buf:
    def __init__(self, t, name):
        self.t = t
        self.name = name
        self.w = None
        self.r = {}

    def __getitem__(self, key):
        return V(self, self.t[key])


class V:
    def __init__(self, buf, ap):
        self.buf = buf
        self.ap = ap

    def __getitem__(self, key):
        return V(self.buf, self.ap[key])

    def re(self, pat, **kw):
        return V(self.buf, self.ap.rearrange(pat, **kw))

    def bc(self, shape):
        return V(self.buf, self.ap.to_broadcast(list(shape)))


def _aps(x):
    return x.ap if isinstance(x, V) else x


class _HeadView:
    def __init__(self, v, h):
        self.v, self.h = v, h

    def __getitem__(self, key):
        k = list(key)
        if isinstance(k[2], int):
            k[2] = 0
        return self.v[tuple(k)]


class Ctx:
    def __init__(self, nc, es, n_dma_sems=14):
        self.nc, self.es = nc, es
        self.eng = {'pe': nc.tensor, 'act': nc.scalar, 'dve': nc.vector, 'pool': nc.gpsimd, 'sp': nc.sync}
        self.sem = {k: es.enter_context(nc.semaphore('s_' + k)) for k in self.eng}
        self.cnt = {k: 0 for k in self.eng}
        self.seen = {k: {} for k in self.eng}
        self.dsem = [es.enter_context(nc.semaphore('d%d' % i)) for i in range(n_dma_sems)]
        self.dcnt = [0] * n_dma_sems
        self.dnext = {'sp': 0, 'pool': 0, 'act': 0}
        self.dpool = {'sp': list(range(0, n_dma_sems - 5)), 'pool': list(range(n_dma_sems - 5, n_dma_sems)), 'act': []}
        self.nbuf = 0
        self.banks = []
        self.bnext = 0
        self.ninstr = 0

    def sb(self, shape, dt=F32, name=None, es=None):
        self.nbuf += 1
        name = (name or 'b') + str(self.nbuf)
        esz = 2 if dt == BF16 else 4
        n = 1
        for d_ in shape[1:]:
            n *= d_
        npad = -(-(n * esz) // 64) * 64 // esz
        t = (es or self.es).enter_context(self.nc.sbuf_tensor(name, [shape[0], npad], dt))
        ap = t[:, 0:n]
        if len(shape) > 2:
            names = ['d%d' % i for i in range(len(shape) - 1)]
            pat = 'p (' + ' '.join(names) + ') -> p ' + ' '.join(names)
            ap = ap.rearrange(pat, **{nm: sz for nm, sz in zip(names[:-1], shape[1:-1])})
        return Buf(ap, name)

    def barrier(self):
        for e in self.eng:
            for k in self.eng:
                if k != e and self.cnt[k] > 0:
                    self._wait(e, k, self.cnt[k])
            for j in range(len(self.dsem)):
                if self.dcnt[j] > 0:
                    self._wait(e, j, self.dcnt[j])

    def mkbanks(self):
        for i in range(8):
            t = self.es.enter_context(self.nc.psum_tensor('bank%d' % i, [128, 512], F32))
            self.banks.append(Buf(t, 'bank%d' % i))

    def bank(self):
        if not hasattr(self, 'rot'):
            self.rot = list(self.banks)
        b = self.rot.pop(0)
        self.rot.append(b)
        return b

    def hold(self):
        if not hasattr(self, 'rot'):
            self.rot = list(self.banks)
        return self.rot.pop(0)

    def release(self, b):
        self.rot.append(b)

    def dram(self, ap, name):
        class _T:
            pass
        b = Buf(None, name)
        b.t = ap
        return b

    def _semof(self, key):
        return self.sem[key] if isinstance(key, str) else self.dsem[key]

    def _wait(self, e, key, val):
        if key == e and e == 'pe':
            return
        if self.seen[e].get(key, 0) >= val:
            return
        self.eng[e].wait_ge(self._semof(key), val)
        self.seen[e][key] = val
        self.ninstr += 1

    def _deps(self, e, reads, writes):
        for b in reads:
            if b.w is not None:
                self._wait(e, *b.w)
        for b in writes:
            if b.w is not None:
                self._wait(e, *b.w)
            for k, v in b.r.items():
                self._wait(e, k, v)

    def _mark(self, ev, reads, writes):
        for b in reads:
            b.r[ev[0]] = ev[1]
        for b in writes:
            b.w = ev
            b.r = {}

    def op(self, e, fn, reads=(), writes=(), signal=True):
        reads = [x.buf if isinstance(x, V) else x for x in reads]
        writes = [x.buf if isinstance(x, V) else x for x in writes]
        self._deps(e, reads, writes)
        ins = fn(self.eng[e])
        self.ninstr += 1
        if signal or e != 'pe':
            self.cnt[e] += 1
            ins.then_inc(self.sem[e], 1)
            ev = (e, self.cnt[e])
        else:
            ev = (e, self.cnt[e] + 1)
        self._mark(ev, reads, writes)
        return ins

    def dma(self, e, out, in_, slow=False):
        pl = self.dpool[e]
        j = pl[self.dnext[e]]
        self.dnext[e] = (self.dnext[e] + 1) % len(pl)
        if self.dcnt[j] > 0:
            self._wait(e, j, self.dcnt[j])
        self._deps(e, [in_.buf], [out.buf])
        self.dcnt[j] += 16
        if slow:
            ins = self.eng[e].dma_start(out=out.ap, in_=in_.ap, allow_slow_non_contiguous=True)
        else:
            ins = self.eng[e].dma_start(out=out.ap, in_=in_.ap)
        ins.then_inc(self.dsem[j], 16)
        self.ninstr += 1
        self._mark((j, self.dcnt[j]), [in_.buf], [out.buf])
        return ins

    def dma_custom(self, e, fn, reads, writes):
        pl = self.dpool[e]
        j = pl[self.dnext[e]]
        self.dnext[e] = (self.dnext[e] + 1) % len(pl)
        if self.dcnt[j] > 0:
            self._wait(e, j, self.dcnt[j])
        self._deps(e, reads, writes)
        self.dcnt[j] += 16
        ins = fn(self.eng[e])
        ins.then_inc(self.dsem[j], 16)
        self.ninstr += 1
        self._mark((j, self.dcnt[j]), reads, writes)
        return ins

    def mm(self, out, lhsT, rhs, start=True, stop=True, signal=None):
        if signal is None:
            signal = stop
        return self.op('pe', lambda e: e.matmul(out.ap, lhsT=lhsT.ap, rhs=rhs.ap, start=start, stop=stop),
                       [lhsT, rhs], [out], signal=signal)

    def tr(self, out, in_, ident, signal=True):
        return self.op('pe', lambda e: e.transpose(out.ap, in_.ap, ident.ap), [in_, ident], [out], signal=signal)

    def act(self, out, in_, func, bias=None, scale=None, accum=None, e='act'):
        kw = {}
        rd = [in_]
        if bias is not None:
            kw['bias'] = _aps(bias)
            if isinstance(bias, V):
                rd.append(bias)
        if scale is not None:
            kw['scale'] = _aps(scale)
            if isinstance(scale, V):
                rd.append(scale)
        wr = [out]
        if accum is not None:
            kw['accum_out'] = accum.ap
            wr.append(accum)
        return self.op('act', lambda e_: e_.activation(out.ap, in_.ap, func, **kw), rd, wr)

    def copy(self, e, out, in_):
        if e == 'act':
            return self.op('act', lambda e_: e_.copy(out.ap, in_.ap), [in_], [out])
        return self.op(e, lambda e_: e_.tensor_copy(out.ap, in_.ap), [in_], [out])

    def tt(self, e, out, in0, in1, op):
        return self.op(e, lambda e_: e_.tensor_tensor(out.ap, in0.ap, in1.ap, op), [in0, in1], [out])

    def ts(self, e, out, in0, s1, op0, s2=None, op1=None, accum=None):
        rd = [in0] + [s for s in (s1, s2) if isinstance(s, V)]
        wr = [out] + ([accum] if accum is not None else [])
        kw = {}
        if accum is not None:
            kw['accum_out'] = accum.ap
        if op1 is None:
            return self.op(e, lambda e_: e_.tensor_scalar(out.ap, in0.ap, _aps(s1), None, op0, **kw), rd, wr)
        return self.op(e, lambda e_: e_.tensor_scalar(out.ap, in0.ap, _aps(s1), _aps(s2), op0, op1, **kw), rd, wr)

    def stt(self, out, in0, s, in1, op0, op1):
        rd = [in0, in1] + ([s] if isinstance(s, V) else [])
        return self.op('dve', lambda e_: e_.scalar_tensor_tensor(out.ap, in0.ap, _aps(s), in1.ap, op0, op1), rd, [out])

    def memset(self, e, out, val):
        return self.op(e, lambda e_: e_.memset(out.ap, val), [], [out])

    def finish(self):
        for j in range(len(self.dsem)):
            if self.dcnt[j] > 0:
                self._wait('sp', j, self.dcnt[j])


def gdn_consts(nseq, TS):
    i = np.arange(128)
    seq = i // TS if nseq > 1 else np.zeros(128, np.int64)
    same = seq[:, None] == seq[None, :]
    lowi = same & (i[:, None] >= i[None, :])
    lows = same & (i[:, None] > i[None, :])
    cg = np.zeros((128, 19, 128), np.float32)
    cg[:, 0], cg[:, 1], cg[:, 2], cg[:, 3], cg[:, 4] = lowi, lows, lowi.T, lows.T, same
    for k in range(7):
        b = 2 << k
        lm = same & (i[:, None] // b == i[None, :] // b) & ((i[:, None] % b) >= b // 2) & ((i[None, :] % b) < b // 2)
        cg[:, 5 + k] = lm
        cg[:, 12 + k] = lm.T
    rowm = np.zeros((128, 16), np.float32)
    rowm[i, seq] = 1.0
    colm = np.ascontiguousarray(np.broadcast_to(rowm.T[None], (64, 16, 128))).astype(np.float32)
    return cg, rowm, colm


def nsa_consts(cfg):
    T, NT = cfg.T, cfg.NT
    c_ = np.arange(128)
    n = np.arange(64)
    ov = np.zeros((128, 2, 65), np.float32)
    for ct in range(2):
        cc = (ct * 128 + c_)[:, None] * 16
        ov[:, ct, :64] = (cc < n[None, :] * 64 + 64) & (cc + 32 > n[None, :] * 64)
        ov[:, ct, 64] = 1.0
    cq = (16.0 * c_[:, None] - c_[None, :]).astype(np.float32)
    eall = (np.arange(T)[None, :] // 64 == n[:, None]).astype(np.float32)
    qpos = np.arange(T)
    valid = (n[None, :] * 64 <= qpos[:, None])
    forced = (n[None, :] == 0) | (n[None, :] == qpos[:, None] // 64)
    mult = (valid & ~forced).astype(np.float32).reshape(NT, 128, 64)
    bias = np.where(forced, 1e9, np.where(valid, 0.0, -1e9)).astype(np.float32).reshape(NT, 128, 64)
    return ov, cq, eall, mult, bias


def build(cfg):
    T, L, NS, TS, NTILES = cfg.T, cfg.L, cfg.NS, cfg.TS, cfg.NT
    nc = bass.Bass("TRN2", target_bir_lowering=False)
    es = ExitStack()

    def din(name, shape, dt=F32):
        return nc.dram_tensor(name, list(shape), dt, kind="ExternalInput").ap()

    def dout(name, shape, dt=F32):
        return nc.dram_tensor(name, list(shape), dt, kind="ExternalOutput").ap()

    def dscr(name, shape, dt=F32):
        return nc.dram_tensor(name, list(shape), dt, kind="Internal").ap()

    I = {}
    I['x_p'] = din('x_p', [T, D]); I['x_s'] = din('x_s', [128, D])
    I['st_conv_a'] = din('st_conv_a', [L, NS * 2, 256])
    I['st_gdn_conv'] = din('st_gdn_conv', [L, NS * 3, 768])
    I['st_gdn'] = din('st_gdn', [L, NS, 4, 64, 64])
    I['st_ffn_conv'] = din('st_ffn_conv', [L, NS * 2, DFF])
    I['cache_win'] = din('cache_win', [L, NS, min(512, cfg.PAST), 256])
    for v_ in ('p', 's'):
        I['cg_' + v_] = din('cg_' + v_, [128, 19, 128]); I['rowm_' + v_] = din('rowm_' + v_, [128, 16]); I['colm_' + v_] = din('colm_' + v_, [64, 16, 128])
    I['gdn_a_log'] = din('gdn_a_log', [L, 4]); I['gdn_dt_bias'] = din('gdn_dt_bias', [L, 4]); I['gdn_norm_g'] = din('gdn_norm_g', [L, 64])
    NPG = cfg.PAST // 128
    I['cache_kv'] = din('cache_kv', [L * cfg.NPOOL * 128 * 2, 256])
    I['page_table'] = din('page_table', [1, NS * NPG], I32)
    I['eall_s'] = din('eall_s', [64, cfg.PAST + 128]); I['impm_s'] = din('impm_s', [128, 64]); I['impb_s'] = din('impb_s', [128, 64])
    I['smask'] = din('smask', [128, 16])
    I['ov'] = din('ov', [128, 2, 65]); I['cq'] = din('cq', [128, 128]); I['eall'] = din('eall', [64, T])
    I['impm'] = din('impm', [NTILES, 128, 64]); I['impb'] = din('impb', [NTILES, 128, 64])
    I['cmp_pe'] = din('cmp_pe', [L, 2, 32, 64]); I['cmp_w1'] = din('cmp_w1', [L, 2, 32, 64, 64]); I['cmp_b1'] = din('cmp_b1', [L, 2, 64])
    I['cmp_w2'] = din('cmp_w2', [L, 2, 64, 64])
    I['ln_emb'] = din('ln_emb', [2, D])
    I['w_in'] = din('w_in', [L, D, NIN]); I['w_out'] = din('w_out', [L, D, D])
    I['w_up'] = din('w_up', [L, D, 2 * DFF]); I['w_down'] = din('w_down', [L, DFF, D])
    I['conv_a_w'] = din('conv_a_w', [L, 3, 256]); I['gdn_conv_w'] = din('gdn_conv_w', [L, 4, 768])
    I['ffn_conv_w'] = din('ffn_conv_w', [L, 3, DFF])
    I['ln1'] = din('ln1', [L, 2, D]); I['ln2'] = din('ln2', [L, 2, D])
    O = {}
    if os.environ.get('DBG'):
        O['dbg2'] = dout('dbg2', [128, 2048]); O['dbg3'] = dout('dbg3', [128, 1024], BF16)
    O['y_p'] = dout('y_p', [T, D]); O['y_s'] = dout('y_s', [128, D])
    O['ca_p'] = dout('ca_p', [L, 2, 256]); O['ca_s'] = dout('ca_s', [L, NS * 2, 256])
    O['gc_p'] = dout('gc_p', [L, 3, 768]); O['gc_s'] = dout('gc_s', [L, NS * 3, 768])
    O['fc_p'] = dout('fc_p', [L, 2, DFF]); O['fc_s'] = dout('fc_s', [L, NS * 2, DFF])
    O['kv_p'] = dout('kv_p', [L, T, 512]); O['kv_s'] = dout('kv_s', [L, 128, 512])
    O["win_p"] = dout("win_p", [L, min(512, T), 256]); O["win_s"] = dout("win_s", [L, NS, min(512, cfg.PAST), 256])
    O['gs_p'] = dout('gs_p', [L, 4, 64, 64]); O['gs_s'] = dout('gs_s', [L, NS, 4, 64, 64])
    xs0 = dscr('xs0', [NTILES + 1, 128, D]); xs1 = (dout if os.environ.get('DBG') else dscr)('xs1', [NTILES + 1, 128, D])

    with es:
        c = Ctx(nc, es)
        c.mkbanks()
        Dm = {k: c.dram(v, k) for k, v in list(I.items()) + list(O.items())}
        XS0 = [c.dram(xs0[i], 'xs0_%d' % i) for i in range(NTILES + 1)]
        XS1 = [c.dram(xs1[i], 'xs1_%d' % i) for i in range(NTILES + 1)]

        identf = c.sb([128, 128], F32, 'identf')
        c.memset('pool', identf[:], 0.0)
        c.op('pool', lambda e: e.affine_select(out=identf.t[:], in_=identf.t[:], pattern=[[-1, 128]],
                                                compare_op=ALU.not_equal, fill=1.0, base=0, channel_multiplier=1),
             [identf], [identf])

        identb = c.sb([128, 128], BF16, 'identb')
        c.copy('dve', identb[:], identf[:])
        piota = c.sb([128, 1], I32, 'piota')
        c.op('pool', lambda e: e.iota(piota.t[:], pattern=[[0, 1]], base=0, channel_multiplier=1), [], [piota])
        ones128 = c.sb([128, 128], F32, 'ones128')
        c.memset('pool', ones128[:], 1.0)
        lnG = c.sb([128, 2, D], F32, 'lnG')
        xt = [c.sb([128, D], F32, 'xt')] * 2
        xr = c.sb([128, D], F32, 'xr')
        xT = c.sb([128, 8, 128], BF16, 'xT')
        st6 = c.sb([128, 2, 6], F32, 'st6'); mv = c.sb([128, 2], F32, 'mv'); rstd = c.sb([128, 1], F32, 'rstd')
        rows = c.sb([48, 512], F32, 'rows')
        z1 = c.sb([128, 128], F32, 'z1'); z2 = c.sb([128, 128], F32, 'z2')

        def layer_norm(src, dst, gb):
            for k in range(2):
                c.op('dve', lambda e: e.bn_stats(st6.t[:, k, :], src.ap[:, k * 512:(k + 1) * 512]), [src], [st6])
            c.op('dve', lambda e: e.bn_aggr(mv.t[:], st6.t[:]), [st6], [mv])
            c.ts('dve', rstd[:], mv[:, 1:2], EPS, ALU.add)
            c.act(rstd[:], rstd[:], AF.Sqrt)
            c.op('dve', lambda e: e.reciprocal(rstd.t[:], rstd.t[:]), [rstd], [rstd])
            c.ts('dve', dst, src, mv[:, 0:1], ALU.subtract, rstd[:, 0:1], ALU.mult)
            c.tt('pool', dst, dst, gb[:, 0, :], ALU.mult)
            c.tt('pool', dst, dst, gb[:, 1, :], ALU.add)

        def make_xT(src):
            for h in range(2):
                pb = c.bank()
                for k in range(4):
                    c.tr(pb[:, k * 128:(k + 1) * 128], src[:, (h * 4 + k) * 128:(h * 4 + k + 1) * 128], identf[:], signal=(k == 3))
                c.copy('act', xT[:, h * 4:(h + 1) * 4, :], pb[:].re('p (k t) -> p k t', k=4))

        def proj_chunk(w, col, m, pb, n0=0):
            for k in range(8):
                c.mm(pb[0:m, n0:n0 + 128], w[:, k, col:col + m], xT[:, k, :], start=(k == 0), stop=(k == 7))

        def load_state_T(dst, nch, dram_rows, R):
            for c0 in range(0, nch, 4):
                n = min(4, nch - c0)
                c.dma('sp', rows[0:R, 0:n * 128], dram_rows[:, c0 * 128:(c0 + n) * 128])
                for ch in range(n):
                    pb = c.bank()
                    c.tr(pb[:, 0:R], rows[0:R, ch * 128:(ch + 1) * 128], identf[0:R, 0:R])
                    d = dst(c0 + ch)
                    c.copy('dve', d, pb[:, 0:R].re('p (s j) -> p s j', s=d.ap.shape[1]))

        def store_state_T(src, nch, dram_rows, R):
            for c0 in range(0, nch, 4):
                n = min(4, nch - c0)
                for ch in range(n):
                    pb = c.bank()
                    sv = src(c0 + ch)
                    c.copy('dve', z1[:, 0:R].re('p (s j) -> p s j', s=sv.ap.shape[1]), sv)
                    c.tr(pb[0:R, 0:128], z1[:, 0:R], identf[:])
                    c.copy('act', rows[0:R, ch * 128:(ch + 1) * 128], pb[0:R, 0:128])
                c.dma('sp', dram_rows[:, c0 * 128:(c0 + n) * 128], rows[0:R, 0:n * 128])

        def conv_taps(dst, ch, cidx, wts, K, TT):
            c.ts('dve', dst, ch[:, cidx, :, 0:TT], wts[:, cidx, 0:1], ALU.mult)
            for i in range(1, K):
                c.stt(dst, ch[:, cidx, :, i:i + TT], wts[:, cidx, i:i + 1], dst, ALU.mult, ALU.add)

        def v3(v, nseq):
            return v.re('p (s t) -> p s t', s=nseq)


        def gdn_tile(l, i, samp, G, wA, qkvT, yT):
            nseq = NS if samp else 1
            nlev = 3 if samp else 7
            vn = 's' if samp else 'p'
            if i == 0 or samp:
                c.dma('sp', G['cg'][:], Dm['cg_' + vn][:]); c.dma('sp', G['rowm'][:], Dm['rowm_' + vn][:])
                if samp:
                    c.dma('pool', G['colm'][:], Dm['colm_' + vn][:])
            cg = G['cg']
            LOWI, LOWS, UPI, UPS, BLK = (cg[:, j, :] for j in range(5))
            S_full = G['S']
            if (not samp) and i == 0:
                c.memset('pool', S_full[:], 0.0)
            qkv = G['qkv_tm']
            for ch in range(6):
                pb = c.bank()
                c.tr(pb[:, 0:128], qkvT[:, ch, :], identf[:])
                c.copy('act', qkv[:, ch * 128:(ch + 1) * 128], pb[:, 0:128])
            pb = c.bank()
            for k in range(8):
                c.mm(pb[:, 0:8], xT[:, k, :], wA[:, k, O_GA:O_GA + 8], start=(k == 0), stop=(k == 7))
            c.copy('act', G['ab'][:], pb[:, 0:8])
            pb = c.bank()
            for k in range(8):
                c.mm(pb[:, 0:256], xT[:, k, :], wA[:, k, O_GG:O_GG + 256], start=(k == 0), stop=(k == 7))
            c.act(G['gate'][:], pb[:, 0:256], AF.Silu)
            t4, g4, beta4, gc4, egc4, egd4, bgc4 = (G[n] for n in ('t4', 'g4', 'beta4', 'gc4', 'egc4', 'egd4', 'bgc4'))
            c.tt('dve', t4[:], G['ab'][:, 0:4], G['dtb'][:], ALU.add)
            c.act(t4[:], t4[:], AF.Exp)
            c.ts('dve', t4[:], t4[:], 1.0, ALU.add)
            c.act(t4[:], t4[:], AF.Ln)
            c.tt('dve', g4[:], t4[:], G['negA'][:], ALU.mult)
            c.act(beta4[:], G['ab'][:, 4:8], AF.Sigmoid)
            pb = c.bank()
            c.mm(pb[:, 0:4], UPI, g4[:])
            c.mm(pb[:, 4:8], BLK, g4[:])
            c.copy('act', gc4[:], pb[:, 0:4])
            c.act(egc4[:], gc4[:], AF.Exp)
            c.tt('dve', egd4[:], pb[:, 4:8], gc4[:], ALU.subtract)
            c.act(egd4[:], egd4[:], AF.Exp)
            c.tt('dve', bgc4[:], beta4[:], egc4[:], ALU.mult)
            sc = G['sc']
            ss = G['ss']
            for hh in range(2):
                c.tt('dve', sc[:], qkv[:, hh * 256:(hh + 1) * 256].re('p (h d) -> p h d', h=4), qkv[:, hh * 256:(hh + 1) * 256].re('p (h d) -> p h d', h=4), ALU.mult)
                c.op('dve', lambda e: e.reduce_sum(ss.t[:, hh * 4:(hh + 1) * 4], sc.t[:], axis=AX.X), [sc], [ss])
            c.ts('dve', ss[:], ss[:], 1e-6, ALU.add)
            c.act(ss[:], ss[:], AF.Sqrt)
            c.op('dve', lambda e: e.reciprocal(ss.t[:], ss.t[:]), [ss], [ss])
            c.ts('dve', ss[:, 0:4], ss[:, 0:4], 0.125, ALU.mult)
            for hh in range(2):
                c.tt('dve', qkv[:, hh * 256:(hh + 1) * 256].re('p (h d) -> p h d', h=4), qkv[:, hh * 256:(hh + 1) * 256].re('p (h d) -> p h d', h=4),
                     ss[:, hh * 4:(hh + 1) * 4].re('p (h o) -> p h o', o=1).bc([128, 4, 64]), ALU.mult)
            for h in range(4):
                Qh, Kh, Vh = qkv[:, h * 64:(h + 1) * 64], qkv[:, 256 + h * 64:256 + (h + 1) * 64], qkv[:, 512 + h * 64:512 + (h + 1) * 64]
                hs = slice(h, h + 1)
                if samp:
                    c.dma('sp', S_full[:, :, 0, :], Dm['st_gdn'][l, :, h].re('s k v -> k s v'))
                    S = _HeadView(S_full, h)
                else:
                    S = S_full
                Vb, Kbg, Kd, Qg = G['t1'][:, 0:64], G['t1'][:, 64:128], G['t2'][:, 0:64], G['t2'][:, 64:128]
                c.ts('dve', Qg, Qh, egc4[:, hs], ALU.mult)
                pb = c.bank()
                c.tr(pb[0:64, 0:128], Kh, identf[:])
                c.tr(pb[0:64, 128:256], Qh, identf[:])
                c.tr(pb[0:64, 256:384], Qg, identf[:])
                kqT = G['kqT']
                c.copy('act', kqT[:], pb[0:64, 0:384].re('p (a t) -> p a t', a=3))
                KT, QT, QgT = kqT[:, 0, :], kqT[:, 1, :], kqT[:, 2, :]
                dg = G['dg']
                c.ts('pool', dg[:, 0, :], identf[:], gc4[:, hs], ALU.mult)
                c.ts('pool', dg[:, 1, :], identf[:], beta4[:, hs], ALU.mult)
                pbG = c.bank()
                c.mm(pbG[:, 0:128], ones128[:], dg[:, 0, :])
                c.mm(pbG[:, 128:256], ones128[:], dg[:, 1, :])
                pbK = c.bank()
                c.mm(pbK[:, 0:128], KT, KT)
                c.mm(pbK[:, 128:256], KT, QT)
                t3 = G['t3']
                A, AT, AqkT = G['A'], G['AT'], G['AqkT']
                c.ts('dve', t3[:], pbG[:, 0:128], gc4[:, hs], ALU.subtract, -1.0, ALU.mult)
                c.tt('pool', t3[:], t3[:], LOWI, ALU.mult)
                c.act(t3[:], t3[:], AF.Exp)
                c.tt('pool', t3[:], t3[:], LOWS, ALU.mult)
                c.stt(A[:], pbK[:, 0:128], beta4[:, hs], t3[:], ALU.mult, ALU.mult)
                c.ts('dve', t3[:], pbG[:, 0:128], gc4[:, hs], ALU.subtract)
                c.tt('pool', t3[:], t3[:], UPI, ALU.mult)
                c.act(t3[:], t3[:], AF.Exp)
                c.tt('pool', t3[:], t3[:], UPI, ALU.mult)
                c.tt('dve', AqkT[:], pbK[:, 128:256], t3[:], ALU.mult)
                c.tt('pool', t3[:], t3[:], UPS, ALU.mult)
                c.tt('dve', t3[:], pbG[:, 128:256], t3[:], ALU.mult)
                c.tt('dve', AT[:], pbK[:, 0:128], t3[:], ALU.mult)
                c.ts('pool', Vb, Vh, beta4[:, hs], ALU.mult)
                c.ts('pool', Kbg, Kh, bgc4[:, hs], ALU.mult)
                c.ts('pool', Kd, Kh, egd4[:, hs], ALU.mult)
                Db, DTb = [G['D0'], G['D1']], [G['DT0'], G['DT1']]
                X, XT = G['X'], G['XT']
                c.tt('pool', X[:], A[:], cg[:, 5, :], ALU.mult)
                c.tt('pool', Db[0][:], identf[:], X[:], ALU.subtract)
                c.tt('pool', XT[:], AT[:], cg[:, 12, :], ALU.mult)
                c.tt('pool', DTb[0][:], identf[:], XT[:], ALU.subtract)
                cur = 0
                for k in range(1, nlev):
                    last = (k == nlev - 1)
                    Dc, DTc, Dn, DTn = Db[cur], DTb[cur], Db[1 - cur], DTb[1 - cur]
                    c.tt('pool', X[:], A[:], cg[:, 5 + k, :], ALU.mult)
                    c.tt('pool', XT[:], AT[:], cg[:, 12 + k, :], ALU.mult)
                    if not last:
                        pb = c.bank()
                        c.mm(pb[:, 0:128], XT[:], Dc[:])
                        c.copy('act', G['Ys'][:], pb[:, 0:128])
                        c.mm(pb[:, 128:256], DTc[:], G['Ys'][:])
                        c.tt('dve', Dn[:], Dc[:], pb[:, 128:256], ALU.subtract)
                    pb2 = c.bank()
                    c.mm(pb2[:, 0:128], X[:], DTc[:])
                    c.copy('act', G['Y2s'][:], pb2[:, 0:128])
                    c.mm(pb2[:, 128:256], Dc[:], G['Y2s'][:])
                    c.tt('dve', DTn[:], DTc[:], pb2[:, 128:256], ALU.subtract)
                    cur = 1 - cur
                TT_ = DTb[cur]
                pb = c.bank()
                c.mm(pb[0:64, 0:128], Kbg, TT_[:])
                negWT = G['negWT']
                c.ts('dve', negWT[:], pb[0:64, 0:128], -1.0, ALU.mult)
                Vnew = G['Vnew']
                bc16 = lambda v_, n_: v_.re('p (o t) -> p o t', o=1).bc([64, n_, 128])
                pbV = c.bank()
                c.mm(pbV[:, 0:64], TT_[:], Vb, start=True, stop=False)
                if nseq > 1:
                    for c4 in range(nseq // 4):
                        c.tt('pool', G['negWTm'][:], bc16(negWT[:], 4), G['colm'][:, c4 * 4:(c4 + 1) * 4, :], ALU.mult)
                        for s4 in range(4):
                            s_ = c4 * 4 + s4
                            c.mm(pbV[:, 0:64], G['negWTm'][:, s4, :], S[:, s_, h, :], start=False, stop=(s_ == nseq - 1), signal=True)
                else:
                    c.mm(pbV[:, 0:64], negWT[:], S[:, 0, h, :], start=False, stop=True)
                c.copy('act', Vnew[:], pbV[:, 0:64])
                pbO = c.bank()
                if nseq > 1:
                    for c4 in range(nseq // 4):
                        c.tt('pool', G['QgTm'][:], bc16(QgT, 4), G['colm'][:, c4 * 4:(c4 + 1) * 4, :], ALU.mult)
                        for s4 in range(4):
                            s_ = c4 * 4 + s4
                            c.mm(pbO[:, 0:64], G['QgTm'][:, s4, :], S[:, s_, h, :], start=(s_ == 0), stop=False, signal=True)
                else:
                    c.mm(pbO[:, 0:64], QgT, S[:, 0, h, :], start=True, stop=False)
                c.mm(pbO[:, 0:64], AqkT[:], Vnew[:], start=False, stop=True)
                c.copy('act', G['o_tm'][:, h * 64:(h + 1) * 64], pbO[:, 0:64])
                c.ts('pool', G['grow'][:], G['rowm'][:], g4[:, hs], ALU.mult)
                pbE = c.bank()
                c.mm(pbE[0:64, 0:16], ones128[:, 0:64], G['grow'][:])
                c.act(G['egl'][:], pbE[0:64, 0:16], AF.Exp)
                for s0 in range(0, nseq, 8):
                    n8 = min(8, nseq - s0)
                    pbS = c.bank()
                    if nseq > 1:
                        c.tt('pool', G['Kdm'][:], Kd.re('p (o d) -> p o d', o=1).bc([128, 8, 64]),
                             G['rowm'][:, s0:s0 + 8].re('p (s o) -> p s o', o=1).bc([128, 8, 64]), ALU.mult)
                    for s_ in range(s0, s0 + n8):
                        kd_ = G['Kdm'][:, s_ - s0, :] if nseq > 1 else Kd
                        c.mm(pbS[0:64, (s_ - s0) * 64:(s_ - s0 + 1) * 64], kd_, Vnew[:])
                    c.tt('dve', S[:, s0:s0 + n8, h, :], S[:, s0:s0 + n8, h, :],
                         G['egl'][:, s0:s0 + n8].re('p (s o) -> p s o', o=1).bc([64, n8, 64]), ALU.mult)
                    c.tt('dve', S[:, s0:s0 + n8, h, :], S[:, s0:s0 + n8, h, :], pbS[0:64, 0:n8 * 64].re('p (s v) -> p s v', s=n8), ALU.add)
                if samp:
                    c.dma('sp', Dm['gs_s'][l, :, h].re('s k v -> k s v'), S_full[:, :, 0, :])
            o_tm, rr = G['o_tm'], G['rr']
            c.tt('dve', sc[:], o_tm[:].re('p (h d) -> p h d', h=4), o_tm[:].re('p (h d) -> p h d', h=4), ALU.mult)
            c.op('dve', lambda e: e.reduce_sum(rr.t[:], sc.t[:], axis=AX.X), [sc], [rr])
            c.ts('dve', rr[:], rr[:], 1.0 / 64, ALU.mult, 1e-6, ALU.add)
            c.act(rr[:], rr[:], AF.Sqrt)
            c.op('dve', lambda e: e.reciprocal(rr.t[:], rr.t[:]), [rr], [rr])
            c.tt('dve', o_tm[:].re('p (h d) -> p h d', h=4), o_tm[:].re('p (h d) -> p h d', h=4),
                 rr[:].re('p (h o) -> p h o', o=1).bc([128, 4, 64]), ALU.mult)
            c.tt('dve', o_tm[:].re('p (h d) -> p h d', h=4), o_tm[:].re('p (h d) -> p h d', h=4),
                 G['gain'][:].re('p (o d) -> p o d', o=1).bc([128, 4, 64]), ALU.mult)
            c.tt('dve', o_tm[:], o_tm[:], G['gate'][:], ALU.mult)
            if samp and os.environ.get('DBG'):
                pass
            pb = c.bank()
            for cc in range(2):
                c.tr(pb[:, cc * 128:(cc + 1) * 128], o_tm[:, cc * 128:(cc + 1) * 128], identf[:])
            c.copy('act', yT[:, 2:4, :], pb[:, 0:256].re('p (a t) -> p a t', a=2))
            if (not samp) and i == NTILES - 1:
                c.dma('sp', Dm['gs_p'][l].re('h k v -> k h v'), S_full[:, 0, :, :])


        def proj_q(wA, N):
            for h2_ in range(2):
                pb = c.bank()
                for hl in range(2):
                    hh = h2_ * 2 + hl
                    for g in range(2):
                        c0 = O_NQ + g * 256 + hh * 64 - g * 64
                        r = (hl * 2 + g) * 128
                        for k in range(8):
                            c.mm(pb[:, r:r + 128], wA[:, k, c0:c0 + 128], xT[:, k, :], start=(k == 0), stop=(k == 7))
                for g in range(2):
                    ps = slice(g * 64, (g + 1) * 64)
                    src = pb[ps, :].re('p (hl g t) -> p hl g t', hl=2, g=2)[:, :, g, :]
                    c.copy('act', N['qT'][ps, h2_ * 2:h2_ * 2 + 2, :], src)

        def nsa_setup(l, N):
            for half in range(2):
                if os.environ.get('SKIP_W1'):
                    continue
                for a8 in range(8):
                    c.dma('pool', N['w1b'][half * 64:(half + 1) * 64, a8 * 8:(a8 + 1) * 8, :],
                          Dm['cmp_w1'][l].re('a l d e -> d (a l) e')[:, a8 * 8:(a8 + 1) * 8, :])
            c.dma('pool', N['w2b'][:], Dm['cmp_w2'][l].re('a e d -> e a d'))
            c.dma('pool', N['ov'][:], Dm['ov'][:])
            c.dma('sp', N['cq'][:], Dm['cq'][:])
            c.memset('pool', N['w2pad'][:], 0.0)
            c.copy('pool', N['w2pad'][:, 0, 0:64], N['w2b'][:, 0, :])
            c.copy('pool', N['w2pad'][:, 1, 64:128], N['w2b'][:, 0, :])
            for a in range(2):
                c.dma('sp', N['peT'][:, a, :], Dm['cmp_pe'][l, a].re('l d -> d l'), slow=True)
            c.dma('sp', N['b1T'][:], Dm['cmp_b1'][l].re('a e -> e a'), slow=True)
            c.copy('dve', N['peTb'][:], N['peT'][:])
            pb = c.bank()
            for a in range(2):
                for l_ in range(32):
                    c.mm(pb[0:64, a:a + 1], N['w1b'][0:64, a * 32 + l_, :], N['peTb'][:, a, l_:l_ + 1], start=(l_ == 0), stop=(l_ == 31))
            c.tt('dve', N['biasv'][:], pb[0:64, 0:2], N['b1T'][:], ALU.add)
            c.memset('pool', N['kcT'][:], 0.0)
            c.memset('pool', N['hidvT'][:], 0.0)
            c.memset('pool', N['VC1'][:], 1.0)

        def nsa_tile_p(l, i, N, G, wA, kvtm, yT):
            cg = G['cg']
            LOWS, UPI = cg[:, 1, :], cg[:, 2, :]
            proj_q(wA, N)
            for col, dst in ((O_NKV + 256, N['kslcT'][:, i * 128:(i + 1) * 128]), (O_NWIN, N['kwinT'][:, (i % 8) * 128:(i % 8 + 1) * 128]),
                             (O_NKV, N['cmpT'][:, 0, 16:144]), (O_NKV + 128, N['cmpT'][:, 1, 16:144])):
                pb = c.bank()
                proj_chunk(wA, col, 128, pb)
                c.copy('act', dst, pb[:, 0:128])
            c.copy('pool', N['vslc1'][:, i, :, 0:64], kvtm[:, 384:512].re('p (g d) -> p g d', g=2))
            c.copy('pool', N['vwin1'][:, i % 8, :, 0:64], kvtm[:, 640:768].re('p (g d) -> p g d', g=2))
            nb, col0, c0 = (7, 16, 0) if i == 0 else (8, 0, 8 * i - 1)
            hv = lambda t_: t_[0:64, 0:16].re('p (g j) -> p g j', g=2)[:, :, 0:nb]
            hx, h2, hb = N['hx'], N['h2'], N['hb']
            for a in range(2):
                for g in range(2):
                    ps = slice(g * 64, (g + 1) * 64)
                    pbh = c.bank()
                    for l_ in range(32):
                        rhs = N['cmpT'][ps, a, col0 + l_:col0 + l_ + 16 * (nb - 1) + 1:16]
                        c.mm(pbh[0:64, 0:nb], N['w1b'][ps, a * 32 + l_, :], rhs, start=(l_ == 0), stop=(l_ == 31))
                    c.ts('dve', hx[0:64, g * 8:g * 8 + nb], pbh[0:64, 0:nb], N['biasv'][:, a:a + 1], ALU.add)
                c.tt('dve', hv(h2), hv(hx), hv(hx), ALU.mult)
                c.ts('dve', hv(h2), hv(h2), 0.044715, ALU.mult, 1.0, ALU.add)
                c.tt('dve', hv(h2), hv(h2), hv(hx), ALU.mult)
                c.act(hv(h2), hv(h2), AF.Tanh, scale=0.7978845608028654)
                c.ts('dve', hv(h2), hv(h2), 1.0, ALU.add, 0.5, ALU.mult)
                c.tt('dve', hv(hb), hv(h2), hv(hx), ALU.mult)
                if a == 0:
                    pbk = c.bank()
                    for g in range(2):
                        c.mm(pbk[:, g * 8:g * 8 + nb], N['w2pad'][:, g, :], hb[:, g * 8:g * 8 + nb])
                    for g in range(2):
                        c.copy('act', N['kcT'][g * 64:(g + 1) * 64, c0:c0 + nb], pbk[g * 64:(g + 1) * 64, g * 8:g * 8 + nb])
                else:
                    c.copy('act', N['hidvT'][:, :, c0:c0 + nb], hv(hb))
            c.copy('pool', N['cmpT'][:, :, 0:16], N['cmpT'][:, :, 128:144])
            nct = 1 if 8 * i + 6 < 128 else 2
            for ct in range(nct):
                pbv = c.bank()
                for g in range(2):
                    c.mm(pbv[:, g * 64:(g + 1) * 64], N['hidvT'][:, g, ct * 128:(ct + 1) * 128], N['w2b'][:, 1, :])
                c.copy('act', N['VC1'][:, ct, :, 0:64], pbv[:, 0:128].re('p (g d) -> p g d', g=2))
            pbg = c.bank()
            for k in range(8):
                c.mm(pbg[:, 0:24], xT[:, k, :], wA[:, k, O_NG:O_NG + 24], start=(k == 0), stop=(k == 7))
            c.act(N['gate'][:], pbg[:, 0:24], AF.Sigmoid)
            c.dma('sp', N['impm'][:], Dm['impm'][i])
            c.dma('sp', N['impb'][:], Dm['impb'][i])
            Ebufs = [N['E0'], N['E1']]
            bc4 = lambda v_: v_.re('p (o t) -> p o t', o=1).bc([128, 4, 128])
            for g in range(2):
                ps = slice(g * 64, (g + 1) * 64)
                qg = N['qT'][ps, :, :].re('p h t -> p (h t)')

                def attend(kts, lhs_of, v1_of, mask_of, pbO, extra=None):
                    for idx, kt in enumerate(kts):
                        pbS = c.bank()
                        c.mm(pbS[:, :], lhs_of(kt), qg)
                        Eb = Ebufs[idx % 2]
                        c.act(Eb[:], pbS[:, :].re('p (h t) -> p h t', h=4), AF.Exp, scale=0.125)
                        mask_of(kt, Eb)
                        for hh in range(4):
                            if extra is not None:
                                extra(kt, idx, hh, Eb)
                            c.mm(pbO[:, hh * 65:(hh + 1) * 65], Eb[:, hh, :], v1_of(kt), start=(idx == 0 and hh == 0), stop=(idx == len(kts) - 1 and hh == 3))

                def finalize(pbO, br, first):
                    o3 = pbO[:, 0:260].re('p (h e) -> p h e', h=4)
                    rs = N['rs']
                    c.ts('dve', rs[:], o3[:, :, 64], 1e-30, ALU.max)
                    c.op('dve', lambda e: e.reciprocal(rs.t[:], rs.t[:]), [rs], [rs])
                    c.tt('dve', rs[:], rs[:], N['gate'][:, br * 8 + g * 4:br * 8 + g * 4 + 4], ALU.mult)
                    dst = N['ytm'][:, g * 256:(g + 1) * 256].re('p (h d) -> p h d', h=4)
                    rb = rs[:].re('p (h o) -> p h o', o=1).bc([128, 4, 64])
                    if first:
                        c.tt('dve', dst, o3[:, :, 0:64], rb, ALU.mult)
                    else:
                        c.tt('dve', N['tmp'][:], o3[:, :, 0:64], rb, ALU.mult)
                        c.tt('pool', dst, dst, N['tmp'][:], ALU.add)

                pbI = c.hold()
                pbOc = c.hold()

                def mask_cmp(ct, Eb):
                    thr = float(128 * i - 31 - 2048 * ct)
                    c.op('pool', lambda e: e.tensor_single_scalar(N['cm'].t[:], N['cq'].t[:], thr, ALU.is_le), [N['cq']], [N['cm']])
                    c.tt('pool', Eb[:], Eb[:], bc4(N['cm'][:]), ALU.mult)

                def extra_imp(ct, idx, hh, Eb):
                    c.mm(pbI[:, hh * 65:(hh + 1) * 65], Eb[:, hh, :], N['ov'][:, ct, :], start=(idx == 0 and hh == 0), stop=(idx == nct - 1 and hh == 3))

                attend(list(range(nct)), lambda ct: N['kcT'][ps, ct * 128:(ct + 1) * 128], lambda ct: N['VC1'][:, ct, g, :], mask_cmp, pbOc, extra_imp)
                i3 = pbI[:, 0:260].re('p (h e) -> p h e', h=4)
                rs2, imp = N['rs2'], N['imp']
                c.ts('dve', rs2[:], i3[:, :, 64], 1e-30, ALU.max)
                c.op('dve', lambda e: e.reciprocal(rs2.t[:], rs2.t[:]), [rs2], [rs2])
                c.ts('dve', imp[:], i3[:, 0, 0:64], rs2[:, 0:1], ALU.mult)
                for hh in range(1, 4):
                    c.stt(imp[:], i3[:, hh, 0:64], rs2[:, hh:hh + 1], imp[:], ALU.mult, ALU.add)
                c.tt('dve', imp[:], imp[:], N['impm'][:], ALU.mult)
                c.tt('dve', imp[:], imp[:], N['impb'][:], ALU.add)
                c.op('dve', lambda e: e.max(N['m8'].t[:], imp.t[:]), [imp], [N['m8']])
                c.ts('dve', N['sel'][:], imp[:], N['m8'][:, 7:8], ALU.is_ge)
                pbT = c.bank()
                c.tr(pbT[0:64, 0:128], N['sel'][:], identf[:])
                c.copy('act', N['selT'][:], pbT[0:64, 0:128])
                finalize(pbOc, 0, True)
                c.release(pbI)
                c.release(pbOc)
                pbO = c.hold()

                def mask_slc(kt, Eb):
                    pbM = c.bank()
                    c.mm(pbM[:, 0:128], N['eall'][:, kt * 128:(kt + 1) * 128], N['selT'][:])
                    c.tt('dve', Eb[:], Eb[:], bc4(pbM[:, 0:128]), ALU.mult)
                    if kt == i:
                        c.tt('pool', Eb[:], Eb[:], bc4(UPI), ALU.mult)

                attend(list(range(0, i + 1)), lambda kt: N['kslcT'][ps, kt * 128:(kt + 1) * 128], lambda kt: N['vslc1'][:, kt, g, :], mask_slc, pbO)
                finalize(pbO, 1, False)
                c.release(pbO)
                pbO = c.hold()

                def mask_win(kt, Eb):
                    if kt == i:
                        c.tt('pool', Eb[:], Eb[:], bc4(UPI), ALU.mult)
                    if kt == i - 4:
                        c.tt('pool', Eb[:], Eb[:], bc4(LOWS), ALU.mult)

                attend(list(range(max(0, i - 4), i + 1)), lambda kt: N['kwinT'][ps, (kt % 8) * 128:(kt % 8 + 1) * 128],
                       lambda kt: N['vwin1'][:, kt % 8, g, :], mask_win, pbO)
                finalize(pbO, 2, False)
                c.release(pbO)
            pb = c.bank()
            for cc in range(4):
                c.tr(pb[:, cc * 128:(cc + 1) * 128], N['ytm'][:, cc * 128:(cc + 1) * 128], identf[:])
            c.copy('act', yT[:, 4:8, :], pb[:].re('p (a t) -> p a t', a=4))


        def nsa_tile_s(l, N, G, wA, kvtm, yT):
            P_ = cfg.PAST
            npg = P_ // 128
            WS = min(512, P_)
            nwt = WS // 128
            nblk = P_ // 16 - 1
            cgS = G['cg']
            c.dma('sp', N['ptb'][:], V(Dm['page_table'], I['page_table'].partition_broadcast(128)).re('p o n -> p (o n)'))
            c.ts('dve', N['idxA'][:], N['ptb'][:], float(l * cfg.NPOOL), ALU.add, 128.0, ALU.mult)
            c.ts('dve', N['idxA'][:], N['idxA'][:], piota[:, 0:1], ALU.add, 2.0, ALU.mult)
            c.ts('dve', N['idxB'][:], N['idxA'][:], 1.0, ALU.add)
            c.dma('pool', N['eall_s'][:], Dm['eall_s'][:])
            c.dma('sp', N['smask'][:], Dm['smask'][:])
            c.dma('sp', N['impm'][:], Dm['impm_s'][:])
            c.dma('sp', N['impb'][:], Dm['impb_s'][:])
            c.memset('pool', N['Ebig'][:], 0.0)
            c.memset('pool', N['VCs'][:], 1.0)
            c.memset('pool', N['vs1'][:], 1.0)
            c.memset('pool', N['vw1'][:], 1.0)
            c.memset('pool', N['v1n'][:], 1.0)
            proj_q(wA, N)
            c.copy('pool', N['qTs'][:].re('p s h t -> p h s t'), N['qT'][:].re('p h (s t) -> p h s t', s=NS))
            for col, dst in ((O_NKV + 256, N['kslcTn']), (O_NWIN, N['kwinTn'])):
                pb = c.bank()
                proj_chunk(wA, col, 128, pb)
                c.copy('act', dst[:], pb[:, 0:128])
            pbg = c.bank()
            for k in range(8):
                c.mm(pbg[:, 0:24], xT[:, k, :], wA[:, k, O_NG:O_NG + 24], start=(k == 0), stop=(k == 7))
            c.act(N['gate'][:], pbg[:, 0:24], AF.Sigmoid)

            def gather(idx, s_, h0, n8):
                for j in range(n8):
                    col = s_ * npg + h0 + j
                    def fn(e, j=j, col=col):
                        return e.indirect_dma_start(out=N['pk'].t[:, j, :], out_offset=None, in_=I['cache_kv'],
                                                    in_offset=bass.IndirectOffsetOnAxis(ap=idx.t[:, col:col + 1], axis=0))
                    c.dma_custom('pool', fn, [Dm['cache_kv'], idx], [N['pk']])

            def transpose_pages(src_of, dst, ntile):
                for j0 in range(0, ntile, 8):
                    n8 = min(8, ntile - j0)
                    pb = c.bank()
                    pbb = V(pb, pb.t[:].bitcast(BF16))
                    for j in range(j0, j0 + n8):
                        c.tr(pbb[:, (j - j0) * 128:(j - j0 + 1) * 128], src_of(j), identb[:], signal=(j == j0 + n8 - 1))
                    c.copy('act', dst[:, j0 * 128:(j0 + n8) * 128], pbb[:, 0:n8 * 128])

            def ocols(pbO, s_):
                return pbO[0:65, s_ * 32:(s_ + 1) * 32]

            def finalize_s(pbO, g, br, first):
                c.copy('act', N['osb'][:].re('p (h s t) -> p s h t', h=4, s=NS), pbO[0:65, :].re('p (s h t) -> p s h t', s=NS, h=4))
                pbt = c.bank()
                for hh in range(4):
                    c.tr(pbt[:, hh * 65:(hh + 1) * 65], N['osb'][:, hh * 128:(hh + 1) * 128], identf[0:65, 0:65])
                o3 = pbt[:, 0:260].re('p (h e) -> p h e', h=4)
                rs = N['rs']
                c.ts('dve', rs[:], o3[:, :, 64], 1e-30, ALU.max)
                c.op('dve', lambda e: e.reciprocal(rs.t[:], rs.t[:]), [rs], [rs])
                c.tt('dve', rs[:], rs[:], N['gate'][:, br * 8 + g * 4:br * 8 + g * 4 + 4], ALU.mult)
                dst = N['ytm'][:, g * 256:(g + 1) * 256].re('p (h d) -> p h d', h=4)
                rb = rs[:].re('p (h o) -> p h o', o=1).bc([128, 4, 64])
                if first:
                    c.tt('dve', dst, o3[:, :, 0:64], rb, ALU.mult)
                else:
                    c.tt('dve', N['tmp'][:], o3[:, :, 0:64], rb, ALU.mult)
                    c.tt('pool', dst, dst, N['tmp'][:], ALU.add)

            hx, h2, hb = N['hxs'], N['h2s'], N['hbs']
            hv = lambda t_: t_[:, :, 0:nblk]
            pbOc = [c.hold(), c.hold()]
            if os.environ.get('SWAPB'):
                pbOc = pbOc[::-1]
            for s_ in range(NS):
                for h0 in range(0, npg, 8):
                    n8 = min(8, npg - h0)
                    gather(N['idxA'], s_, h0, n8)
                    for a in range(2):
                        transpose_pages(lambda j: N['pk'][:, j, a * 128:(a + 1) * 128], N['cmpTs'][:, a, h0 * 128:(h0 + n8) * 128], n8)
                for a in range(2):
                    for g in range(2):
                        ps = slice(g * 64, (g + 1) * 64)
                        pbh = c.bank()
                        for l_ in range(32):
                            rhs = N['cmpTs'][ps, a, l_:l_ + 16 * (nblk - 1) + 1:16]
                            c.mm(pbh[0:64, 0:nblk], N['w1b'][ps, a * 32 + l_, :], rhs, start=(l_ == 0), stop=(l_ == 31))
                        c.ts('dve', hx[:, g, 0:nblk], pbh[0:64, 0:nblk], N['biasv'][:, a:a + 1], ALU.add)
                    c.tt('dve', hv(h2), hv(hx), hv(hx), ALU.mult)
                    c.ts('dve', hv(h2), hv(h2), 0.044715, ALU.mult, 1.0, ALU.add)
                    c.tt('dve', hv(h2), hv(h2), hv(hx), ALU.mult)
                    c.act(hv(h2), hv(h2), AF.Tanh, scale=0.7978845608028654)
                    c.ts('dve', hv(h2), hv(h2), 1.0, ALU.add, 0.5, ALU.mult)
                    c.tt('dve', hv(hb), hv(h2), hv(hx), ALU.mult)
                    if a == 0:
                        pbk = c.bank()
                        for g in range(2):
                            c.mm(pbk[:, g * 128:g * 128 + nblk], N['w2pad'][:, g, :], hb[:, g, 0:nblk])
                        for g in range(2):
                            c.copy('act', N['kcTs'][g * 64:(g + 1) * 64, 0:nblk], pbk[g * 64:(g + 1) * 64, g * 128:g * 128 + nblk])
                    else:
                        pbv = c.bank()
                        for g in range(2):
                            c.mm(pbv[0:nblk, g * 64:(g + 1) * 64], hb[:, g, 0:nblk], N['w2b'][:, 1, :])
                        c.copy('act', N['VCs'][0:nblk, :, 0:64], pbv[0:nblk, 0:128].re('p (g d) -> p g d', g=2))
                if os.environ.get('DBG') and s_ == NS - 1:
                    dst_ = N['dbgst']
                    c.memset('dve', dst_[:], 0.0)
                    c.copy('dve', dst_[:, 0:nblk], N['kcTs'][:, 0:nblk])
                    c.copy('dve', dst_[0:nblk, 128:258], N['VCs'][0:nblk, :, :].re('p g e -> p (g e)'))
                    c.copy('dve', dst_[0:64, 260:260 + 2 * nblk].re('p (g j) -> p g j', g=2), hb[:, :, 0:nblk])
                    c.copy('dve', dst_[:, 300:332], N['qTs'][:, s_, :, :].re('p h t -> p (h t)'))
                    c.copy('dve', dst_[:, 0:512], N['w1b'][:, 0:8, :].re('p a e -> p (a e)'))
                    c.dma('sp', Dm['dbg2'][:, 1280:1280 + 640], dst_[:])
                for g in range(2):
                    ps = slice(g * 64, (g + 1) * 64)
                    qsel = N['qTs'][ps, s_, :, :].re('p h t -> p (h t)')
                    pbS = c.bank()
                    c.mm(pbS[0:nblk, 0:32], N['kcTs'][ps, 0:nblk], qsel)
                    c.act(N['Ecur'][0:nblk, :], pbS[0:nblk, 0:32], AF.Exp, scale=0.125)
                    c.copy('pool', N['Ebig'][0:nblk, g, :, s_ * 8:(s_ + 1) * 8], N['Ecur'][0:nblk, :].re('p (h t) -> p h t', h=4))
                    c.mm(ocols(pbOc[g], s_), N['VCs'][0:nblk, g, :], N['Ecur'][0:nblk, :])
            for g in range(2):
                pbI = c.bank()
                for hh in range(4):
                    c.mm(pbI[:, hh * 65:(hh + 1) * 65], N['Ebig'][:, g, hh, :], N['ov'][:, 0, :], start=(hh == 0), stop=(hh == 3))
                i3 = pbI[:, 0:260].re('p (h e) -> p h e', h=4)
                rs2, imp = N['rs2'], N['imp']
                c.ts('dve', rs2[:], i3[:, :, 64], 1e-30, ALU.max)
                c.op('dve', lambda e: e.reciprocal(rs2.t[:], rs2.t[:]), [rs2], [rs2])
                c.ts('dve', imp[:], i3[:, 0, 0:64], rs2[:, 0:1], ALU.mult)
                for hh in range(1, 4):
                    c.stt(imp[:], i3[:, hh, 0:64], rs2[:, hh:hh + 1], imp[:], ALU.mult, ALU.add)
                c.tt('dve', imp[:], imp[:], N['impm'][:], ALU.mult)
                c.tt('dve', imp[:], imp[:], N['impb'][:], ALU.add)
                c.op('dve', lambda e: e.max(N['m8'].t[:], imp.t[:]), [imp], [N['m8']])
                c.ts('dve', N['sel'][:], imp[:], N['m8'][:, 7:8], ALU.is_ge)
                pbT = c.bank()
                c.tr(pbT[0:64, 0:128], N['sel'][:], identf[:])
                c.copy('act', N['selTs'][:, g, :], pbT[0:64, 0:128])
                finalize_s(pbOc[g], g, 0, True)
                c.release(pbOc[g])
                if os.environ.get('DBG'):
                    c.dma('sp', Dm['dbg2'][:, g * 256:(g + 1) * 256], N['ytm'][:, g * 256:(g + 1) * 256])
                    c.dma('sp', Dm['dbg2'][:, 1024 + g * 64:1024 + (g + 1) * 64], N['sel'][:])
            pbOs = [c.hold(), c.hold()]
            pbOw = [c.hold(), c.hold()]
            kTs = N['cmpTs'][:, 0, :]
            for s_ in range(NS):
                for h0 in range(0, npg, 8):
                    n8 = min(8, npg - h0)
                    gather(N['idxB'], s_, h0, n8)
                    transpose_pages(lambda j: N['pk'][:, j, 0:128], kTs[:, h0 * 128:(h0 + n8) * 128], n8)
                    c.copy('pool', N['vs1'][:, h0:h0 + n8, :, 0:64], N['pk'][:, 0:n8, 128:256].re('p j (g d) -> p j g d', g=2))
                c.dma('pool', N['pw'][:], Dm['cache_win'][l, s_].re('(j r) f -> r j f', r=128))
                transpose_pages(lambda j: N['pw'][:, j, 0:128], N['kwTs'][:, :], nwt)
                c.copy('pool', N['vw1'][:, :, :, 0:64], N['pw'][:, :, 128:256].re('p j (g d) -> p j g d', g=2))
                pbn = c.bank()
                c.mm(pbn[0:8, 0:128], identf[:, s_ * 8:(s_ + 1) * 8], kvtm[:, 384:512])
                c.mm(pbn[0:8, 128:256], identf[:, s_ * 8:(s_ + 1) * 8], kvtm[:, 640:768])
                c.copy('act', N['v1n'][:, :, :, 0:64], pbn[0:8, 0:256].re('p (b g d) -> p b g d', b=2, g=2))
                for g in range(2):
                    ps = slice(g * 64, (g + 1) * 64)
                    qsel = N['qTs'][ps, s_, :, :].re('p h t -> p (h t)')
                    for br, (KT_, V1_, KTn, ntile, pbO) in enumerate(((kTs, N['vs1'], N['kslcTn'], npg, pbOs[g]),
                                                                      (N['kwTs'][:, :], N['vw1'], N['kwinTn'], nwt, pbOw[g]))):
                        first = True
                        for j0 in range(0, ntile, 8):
                            n8 = min(8, ntile - j0)
                            pbS = c.bank()
                            for j in range(j0, j0 + n8):
                                c.mm(pbS[:, (j - j0) * 32:(j - j0 + 1) * 32], KT_[ps, j * 128:(j + 1) * 128], qsel)
                            Es = N['Es']
                            c.act(Es[:, 0:n8 * 32], pbS[:, 0:n8 * 32], AF.Exp, scale=0.125)
                            E4 = Es[:, 0:n8 * 32].re('p (j h t) -> p j h t', j=n8, h=4)
                            if br == 0:
                                pbM = c.bank()
                                for j in range(j0, j0 + n8):
                                    c.mm(pbM[:, (j - j0) * 8:(j - j0 + 1) * 8], N['eall_s'][:, j * 128:(j + 1) * 128], N['selTs'][:, g, s_ * 8:(s_ + 1) * 8])
                                for hh in range(4):
                                    c.tt('dve', E4[:, :, hh, :], E4[:, :, hh, :], pbM[:, 0:n8 * 8].re('p (j t) -> p j t', j=n8), ALU.mult)
                            elif j0 == 0 and WS == 512:
                                c.tt('pool', E4[:, 0, :, :], E4[:, 0, :, :], N['smask'][:, 0:8].re('p (o t) -> p o t', o=1).bc([128, 4, 8]), ALU.mult)
                            for j in range(j0, j0 + n8):
                                c.mm(ocols(pbO, s_), V1_[:, j, g, :], Es[:, (j - j0) * 32:(j - j0 + 1) * 32], start=first, stop=False)
                                first = False
                        pbS = c.bank()
                        c.mm(pbS[0:8, 0:32], KTn[ps, s_ * 8:(s_ + 1) * 8], qsel)
                        En = N['En']
                        c.act(En[:], pbS[0:8, 0:32].re('p (h t) -> p h t', h=4), AF.Exp, scale=0.125)
                        c.tt('pool', En[:], En[:], N['smask'][0:8, 8:16].re('p (o t) -> p o t', o=1).bc([8, 4, 8]), ALU.mult)
                        c.mm(ocols(pbO, s_), N['v1n'][:, br, g, :], En[:].re('p h t -> p (h t)'), start=False, stop=True)
            for g in range(2):
                finalize_s(pbOs[g], g, 1, False)
                c.release(pbOs[g])
                if os.environ.get('DBG'):
                    c.dma('sp', Dm['dbg2'][:, 512 + g * 256:512 + (g + 1) * 256], N['ytm'][:, g * 256:(g + 1) * 256])
                finalize_s(pbOw[g], g, 2, False)
                c.release(pbOw[g])
            pb = c.bank()
            for cc in range(4):
                c.tr(pb[:, cc * 128:(cc + 1) * 128], N['ytm'][:, cc * 128:(cc + 1) * 128], identf[:])
            c.copy('act', yT[:, 4:8, :], pb[:].re('p (a t) -> p a t', a=4))

        c.dma('sp', lnG[:], V(Dm['ln_emb'], I['ln_emb'].partition_broadcast(128)))
        tiles = list(range(NTILES + 1))

        def xsrc(i):
            return Dm['x_p'][i * 128:(i + 1) * 128, :] if i < NTILES else Dm['x_s'][:, :]

        for i in tiles:
            b = xt[i % 2]
            c.dma('sp', b[:], xsrc(i))
            layer_norm(b[:], xr[:], lnG)
            c.dma('sp', XS0[i][:], xr[:])

        for l in range(L):
            with ExitStack() as ph:
                wA = c.sb([128, 8, NIN], BF16, 'wA', ph)
                wO = c.sb([128, 8, D], BF16, 'wO', ph)
                yT = c.sb([128, 8, 128], BF16, 'yT', ph)
                cwa = c.sb([128, 2, 3], F32, 'cwa', ph); cwg = c.sb([128, 6, 4], F32, 'cwg', ph)
                CH = {}
                qkvT = c.sb([128, 6, 128], F32, 'qkvT', ph)
                kvtm = c.sb([128, 768], F32, 'kvtm', ph)
                G = {}
                N = {}
                if cfg.mix_b:
                    for nm, shp in (('cg', [128, 19, 128]), ('rowm', [128, 16]), ('qkv_tm', [128, 768]),
                                    ('ab', [128, 8]), ('gate', [128, 256]), ('t4', [128, 4]), ('g4', [128, 4]), ('beta4', [128, 4]),
                                    ('gc4', [128, 4]), ('egc4', [128, 4]), ('egd4', [128, 4]), ('bgc4', [128, 4]), ('ss', [128, 8]),
                                    ('dtb', [128, 4]), ('negA', [128, 4]), ('gain', [128, 64]), ('sc', [128, 4, 64]),
                                    ('kqT', [64, 3, 128]), ('dg', [128, 2, 128]), ('t1', [128, 128]), ('t2', [128, 128]), ('t3', [128, 128]),
                                    ('A', [128, 128]), ('AT', [128, 128]), ('AqkT', [128, 128]), ('X', [128, 128]), ('XT', [128, 128]),
                                    ('D0', [128, 128]), ('D1', [128, 128]), ('DT0', [128, 128]), ('DT1', [128, 128]),
                                    ('Ys', [128, 128]), ('Y2s', [128, 128]), ('negWT', [64, 128]),
                                    ('Vnew', [128, 64]), ('o_tm', [128, 256]), ('egl', [64, 16]), ('grow', [128, 16]), ('rr', [128, 4])):
                        G[nm] = c.sb(shp, F32, nm, ph)
                    c.dma('sp', G['dtb'][:], V(Dm['gdn_dt_bias'], I['gdn_dt_bias'][l].partition_broadcast(128)))
                    c.dma('sp', G['negA'][:], V(Dm['gdn_a_log'], I['gdn_a_log'][l].partition_broadcast(128)))
                    c.dma('sp', G['gain'][:], V(Dm['gdn_norm_g'], I['gdn_norm_g'][l].partition_broadcast(128)))
                    c.act(G['negA'][:], G['negA'][:], AF.Exp)
                    c.ts('dve', G['negA'][:], G['negA'][:], -1.0, ALU.mult)
                if cfg.mix_c:
                    for nm, shp, dt_ in (('w1b', [128, 64, 64], BF16), ('w2b', [64, 2, 64], BF16), ('w2pad', [64, 2, 128], BF16),
                                         ('peT', [64, 2, 32], F32), ('peTb', [64, 2, 32], BF16), ('b1T', [64, 2], F32), ('biasv', [64, 2], F32),
                                         ('ov', [128, 2, 65], BF16), ('cq', [128, 128], F32), ('qT', [128, 4, 128], BF16),
                                         ('cmpT', [128, 2, 144], BF16), ('hx', [64, 16], F32), ('h2', [64, 16], F32), ('hb', [64, 16], BF16),
                                         ('kcT', [128, 256], BF16), ('hidvT', [64, 2, 256], BF16), ('VC1', [128, 2, 2, 65], BF16),
                                         ('gate', [128, 24], F32), ('impm', [128, 64], F32), ('impb', [128, 64], F32), ('imp', [128, 64], F32),
                                         ('sel', [128, 64], F32), ('m8', [128, 8], F32), ('selT', [64, 128], BF16), ('rs', [128, 4], F32),
                                         ('rs2', [128, 4], F32), ('ytm', [128, 512], F32), ('tmp', [128, 4, 64], F32), ('cm', [128, 128], F32),
                                         ('E0', [128, 4, 128], BF16), ('E1', [128, 4, 128], BF16)):
                        N[nm] = c.sb(shp, dt_, 'n_' + nm, ph)
                    nsa_setup(l, N)
                WP, WS = min(512, T), min(512, cfg.PAST)
                c.dma('sp', Dm['win_s'][l, :, 0:WS - TS, :], Dm['cache_win'][l, :, TS:WS, :])
                for k in range(8):
                    c.dma('pool', wA[:, k, :], Dm['w_in'][l, k * 128:(k + 1) * 128, :])
                    c.dma('pool', wO[:, k, :], Dm['w_out'][l, k * 128:(k + 1) * 128, :])
                c.dma('sp', lnG[:], V(Dm['ln1'], I['ln1'][l].partition_broadcast(128)))
                for tap in range(3):
                    c.dma('sp', cwa[:, :, tap], Dm['conv_a_w'][l, tap].re('(c p) -> p c', p=128), slow=True)
                for tap in range(4):
                    c.dma('sp', cwg[:, :, tap], Dm['gdn_conv_w'][l, tap].re('(c p) -> p c', p=128), slow=True)

                def run_tile(i):
                    samp = (i == NTILES)
                    nseq, TT = (NS, TS) if samp else (1, 128)
                    chA = CH['A']
                    xb = xt[i % 2]
                    c.dma('sp', xb[:], XS0[i][:])
                    make_xT(xb[:])
                    if samp:
                        load_state_T(lambda ch: chA[:, ch, :, 0:2], 2, Dm['st_conv_a'][l], NS * 2)
                    elif i == 0:
                        c.memset('pool', chA[:, :, :, 0:2], 0.0)
                    for ch in range(2):
                        pb = c.bank()
                        proj_chunk(wA, O_AC + ch * 128, 128, pb)
                        c.copy('act', z1[:], pb[:, 0:128])
                        pb2 = c.bank()
                        proj_chunk(wA, O_AH + ch * 128, 128, pb2)
                        c.tt('dve', chA[:, ch, :, 2:2 + TT], v3(z1[:], nseq), v3(pb2[:, 0:128], nseq), ALU.mult)
                        conv_taps(v3(z2[:], nseq), chA, ch, cwa, 3, TT)
                        pb3 = c.bank()
                        proj_chunk(wA, O_AB + ch * 128, 128, pb3)
                        c.tt('dve', yT[:, ch, :], z2[:], pb3[:, 0:128], ALU.mult)
                    if samp:
                        store_state_T(lambda ch: chA[:, ch, :, TT:TT + 2], 2, Dm['ca_s'][l], NS * 2)
                    elif i == NTILES - 1:
                        store_state_T(lambda ch: chA[:, ch, :, TT:TT + 2], 2, Dm['ca_p'][l], 2)
                    if not samp:
                        c.copy('pool', chA[:, :, :, 0:2], chA[:, :, :, TT:TT + 2])
                    pb = c.bank()
                    for k in range(8):
                        c.mm(pb[:, :], xT[:, k, :], wA[:, k, O_NKV:O_NKV + 512], start=(k == 0), stop=(k == 7))
                    c.copy('act', kvtm[:, 0:512], pb[:, :])
                    pb = c.bank()
                    for k in range(8):
                        c.mm(pb[:, 0:256], xT[:, k, :], wA[:, k, O_NWIN:O_NWIN + 256], start=(k == 0), stop=(k == 7))
                    c.copy('act', kvtm[:, 512:768], pb[:, 0:256])
                    if samp:
                        c.dma('sp', Dm['kv_s'][l], kvtm[:, 0:512])
                        for sq in range(NS):
                            c.dma('sp', Dm['win_s'][l, sq, WS - TS:WS, :], kvtm[sq * TS:(sq + 1) * TS, 512:768])
                    else:
                        c.dma('sp', Dm['kv_p'][l, i * 128:(i + 1) * 128, :], kvtm[:, 0:512])
                        if i * 128 >= T - WP:
                            r0 = i * 128 - (T - WP)
                            c.dma('sp', Dm['win_p'][l, r0:r0 + 128, :], kvtm[:, 512:768])
                    chG = CH['G']
                    if samp:
                        load_state_T(lambda ch: chG[:, ch, :, 0:3], 6, Dm['st_gdn_conv'][l], NS * 3)
                    elif i == 0:
                        c.memset('pool', chG[:, :, :, 0:3], 0.0)
                    for ch in range(6):
                        pb = c.bank()
                        proj_chunk(wA, O_GQ + ch * 128, 128, pb)
                        c.copy('act', chG[:, ch, :, 3:3 + TT], v3(pb[:, 0:128], nseq))
                        conv_taps(v3(z2[:], nseq), chG, ch, cwg, 4, TT)
                        c.act(qkvT[:, ch, :], z2[:], AF.Silu)
                    if samp:
                        store_state_T(lambda ch: chG[:, ch, :, TT:TT + 3], 6, Dm['gc_s'][l], NS * 3)
                    elif i == NTILES - 1:
                        store_state_T(lambda ch: chG[:, ch, :, TT:TT + 3], 6, Dm['gc_p'][l], 3)
                    if not samp:
                        c.copy('pool', chG[:, :, :, 0:3], chG[:, :, :, TT:TT + 3])
                    if cfg.mix_b:
                        gdn_tile(l, i, samp, G, wA, qkvT, yT)
                    else:
                        c.memset('dve', yT[:, 2:4, :], 0.0)
                    if cfg.mix_c == 1 and samp:
                        nsa_tile_s(l, N, G, wA, kvtm, yT)
                    elif cfg.mix_c and not samp and not os.environ.get('NSA_SKIP_TILE'):
                        nsa_tile_p(l, i, N, G, wA, kvtm, yT)
                    else:
                        c.memset('dve', yT[:, 4:8, :], 0.0)
                    if samp and os.environ.get('DBG'):
                        c.dma('sp', Dm['dbg3'][:, :], yT[:].re('p k t -> p (k t)'))
                    for h in range(2):
                        pb = c.bank()
                        for k in range(8):
                            c.mm(pb[:, :], yT[:, k, :], wO[:, k, h * 512:(h + 1) * 512], start=(k == 0), stop=(k == 7))
                        c.stt(xr[:, h * 512:(h + 1) * 512], xb[:, h * 512:(h + 1) * 512], ALPHA, pb[:, :], ALU.mult, ALU.add)
                    if samp and os.environ.get('DBG'):
                        pass
                    layer_norm(xr[:], xr[:], lnG)
                    c.dma('sp', XS1[i][:], xr[:])

                with ExitStack() as sub:
                    CH['A'] = c.sb([128, 2, 1, 2 + 128], F32, 'chA_p', sub); CH['G'] = c.sb([128, 6, 1, 3 + 128], F32, 'chG_p', sub)
                    if cfg.mix_b:
                        G['S'] = c.sb([64, 1, 4, 64], F32, 'S_p', sub)
                    if cfg.mix_c:
                        N['kslcT'] = c.sb([128, T], BF16, 'kslcT', sub)
                        N['vslc1'] = c.sb([128, NTILES, 2, 65], BF16, 'vslc1', sub)
                        N['eall'] = c.sb([64, T], BF16, 'eall', sub)
                        N['kwinT'] = c.sb([128, 1024], BF16, 'kwinT', sub)
                        N['vwin1'] = c.sb([128, 8, 2, 65], BF16, 'vwin1', sub)
                        c.memset('pool', N['vwin1'][:], 1.0)
                        c.dma('pool', N['eall'][:], Dm['eall'][:])
                        c.memset('pool', N['vslc1'][:], 1.0)
                    for i in range(NTILES):
                        run_tile(i)
                    c.barrier()
                with ExitStack() as sub:
                    CH['A'] = c.sb([128, 2, NS, 2 + TS], F32, 'chA_s', sub); CH['G'] = c.sb([128, 6, NS, 3 + TS], F32, 'chG_s', sub)
                    if cfg.mix_b:
                        G['S'] = c.sb([64, 16, 1, 64], F32, 'S_s', sub)
                        for nm, shp, dt_ in (('colm', [64, 16, 128], BF16), ('negWTm', [64, 4, 128], F32), ('QgTm', [64, 4, 128], F32), ('Kdm', [128, 8, 64], F32)):
                            G[nm] = c.sb(shp, dt_, nm, sub)
                    if cfg.mix_c == 1:
                        NPG = cfg.PAST // 128
                        WS_ = min(512, cfg.PAST)
                        for nm, shp, dt_ in (('ptb', [128, NS * NPG], I32), ('idxA', [128, NS * NPG], I32), ('idxB', [128, NS * NPG], I32),
                                             ('eall_s', [64, cfg.PAST + 128], BF16), ('smask', [128, 16], F32), ('Ebig', [128, 2, 4, 128], BF16),
                                             ('VCs', [128, 2, 65], BF16), ('vs1', [128, NPG, 2, 65], BF16), ('vw1', [128, WS_ // 128, 2, 65], BF16),
                                             ('v1n', [8, 2, 2, 65], BF16), ('kslcTn', [128, 128], BF16), ('qTs', [128, NS, 4, TS], BF16), ('Ecur', [128, 32], BF16), ('kwinTn', [128, 128], BF16),
                                             ('pk', [128, min(NPG, 8), 256], BF16), ('cmpTs', [128, 2, cfg.PAST], BF16), ('pw', [128, WS_ // 128, 256], BF16),
                                             ('kwTs', [128, WS_], BF16), ('hxs', [64, 2, 128], F32), ('h2s', [64, 2, 128], F32), ('hbs', [64, 2, 128], BF16),
                                             ('kcTs', [128, 128], BF16), ('selTs', [64, 2, 128], BF16), ('osb', [65, 512], F32),
                                             ('Es', [128, 256], BF16), ('En', [8, 4, 8], BF16), ('dbgst', [128, 640], F32)):
                            N[nm] = c.sb(shp, dt_, 'ns_' + nm, sub)
                    run_tile(NTILES)
                    c.barrier()

            with ExitStack() as ph:
                wU = c.sb([128, 8, 2 * DFF], BF16, 'wU', ph)
                wD = c.sb([128, 22, D], BF16, 'wD', ph)
                cwf = c.sb([128, 22, 3], F32, 'cwf', ph)
                chFb = c.sb([128, 22, NS * (2 + TS)], F32, 'chF', ph)
                actT = c.sb([128, 22, 128], BF16, 'actT', ph)
                chF_p = chFb[:, :, 0:130].re('p c (s t) -> p c s t', s=1)
                chF_s = chFb[:, :, :].re('p c (s t) -> p c s t', s=NS)
                for k in range(8):
                    c.dma('pool', wU[:, k, :], Dm['w_up'][l, k * 128:(k + 1) * 128, :])
                for k in range(22):
                    c.dma('pool', wD[:, k, :], Dm['w_down'][l, k * 128:(k + 1) * 128, :])
                c.dma('sp', lnG[:], V(Dm['ln2'], I['ln2'][l].partition_broadcast(128)))
                for tap in range(3):
                    c.dma('sp', cwf[:, :, tap], Dm['ffn_conv_w'][l, tap].re('(c p) -> p c', p=128), slow=True)
                for i in tiles:
                    samp = (i == NTILES)
                    nseq, TT = (NS, TS) if samp else (1, 128)
                    chF = chF_s if samp else chF_p
                    xb = xt[i % 2]
                    c.dma('sp', xb[:], XS1[i][:])
                    make_xT(xb[:])
                    if samp:
                        load_state_T(lambda ch: chF[:, ch, :, 0:2], 22, Dm['st_ffn_conv'][l], NS * 2)
                    elif i == 0:
                        c.memset('pool', chF[:, :, :, 0:2], 0.0)
                    for ch in range(22):
                        pb = c.bank()
                        proj_chunk(wU, ch * 128, 128, pb)
                        c.copy('act', chF[:, ch, :, 2:2 + TT], v3(pb[:, 0:128], nseq))
                        conv_taps(v3(z2[:], nseq), chF, ch, cwf, 3, TT)
                        c.act(z1[:], z2[:], AF.Silu)
                        pb2 = c.bank()
                        proj_chunk(wU, DFF + ch * 128, 128, pb2)
                        c.tt('dve', actT[:, ch, :], z1[:], pb2[:, 0:128], ALU.mult)
                    if samp:
                        store_state_T(lambda ch: chF[:, ch, :, TT:TT + 2], 22, Dm['fc_s'][l], NS * 2)
                    elif i == NTILES - 1:
                        store_state_T(lambda ch: chF[:, ch, :, TT:TT + 2], 22, Dm['fc_p'][l], 2)
                    if not samp:
                        c.copy('pool', chF[:, :, :, 0:2], chF[:, :, :, TT:TT + 2])
                    for h in range(2):
                        pb = c.bank()
                        for k in range(22):
                            c.mm(pb[:, :], actT[:, k, :], wD[:, k, h * 512:(h + 1) * 512], start=(k == 0), stop=(k == 21))
                        c.stt(xr[:, h * 512:(h + 1) * 512], xb[:, h * 512:(h + 1) * 512], ALPHA, pb[:, :], ALU.mult, ALU.add)
                    layer_norm(xr[:], xr[:], lnG)
                    if l == L - 1:
                        c.dma('sp', (Dm['y_p'][i * 128:(i + 1) * 128, :] if not samp else Dm['y_s'][:, :]), xr[:])
                    else:
                        c.dma('sp', XS0[i][:], xr[:])
                c.barrier()
        c.finish()
        print("instructions:", c.ninstr)
    return nc


OUT_NAMES = ['y_p', 'y_s', 'kv_p', 'kv_s', 'win_p', 'win_s', 'ca_p', 'ca_s', 'gc_p', 'gc_s', 'gs_p', 'gs_s', 'fc_p', 'fc_s']


def make_in_maps(cfg, inp, ncores=8):
    L, NS = cfg.L, cfg.NS
    nb = inp['x_prompt'].shape[0]
    f = np.ascontiguousarray
    shared = {
        'ln_emb': f(np.stack([inp['ln_emb_g'], inp['ln_emb_b']])),
        'w_in': f(inp['w_in']), 'w_out': f(inp['w_out']), 'w_up': f(inp['w_up']), 'w_down': f(inp['w_down']),
        'conv_a_w': f(inp['conv_a_w']), 'gdn_conv_w': f(inp['gdn_conv_w']), 'ffn_conv_w': f(inp['ffn_conv_w']),
        'gdn_a_log': f(inp['gdn_a_log']), 'gdn_dt_bias': f(inp['gdn_dt_bias']), 'gdn_norm_g': f(inp['gdn_norm_g']),
        'cmp_pe': f(inp['cmp_pe']), 'cmp_w1': f(inp['cmp_w1']), 'cmp_b1': f(inp['cmp_b1']), 'cmp_w2': f(inp['cmp_w2']),
        'ln1': f(np.stack([inp['ln1_g'], inp['ln1_b']], axis=1)), 'ln2': f(np.stack([inp['ln2_g'], inp['ln2_b']], axis=1)),
    }
    for v_, (ns_, ts_) in (('p', (1, 128)), ('s', (cfg.NS, cfg.TS))):
        cg, rowm, colm = gdn_consts(ns_, ts_)
        shared['cg_' + v_], shared['rowm_' + v_], shared['colm_' + v_] = cg, rowm, colm
    shared['ov'], shared['cq'], shared['eall'], shared['impm'], shared['impb'] = nsa_consts(cfg)
    P_ = cfg.PAST
    n_ = np.arange(64)
    shared['eall_s'] = (np.arange(P_ + 128)[None, :] // 64 == n_[:, None]).astype(np.float32)
    qpos = P_ + (np.arange(128) % cfg.TS)
    valid = (n_[None, :] * 64 <= qpos[:, None]); forced = (n_[None, :] == 0) | (n_[None, :] == qpos[:, None] // 64)
    shared['impm_s'] = (valid & ~forced).astype(np.float32)
    shared['impb_s'] = np.where(forced, 1e9, np.where(valid, 0.0, -1e9)).astype(np.float32)
    sm = np.zeros((128, 16), np.float32)
    sm[:, 0:8] = (np.arange(128)[:, None] > np.arange(8)[None, :])
    sm[0:8, 8:16] = (np.arange(8)[:, None] <= np.arange(8)[None, :])
    shared['smask'] = sm
    shared['cache_kv'] = f(inp['cache_nsa_kv']).reshape(-1, 256)
    maps = []
    for i in range(ncores):
        sl = slice(NS * i, NS * (i + 1))
        m = dict(shared)
        m['x_p'] = f(inp['x_prompt'][i % nb])
        m['x_s'] = f(inp['x_sample'][sl].reshape(128, D))
        m['st_conv_a'] = f(inp['state_conv_a'][:, sl].reshape(L, NS * 2, 256))
        m['page_table'] = f(inp['page_table'][sl].astype(np.int32).reshape(1, -1))
        m['cache_win'] = f(inp['cache_nsa_win'][:, sl].reshape(L, NS, -1, 256))
        m['st_gdn_conv'] = f(inp['state_gdn_conv'][:, sl].reshape(L, NS * 3, 768))
        m['st_gdn'] = f(inp['state_gdn'][:, sl])
        m['st_ffn_conv'] = f(inp['state_ffn_conv'][:, sl].reshape(L, NS * 2, DFF))
        maps.append(m)
    return maps


def gather(cfg, res, ncores=8, nb=4):
    L, NS, T = cfg.L, cfg.NS, cfg.T
    WP = min(512, T)
    WS = min(512, cfg.PAST)
    r = res

    def P(name, shp):
        return np.stack([r[i][name].reshape((L,) + shp) for i in range(nb)], axis=1)

    def S(name, shp):
        return np.concatenate([r[i][name].reshape((L, NS) + shp) for i in range(ncores)], axis=1)

    y_p = np.stack([r[i]['y_p'] for i in range(nb)], axis=0)
    y_s = np.concatenate([r[i]['y_s'].reshape(NS, cfg.TS, D) for i in range(ncores)], axis=0)
    return (y_p, y_s,
            P('kv_p', (T, 4, 2, 64)), S('kv_s', (cfg.TS, 4, 2, 64)),
            P('win_p', (WP, 2, 2, 64)), S('win_s', (WS, 2, 2, 64)),
            P('ca_p', (2, 256)), S('ca_s', (2, 256)),
            P('gc_p', (3, 768)), S('gc_s', (3, 768)),
            P('gs_p', (4, 64, 64)), S('gs_s', (4, 64, 64)),
            P('fc_p', (2, DFF)), S('fc_s', (2, DFF)))


def kernel(**inputs):
    inp = {k: np.asarray(v) for k, v in inputs.items()}
    cfg = Cfg()
    nc = build(cfg)
    maps = make_in_maps(cfg, inp, 8)
    res = run_bass_kernel_spmd(nc, maps, core_ids=list(range(8)))
    outs = gather(cfg, res.results, 8, 4)
    return tuple(np.ascontiguousarray(o.astype(np.float32)) for o in outs)
```

```python
import numpy as np
import os
from contextlib import ExitStack
import concourse.bass as bass
import concourse.mybir as mybir
from concourse.bass_utils import run_bass_kernel_spmd

F32 = mybir.dt.float32
BF16 = mybir.dt.bfloat16
I32 = mybir.dt.int32
AF = mybir.ActivationFunctionType
ALU = mybir.AluOpType
AX = mybir.AxisListType

D = 1024
DFF = 2816
NIN = 3104
ALPHA = (2.0 * 4) ** 0.25
NHSETS = int(os.environ.get('NHSETS', '4'))
EPS = 1e-5
O_AB, O_AC, O_AH, O_GQ, O_GK, O_GV, O_GG, O_GA, O_GB, O_NQ, O_NKV, O_NWIN, O_NG = (
    0, 256, 512, 768, 1024, 1280, 1536, 1792, 1796, 1800, 2312, 2824, 3080)


class Cfg:
    def __init__(self, T=4096, L=4, NS=16, TS=8, PAST=2048, NPOOL=2560, mix_b=True, mix_c=True):
        self.T, self.L, self.NS, self.TS, self.PAST, self.NPOOL = T, L, NS, TS, PAST, NPOOL
        self.NT = T // 128
        self.mix_b, self.mix_c = mix_b, mix_c


class Buf:
    def __init__(self, t, name):
        self.t = t
        self.name = name
        self.w = None
        self.r = {}

    def __getitem__(self, key):
        return V(self, self.t[key])


class V:
    def __init__(self, buf, ap):
        self.buf = buf
        self.ap = ap

    def __getitem__(self, key):
        return V(self.buf, self.ap[key])

    def re(self, pat, **kw):
        return V(self.buf, self.ap.rearrange(pat, **kw))

    def bc(self, shape):
        return V(self.buf, self.ap.to_broadcast(list(shape)))


def _aps(x):
    return x.ap if isinstance(x, V) else x


def _chain(gens):
    for g_ in gens:
        yield from g_


class _HeadView:
    def __init__(self, v, h):
        self.v, self.h = v, h

    def __getitem__(self, key):
        k = list(key)
        if isinstance(k[2], int):
            k[2] = 0
        return self.v[tuple(k)]


class Ctx:
    def __init__(self, nc, es, n_dma_sems=14):
        self.nc, self.es = nc, es
        self.eng = {'pe': nc.tensor, 'act': nc.scalar, 'dve': nc.vector, 'pool': nc.gpsimd, 'sp': nc.sync}
        self.sem = {k: es.enter_context(nc.semaphore('s_' + k)) for k in self.eng}
        self.cnt = {k: 0 for k in self.eng}
        self.seen = {k: {} for k in self.eng}
        self.dsem = [es.enter_context(nc.semaphore('d%d' % i)) for i in range(n_dma_sems)]
        self.dcnt = [0] * n_dma_sems
        self.dnext = {'sp': 0, 'pool': 0, 'act': 0}
        self.dpool = {'sp': list(range(0, n_dma_sems - 5)), 'pool': list(range(n_dma_sems - 5, n_dma_sems)), 'act': []}
        self.nbuf = 0
        self.banks = []
        self.bnext = 0
        self.ninstr = 0

    def sb(self, shape, dt=F32, name=None, es=None):
        self.nbuf += 1
        name = (name or 'b') + str(self.nbuf)
        esz = 2 if dt == BF16 else 4
        n = 1
        for d_ in shape[1:]:
            n *= d_
        npad = -(-(n * esz) // 64) * 64 // esz
        t = (es or self.es).enter_context(self.nc.sbuf_tensor(name, [shape[0], npad], dt))
        ap = t[:, 0:n]
        if len(shape) > 2:
            names = ['d%d' % i for i in range(len(shape) - 1)]
            pat = 'p (' + ' '.join(names) + ') -> p ' + ' '.join(names)
            ap = ap.rearrange(pat, **{nm: sz for nm, sz in zip(names[:-1], shape[1:-1])})
        return Buf(ap, name)

    def barrier(self):
        for e in self.eng:
            for k in self.eng:
                if k != e and self.cnt[k] > 0:
                    self._wait(e, k, self.cnt[k])
            for j in range(len(self.dsem)):
                if self.dcnt[j] > 0:
                    self._wait(e, j, self.dcnt[j])

    def mkbanks(self):
        for i in range(8):
            t = self.es.enter_context(self.nc.psum_tensor('bank%d' % i, [128, 512], F32))
            self.banks.append(Buf(t, 'bank%d' % i))

    def bank(self):
        if not hasattr(self, 'rot'):
            self.rot = list(self.banks)
        b = self.rot.pop(0)
        self.rot.append(b)
        return b

    def hold(self):
        if not hasattr(self, 'rot'):
            self.rot = list(self.banks)
        return self.rot.pop(0)

    def release(self, b):
        self.rot.append(b)

    def dram(self, ap, name):
        class _T:
            pass
        b = Buf(None, name)
        b.t = ap
        return b

    def _semof(self, key):
        return self.sem[key] if isinstance(key, str) else self.dsem[key]

    def _wait(self, e, key, val):
        if key == e and e == 'pe':
            return
        if self.seen[e].get(key, 0) >= val:
            return
        self.eng[e].wait_ge(self._semof(key), val)
        self.seen[e][key] = val
        self.ninstr += 1

    def _deps(self, e, reads, writes):
        for b in reads:
            if b.w is not None:
                self._wait(e, *b.w)
        for b in writes:
            if b.w is not None:
                self._wait(e, *b.w)
            for k, v in b.r.items():
                self._wait(e, k, v)

    def _mark(self, ev, reads, writes):
        for b in reads:
            b.r[ev[0]] = ev[1]
        for b in writes:
            b.w = ev
            b.r = {}

    def op(self, e, fn, reads=(), writes=(), signal=True):
        reads = [x.buf if isinstance(x, V) else x for x in reads]
        writes = [x.buf if isinstance(x, V) else x for x in writes]
        self._deps(e, reads, writes)
        ins = fn(self.eng[e])
        self.ninstr += 1
        if signal or e != 'pe':
            self.cnt[e] += 1
            ins.then_inc(self.sem[e], 1)
            ev = (e, self.cnt[e])
        else:
            ev = (e, self.cnt[e] + 1)
        self._mark(ev, reads, writes)
        return ins

    def dma(self, e, out, in_, slow=False):
        pl = self.dpool[e]
        j = pl[self.dnext[e]]
        self.dnext[e] = (self.dnext[e] + 1) % len(pl)
        if self.dcnt[j] > 0:
            self._wait(e, j, self.dcnt[j])
        self._deps(e, [in_.buf], [out.buf])
        self.dcnt[j] += 16
        if slow:
            ins = self.eng[e].dma_start(out=out.ap, in_=in_.ap, allow_slow_non_contiguous=True)
        else:
            ins = self.eng[e].dma_start(out=out.ap, in_=in_.ap)
        ins.then_inc(self.dsem[j], 16)
        self.ninstr += 1
        self._mark((j, self.dcnt[j]), [in_.buf], [out.buf])
        return ins

    def dma_custom(self, e, fn, reads, writes):
        pl = self.dpool[e]
        j = pl[self.dnext[e]]
        self.dnext[e] = (self.dnext[e] + 1) % len(pl)
        if self.dcnt[j] > 0:
            self._wait(e, j, self.dcnt[j])
        self._deps(e, reads, writes)
        self.dcnt[j] += 16
        ins = fn(self.eng[e])
        ins.then_inc(self.dsem[j], 16)
        self.ninstr += 1
        self._mark((j, self.dcnt[j]), reads, writes)
        return ins

    def mm(self, out, lhsT, rhs, start=True, stop=True, signal=None):
        if signal is None:
            signal = stop
        return self.op('pe', lambda e: e.matmul(out.ap, lhsT=lhsT.ap, rhs=rhs.ap, start=start, stop=stop),
                       [lhsT, rhs], [out], signal=signal)

    def tr(self, out, in_, ident, signal=True):
        return self.op('pe', lambda e: e.transpose(out.ap, in_.ap, ident.ap), [in_, ident], [out], signal=signal)

    def act(self, out, in_, func, bias=None, scale=None, accum=None, e='act'):
        kw = {}
        rd = [in_]
        if bias is not None:
            kw['bias'] = _aps(bias)
            if isinstance(bias, V):
                rd.append(bias)
        if scale is not None:
            kw['scale'] = _aps(scale)
            if isinstance(scale, V):
                rd.append(scale)
        wr = [out]
        if accum is not None:
            kw['accum_out'] = accum.ap
            wr.append(accum)
        return self.op('act', lambda e_: e_.activation(out.ap, in_.ap, func, **kw), rd, wr)

    def copy(self, e, out, in_):
        if e == 'act':
            return self.op('act', lambda e_: e_.copy(out.ap, in_.ap), [in_], [out])
        return self.op(e, lambda e_: e_.tensor_copy(out.ap, in_.ap), [in_], [out])

    def tt(self, e, out, in0, in1, op):
        return self.op(e, lambda e_: e_.tensor_tensor(out.ap, in0.ap, in1.ap, op), [in0, in1], [out])

    def ts(self, e, out, in0, s1, op0, s2=None, op1=None, accum=None):
        rd = [in0] + [s for s in (s1, s2) if isinstance(s, V)]
        wr = [out] + ([accum] if accum is not None else [])
        kw = {}
        if accum is not None:
            kw['accum_out'] = accum.ap
        if op1 is None:
            return self.op(e, lambda e_: e_.tensor_scalar(out.ap, in0.ap, _aps(s1), None, op0, **kw), rd, wr)
        return self.op(e, lambda e_: e_.tensor_scalar(out.ap, in0.ap, _aps(s1), _aps(s2), op0, op1, **kw), rd, wr)

    def stt(self, out, in0, s, in1, op0, op1):
        rd = [in0, in1] + ([s] if isinstance(s, V) else [])
        return self.op('dve', lambda e_: e_.scalar_tensor_tensor(out.ap, in0.ap, _aps(s), in1.ap, op0, op1), rd, [out])

    def memset(self, e, out, val):
        return self.op(e, lambda e_: e_.memset(out.ap, val), [], [out])

    def finish(self):
        for j in range(len(self.dsem)):
            if self.dcnt[j] > 0:
                self._wait('sp', j, self.dcnt[j])


def gdn_consts(nseq, TS):
    i = np.arange(128)
    seq = i // TS if nseq > 1 else np.zeros(128, np.int64)
    same = seq[:, None] == seq[None, :]
    lowi = same & (i[:, None] >= i[None, :])
    lows = same & (i[:, None] > i[None, :])
    cg = np.zeros((128, 19, 128), np.float32)
    cg[:, 0], cg[:, 1], cg[:, 2], cg[:, 3], cg[:, 4] = lowi, lows, lowi.T, lows.T, same
    for k in range(7):
        b = 2 << k
        lm = same & (i[:, None] // b == i[None, :] // b) & ((i[:, None] % b) >= b // 2) & ((i[None, :] % b) < b // 2)
        cg[:, 5 + k] = lm
        cg[:, 12 + k] = lm.T
    rowm = np.zeros((128, 16), np.float32)
    rowm[i, seq] = 1.0
    colm = np.ascontiguousarray(np.broadcast_to(rowm.T[None], (64, 16, 128))).astype(np.float32)
    return cg, rowm, colm


def nsa_consts(cfg):
    T, NT = cfg.T, cfg.NT
    c_ = np.arange(128)
    n = np.arange(64)
    ov = np.zeros((128, 2, 65), np.float32)
    for ct in range(2):
        cc = (ct * 128 + c_)[:, None] * 16
        ov[:, ct, :64] = (cc < n[None, :] * 64 + 64) & (cc + 32 > n[None, :] * 64)
        ov[:, ct, 64] = 1.0
    cq = (16.0 * c_[:, None] - c_[None, :]).astype(np.float32)
    eall = (np.arange(T)[None, :] // 64 == n[:, None]).astype(np.float32)
    qpos = np.arange(T)
    valid = (n[None, :] * 64 <= qpos[:, None])
    forced = (n[None, :] == 0) | (n[None, :] == qpos[:, None] // 64)
    mult = (valid & ~forced).astype(np.float32).reshape(NT, 128, 64)
    bias = np.where(forced, 1e9, np.where(valid, 0.0, -1e9)).astype(np.float32).reshape(NT, 128, 64)
    return ov, cq, eall, mult, bias


def build(cfg):
    T, L, NS, TS, NTILES = cfg.T, cfg.L, cfg.NS, cfg.TS, cfg.NT
    nc = bass.Bass("TRN2", target_bir_lowering=False)
    es = ExitStack()

    def din(name, shape, dt=F32):
        return nc.dram_tensor(name, list(shape), dt, kind="ExternalInput").ap()

    def dout(name, shape, dt=F32):
        return nc.dram_tensor(name, list(shape), dt, kind="ExternalOutput").ap()

    def dscr(name, shape, dt=F32):
        return nc.dram_tensor(name, list(shape), dt, kind="Internal").ap()

    I = {}
    I['x_p'] = din('x_p', [T, D]); I['x_s'] = din('x_s', [128, D])
    I['st_conv_a'] = din('st_conv_a', [L, NS * 2, 256])
    I['st_gdn_conv'] = din('st_gdn_conv', [L, NS * 3, 768])
    I['st_gdn'] = din('st_gdn', [L, NS, 4, 64, 64])
    I['st_ffn_conv'] = din('st_ffn_conv', [L, NS * 2, DFF])
    I['cache_win'] = din('cache_win', [L, NS, min(512, cfg.PAST), 256])
    for v_ in ('p', 's'):
        I['cg_' + v_] = din('cg_' + v_, [128, 19, 128]); I['rowm_' + v_] = din('rowm_' + v_, [128, 16]); I['colm_' + v_] = din('colm_' + v_, [64, 16, 128])
    I['gdn_a_log'] = din('gdn_a_log', [L, 4]); I['gdn_dt_bias'] = din('gdn_dt_bias', [L, 4]); I['gdn_norm_g'] = din('gdn_norm_g', [L, 64])
    NPG = cfg.PAST // 128
    I['cache_kv'] = din('cache_kv', [L * cfg.NPOOL * 128 * 2, 256])
    I['page_table'] = din('page_table', [1, NS * NPG], I32)
    I['eall_s'] = din('eall_s', [64, cfg.PAST + 128]); I['impm_s'] = din('impm_s', [128, 64]); I['impb_s'] = din('impb_s', [128, 64])
    I['smask'] = din('smask', [128, 16])
    I['ov'] = din('ov', [128, 2, 65]); I['cq'] = din('cq', [128, 128]); I['eall'] = din('eall', [64, T])
    I['impm'] = din('impm', [NTILES, 128, 64]); I['impb'] = din('impb', [NTILES, 128, 64])
    I['cmp_pe'] = din('cmp_pe', [L, 2, 32, 64]); I['cmp_w1'] = din('cmp_w1', [L, 2, 32, 64, 64]); I['cmp_b1'] = din('cmp_b1', [L, 2, 64])
    I['cmp_w2'] = din('cmp_w2', [L, 2, 64, 64])
    I['ln_emb'] = din('ln_emb', [2, D])
    I['w_in'] = din('w_in', [L, D, NIN]); I['w_out'] = din('w_out', [L, D, D])
    I['w_up'] = din('w_up', [L, D, 2 * DFF]); I['w_down'] = din('w_down', [L, DFF, D])
    I['conv_a_w'] = din('conv_a_w', [L, 3, 256]); I['gdn_conv_w'] = din('gdn_conv_w', [L, 4, 768])
    I['ffn_conv_w'] = din('ffn_conv_w', [L, 3, DFF])
    I['ln1'] = din('ln1', [L, 2, D]); I['ln2'] = din('ln2', [L, 2, D])
    O = {}
    if os.environ.get('DBG'):
        O['dbg2'] = dout('dbg2', [128, 2048]); O['dbg3'] = dout('dbg3', [128, 1024], BF16)
    O['y_p'] = dout('y_p', [T, D]); O['y_s'] = dout('y_s', [128, D])
    O['ca_p'] = dout('ca_p', [L, 2, 256]); O['ca_s'] = dout('ca_s', [L, NS * 2, 256])
    O['gc_p'] = dout('gc_p', [L, 3, 768]); O['gc_s'] = dout('gc_s', [L, NS * 3, 768])
    O['fc_p'] = dout('fc_p', [L, 2, DFF]); O['fc_s'] = dout('fc_s', [L, NS * 2, DFF])
    O['kv_p'] = dout('kv_p', [L, T, 512]); O['kv_s'] = dout('kv_s', [L, 128, 512])
    O["win_p"] = dout("win_p", [L, min(512, T), 256]); O["win_s"] = dout("win_s", [L, NS, min(512, cfg.PAST), 256])
    O['gs_p'] = dout('gs_p', [L, 4, 64, 64]); O['gs_s'] = dout('gs_s', [L, NS, 4, 64, 64])
    xs0 = dscr('xs0', [NTILES + 1, 128, D]); xs1 = (dout if os.environ.get('DBG') else dscr)('xs1', [NTILES + 1, 128, D])

    with es:
        c = Ctx(nc, es)
        c.mkbanks()
        Dm = {k: c.dram(v, k) for k, v in list(I.items()) + list(O.items())}
        XS0 = [c.dram(xs0[i], 'xs0_%d' % i) for i in range(NTILES + 1)]
        XS1 = [c.dram(xs1[i], 'xs1_%d' % i) for i in range(NTILES + 1)]

        identf = c.sb([128, 128], F32, 'identf')
        c.memset('pool', identf[:], 0.0)
        c.op('pool', lambda e: e.affine_select(out=identf.t[:], in_=identf.t[:], pattern=[[-1, 128]],
                                                compare_op=ALU.not_equal, fill=1.0, base=0, channel_multiplier=1),
             [identf], [identf])

        identb = c.sb([128, 128], BF16, 'identb')
        c.copy('dve', identb[:], identf[:])
        piota = c.sb([128, 1], I32, 'piota')
        c.op('pool', lambda e: e.iota(piota.t[:], pattern=[[0, 1]], base=0, channel_multiplier=1), [], [piota])
        ones128 = c.sb([128, 128], F32, 'ones128')
        c.memset('pool', ones128[:], 1.0)
        lnG = c.sb([128, 2, D], F32, 'lnG')
        xt = [c.sb([128, D], F32, 'xt')] * 2
        xr = c.sb([128, D], F32, 'xr')
        xT = c.sb([128, 8, 128], BF16, 'xT')
        st6 = c.sb([128, 2, 6], F32, 'st6'); mv = c.sb([128, 2], F32, 'mv'); rstd = c.sb([128, 1], F32, 'rstd')
        rows = c.sb([48, 512], F32, 'rows')
        zring = [c.sb([128, 128], F32, 'z') for _ in range(6)]
        zstate = [0]

        def znext():
            zstate[0] = (zstate[0] + 1) % len(zring)
            return zring[zstate[0]]
        z1 = zring[0]

        def layer_norm(src, dst, gb):
            for k in range(2):
                c.op('dve', lambda e: e.bn_stats(st6.t[:, k, :], src.ap[:, k * 512:(k + 1) * 512]), [src], [st6])
            c.op('dve', lambda e: e.bn_aggr(mv.t[:], st6.t[:]), [st6], [mv])
            c.ts('dve', rstd[:], mv[:, 1:2], EPS, ALU.add)
            c.act(rstd[:], rstd[:], AF.Sqrt)
            c.op('dve', lambda e: e.reciprocal(rstd.t[:], rstd.t[:]), [rstd], [rstd])
            c.ts('dve', dst, src, mv[:, 0:1], ALU.subtract, rstd[:, 0:1], ALU.mult)
            c.tt('pool', dst, dst, gb[:, 0, :], ALU.mult)
            c.tt('pool', dst, dst, gb[:, 1, :], ALU.add)

        def make_xT(src):
            for h in range(2):
                pb = c.bank()
                for k in range(4):
                    c.tr(pb[:, k * 128:(k + 1) * 128], src[:, (h * 4 + k) * 128:(h * 4 + k + 1) * 128], identf[:], signal=(k == 3))
                c.copy('act', xT[:, h * 4:(h + 1) * 4, :], pb[:].re('p (k t) -> p k t', k=4))

        def proj_chunk(w, col, m, pb, n0=0):
            for k in range(8):
                c.mm(pb[0:m, n0:n0 + 128], w[:, k, col:col + m], xT[:, k, :], start=(k == 0), stop=(k == 7))

        def load_state_T(dst, nch, dram_rows, R):
            for c0 in range(0, nch, 4):
                n = min(4, nch - c0)
                c.dma('sp', rows[0:R, 0:n * 128], dram_rows[:, c0 * 128:(c0 + n) * 128])
                for ch in range(n):
                    pb = c.bank()
                    c.tr(pb[:, 0:R], rows[0:R, ch * 128:(ch + 1) * 128], identf[0:R, 0:R])
                    d = dst(c0 + ch)
                    c.copy('dve', d, pb[:, 0:R].re('p (s j) -> p s j', s=d.ap.shape[1]))

        def store_state_T(src, nch, dram_rows, R):
            for c0 in range(0, nch, 4):
                n = min(4, nch - c0)
                for ch in range(n):
                    pb = c.bank()
                    sv = src(c0 + ch)
                    zz = znext()
                    c.copy('dve', zz[:, 0:R].re('p (s j) -> p s j', s=sv.ap.shape[1]), sv)
                    c.tr(pb[0:R, 0:128], zz[:, 0:R], identf[:])
                    c.copy('act', rows[0:R, ch * 128:(ch + 1) * 128], pb[0:R, 0:128])
                c.dma('sp', dram_rows[:, c0 * 128:(c0 + n) * 128], rows[0:R, 0:n * 128])

        def conv_taps(dst, ch, cidx, wts, K, TT):
            c.ts('dve', dst, ch[:, cidx, :, 0:TT], wts[:, cidx, 0:1], ALU.mult)
            for i in range(1, K):
                c.stt(dst, ch[:, cidx, :, i:i + TT], wts[:, cidx, i:i + 1], dst, ALU.mult, ALU.add)

        def v3(v, nseq):
            return v.re('p (s t) -> p s t', s=nseq)


        def gdn_tile(l, i, samp, G, wA, qkvT, yT):
            nseq = NS if samp else 1
            nlev = 3 if samp else 7
            vn = 's' if samp else 'p'
            if i == 0 or samp:
                c.dma('sp', G['cg'][:], Dm['cg_' + vn][:]); c.dma('sp', G['rowm'][:], Dm['rowm_' + vn][:])
                if samp:
                    c.dma('pool', G['colm'][:], Dm['colm_' + vn][:])
            cg = G['cg']
            LOWI, LOWS, UPI, UPS, BLK = (cg[:, j, :] for j in range(5))
            S_full = G['S']
            if (not samp) and i == 0:
                c.memset('pool', S_full[:], 0.0)
            qkv = G['qkv_tm']
            for ch in range(6):
                pb = c.bank()
                c.tr(pb[:, 0:128], qkvT[:, ch, :], identf[:])
                c.copy('act', qkv[:, ch * 128:(ch + 1) * 128], pb[:, 0:128])
            pb = c.bank()
            for k in range(8):
                c.mm(pb[:, 0:8], xT[:, k, :], wA[:, k, O_GA:O_GA + 8], start=(k == 0), stop=(k == 7))
            c.copy('act', G['ab'][:], pb[:, 0:8])
            pb = c.bank()
            for k in range(8):
                c.mm(pb[:, 0:256], xT[:, k, :], wA[:, k, O_GG:O_GG + 256], start=(k == 0), stop=(k == 7))
            c.act(G['gate'][:], pb[:, 0:256], AF.Silu)
            t4, g4, beta4, gc4, egc4, egd4, bgc4 = (G[n] for n in ('t4', 'g4', 'beta4', 'gc4', 'egc4', 'egd4', 'bgc4'))
            c.tt('dve', t4[:], G['ab'][:, 0:4], G['dtb'][:], ALU.add)
            c.act(t4[:], t4[:], AF.Exp)
            c.ts('dve', t4[:], t4[:], 1.0, ALU.add)
            c.act(t4[:], t4[:], AF.Ln)
            c.tt('dve', g4[:], t4[:], G['negA'][:], ALU.mult)
            c.act(beta4[:], G['ab'][:, 4:8], AF.Sigmoid)
            pb = c.bank()
            c.mm(pb[:, 0:4], UPI, g4[:])
            c.mm(pb[:, 4:8], BLK, g4[:])
            c.copy('act', gc4[:], pb[:, 0:4])
            c.act(egc4[:], gc4[:], AF.Exp)
            c.tt('dve', egd4[:], pb[:, 4:8], gc4[:], ALU.subtract)
            c.act(egd4[:], egd4[:], AF.Exp)
            c.tt('dve', bgc4[:], beta4[:], egc4[:], ALU.mult)
            sc = G['sc']
            ss = G['ss']
            for hh in range(2):
                c.tt('dve', sc[:], qkv[:, hh * 256:(hh + 1) * 256].re('p (h d) -> p h d', h=4), qkv[:, hh * 256:(hh + 1) * 256].re('p (h d) -> p h d', h=4), ALU.mult)
                c.op('dve', lambda e: e.reduce_sum(ss.t[:, hh * 4:(hh + 1) * 4], sc.t[:], axis=AX.X), [sc], [ss])
            c.ts('dve', ss[:], ss[:], 1e-6, ALU.add)
            c.act(ss[:], ss[:], AF.Sqrt)
            c.op('dve', lambda e: e.reciprocal(ss.t[:], ss.t[:]), [ss], [ss])
            c.ts('dve', ss[:, 0:4], ss[:, 0:4], 0.125, ALU.mult)
            for hh in range(2):
                c.tt('dve', qkv[:, hh * 256:(hh + 1) * 256].re('p (h d) -> p h d', h=4), qkv[:, hh * 256:(hh + 1) * 256].re('p (h d) -> p h d', h=4),
                     ss[:, hh * 4:(hh + 1) * 4].re('p (h o) -> p h o', o=1).bc([128, 4, 64]), ALU.mult)
            def head_gen(h, HB):
                Qh, Kh, Vh = qkv[:, h * 64:(h + 1) * 64], qkv[:, 256 + h * 64:256 + (h + 1) * 64], qkv[:, 512 + h * 64:512 + (h + 1) * 64]
                hs = slice(h, h + 1)
                if samp:
                    c.dma('sp', S_full[:, :, 0, :], Dm['st_gdn'][l, :, h].re('s k v -> k s v'))
                    S = _HeadView(S_full, h)
                else:
                    S = S_full
                Vb, Kbg, Kd, Qg = HB['t1'][:, 0:64], HB['t1'][:, 64:128], HB['t2'][:, 0:64], HB['t2'][:, 64:128]
                c.ts('dve', Qg, Qh, egc4[:, hs], ALU.mult)
                pb = c.bank()
                c.tr(pb[0:64, 0:128], Kh, identf[:])
                c.tr(pb[0:64, 128:256], Qh, identf[:])
                c.tr(pb[0:64, 256:384], Qg, identf[:])
                kqT = HB['kqT']
                c.copy('act', kqT[:], pb[0:64, 0:384].re('p (a t) -> p a t', a=3))
                KT, QT, QgT = kqT[:, 0, :], kqT[:, 1, :], kqT[:, 2, :]
                dg = G['dg']
                c.ts('pool', dg[:, 0, :], identf[:], gc4[:, hs], ALU.mult)
                c.ts('pool', dg[:, 1, :], identf[:], beta4[:, hs], ALU.mult)
                pbG = c.bank()
                c.mm(pbG[:, 0:128], ones128[:], dg[:, 0, :])
                c.mm(pbG[:, 128:256], ones128[:], dg[:, 1, :])
                pbK = c.bank()
                c.mm(pbK[:, 0:128], KT, KT)
                c.mm(pbK[:, 128:256], KT, QT)
                t3 = G['t3']
                A, AT, AqkT = HB['A'], HB['AT'], HB['AqkT']
                c.ts('dve', t3[:], pbG[:, 0:128], gc4[:, hs], ALU.subtract, -1.0, ALU.mult)
                c.tt('pool', t3[:], t3[:], LOWI, ALU.mult)
                c.act(t3[:], t3[:], AF.Exp)
                c.tt('pool', t3[:], t3[:], LOWS, ALU.mult)
                c.stt(A[:], pbK[:, 0:128], beta4[:, hs], t3[:], ALU.mult, ALU.mult)
                c.ts('dve', t3[:], pbG[:, 0:128], gc4[:, hs], ALU.subtract)
                c.tt('pool', t3[:], t3[:], UPI, ALU.mult)
                c.act(t3[:], t3[:], AF.Exp)
                c.tt('pool', t3[:], t3[:], UPI, ALU.mult)
                c.tt('dve', AqkT[:], pbK[:, 128:256], t3[:], ALU.mult)
                c.tt('pool', t3[:], t3[:], UPS, ALU.mult)
                c.tt('dve', t3[:], pbG[:, 128:256], t3[:], ALU.mult)
                c.tt('dve', AT[:], pbK[:, 0:128], t3[:], ALU.mult)
                c.ts('pool', Vb, Vh, beta4[:, hs], ALU.mult)
                c.ts('pool', Kbg, Kh, bgc4[:, hs], ALU.mult)
                c.ts('pool', Kd, Kh, egd4[:, hs], ALU.mult)
                yield
                Db, DTb = [HB['D0'], HB['D1']], [HB['DT0'], HB['DT1']]
                X, XT = HB['X'], HB['XT']
                c.tt('pool', X[:], A[:], cg[:, 5, :], ALU.mult)
                c.tt('pool', Db[0][:], identf[:], X[:], ALU.subtract)
                c.tt('pool', XT[:], AT[:], cg[:, 12, :], ALU.mult)
                c.tt('pool', DTb[0][:], identf[:], XT[:], ALU.subtract)
                cur = 0
                pbI_ = c.hold()
                for k in range(1, nlev):
                    last = (k == nlev - 1)
                    Dc, DTc, Dn, DTn = Db[cur], DTb[cur], Db[1 - cur], DTb[1 - cur]
                    c.tt('pool', X[:], A[:], cg[:, 5 + k, :], ALU.mult)
                    if not last:
                        c.tt('pool', XT[:], AT[:], cg[:, 12 + k, :], ALU.mult)
                        c.mm(pbI_[:, 0:128], XT[:], Dc[:])
                    c.mm(pbI_[:, 256:384], X[:], DTc[:])
                    if not last:
                        c.copy('act', HB['Ys'][:], pbI_[:, 0:128])
                    c.copy('act', HB['Y2s'][:], pbI_[:, 256:384])
                    yield
                    if not last:
                        c.mm(pbI_[:, 128:256], DTc[:], HB['Ys'][:])
                    c.mm(pbI_[:, 384:512], Dc[:], HB['Y2s'][:])
                    if not last:
                        c.tt('dve', Dn[:], Dc[:], pbI_[:, 128:256], ALU.subtract)
                    c.tt('dve', DTn[:], DTc[:], pbI_[:, 384:512], ALU.subtract)
                    cur = 1 - cur
                    yield
                c.release(pbI_)
                TT_ = DTb[cur]
                pb = c.bank()
                c.mm(pb[0:64, 0:128], Kbg, TT_[:])
                negWT = G['negWT']
                c.ts('dve', negWT[:], pb[0:64, 0:128], -1.0, ALU.mult)
                Vnew = G['Vnew']
                bc16 = lambda v_, n_: v_.re('p (o t) -> p o t', o=1).bc([64, n_, 128])
                pbV = c.bank()
                c.mm(pbV[:, 0:64], TT_[:], Vb, start=True, stop=False)
                if nseq > 1:
                    for c4 in range(nseq // 4):
                        c.tt('pool', G['negWTm'][:], bc16(negWT[:], 4), G['colm'][:, c4 * 4:(c4 + 1) * 4, :], ALU.mult)
                        for s4 in range(4):
                            s_ = c4 * 4 + s4
                            c.mm(pbV[:, 0:64], G['negWTm'][:, s4, :], S[:, s_, h, :], start=False, stop=(s_ == nseq - 1), signal=True)
                else:
                    c.mm(pbV[:, 0:64], negWT[:], S[:, 0, h, :], start=False, stop=True)
                c.copy('act', Vnew[:], pbV[:, 0:64])
                pbO = c.bank()
                if nseq > 1:
                    for c4 in range(nseq // 4):
                        c.tt('pool', G['QgTm'][:], bc16(QgT, 4), G['colm'][:, c4 * 4:(c4 + 1) * 4, :], ALU.mult)
                        for s4 in range(4):
                            s_ = c4 * 4 + s4
                            c.mm(pbO[:, 0:64], G['QgTm'][:, s4, :], S[:, s_, h, :], start=(s_ == 0), stop=False, signal=True)
                else:
                    c.mm(pbO[:, 0:64], QgT, S[:, 0, h, :], start=True, stop=False)
                c.mm(pbO[:, 0:64], AqkT[:], Vnew[:], start=False, stop=True)
                c.copy('act', G['o_tm'][:, h * 64:(h + 1) * 64], pbO[:, 0:64])
                c.ts('pool', G['grow'][:], G['rowm'][:], g4[:, hs], ALU.mult)
                pbE = c.bank()
                c.mm(pbE[0:64, 0:16], ones128[:, 0:64], G['grow'][:])
                c.act(G['egl'][:], pbE[0:64, 0:16], AF.Exp)
                for s0 in range(0, nseq, 8):
                    n8 = min(8, nseq - s0)
                    pbS = c.bank()
                    if nseq > 1:
                        c.tt('pool', G['Kdm'][:], Kd.re('p (o d) -> p o d', o=1).bc([128, 8, 64]),
                             G['rowm'][:, s0:s0 + 8].re('p (s o) -> p s o', o=1).bc([128, 8, 64]), ALU.mult)
                    for s_ in range(s0, s0 + n8):
                        kd_ = G['Kdm'][:, s_ - s0, :] if nseq > 1 else Kd
                        c.mm(pbS[0:64, (s_ - s0) * 64:(s_ - s0 + 1) * 64], kd_, Vnew[:])
                    c.tt('dve', S[:, s0:s0 + n8, h, :], S[:, s0:s0 + n8, h, :],
                         G['egl'][:, s0:s0 + n8].re('p (s o) -> p s o', o=1).bc([64, n8, 64]), ALU.mult)
                    c.tt('dve', S[:, s0:s0 + n8, h, :], S[:, s0:s0 + n8, h, :], pbS[0:64, 0:n8 * 64].re('p (s v) -> p s v', s=n8), ALU.add)
                if samp:
                    c.dma('sp', Dm['gs_s'][l, :, h].re('s k v -> k s v'), S_full[:, :, 0, :])
            sets = G['hsets']
            groups = {}
            for h in range(4):
                groups.setdefault(h % len(sets), []).append(h)
            lanes = [iter(_chain([head_gen(h, sets[k_]) for h in hs_])) for k_, hs_ in groups.items()]
            while lanes:
                for ln in list(lanes):
                    try:
                        next(ln)
                    except StopIteration:
                        lanes.remove(ln)
            o_tm, rr = G['o_tm'], G['rr']
            c.tt('dve', sc[:], o_tm[:].re('p (h d) -> p h d', h=4), o_tm[:].re('p (h d) -> p h d', h=4), ALU.mult)
            c.op('dve', lambda e: e.reduce_sum(rr.t[:], sc.t[:], axis=AX.X), [sc], [rr])
            c.ts('dve', rr[:], rr[:], 1.0 / 64, ALU.mult, 1e-6, ALU.add)
            c.act(rr[:], rr[:], AF.Sqrt)
            c.op('dve', lambda e: e.reciprocal(rr.t[:], rr.t[:]), [rr], [rr])
            c.tt('dve', o_tm[:].re('p (h d) -> p h d', h=4), o_tm[:].re('p (h d) -> p h d', h=4),
                 rr[:].re('p (h o) -> p h o', o=1).bc([128, 4, 64]), ALU.mult)
            c.tt('dve', o_tm[:].re('p (h d) -> p h d', h=4), o_tm[:].re('p (h d) -> p h d', h=4),
                 G['gain'][:].re('p (o d) -> p o d', o=1).bc([128, 4, 64]), ALU.mult)
            c.tt('dve', o_tm[:], o_tm[:], G['gate'][:], ALU.mult)
            if samp and os.environ.get('DBG'):
                pass
            pb = c.bank()
            for cc in range(2):
                c.tr(pb[:, cc * 128:(cc + 1) * 128], o_tm[:, cc * 128:(cc + 1) * 128], identf[:])
            c.copy('act', yT[:, 2:4, :], pb[:, 0:256].re('p (a t) -> p a t', a=2))
            if (not samp) and i == NTILES - 1:
                c.dma('sp', Dm['gs_p'][l].re('h k v -> k h v'), S_full[:, 0, :, :])


        def proj_q(wA, N):
            for h2_ in range(2):
                pb = c.bank()
                for hl in range(2):
                    hh = h2_ * 2 + hl
                    for g in range(2):
                        c0 = O_NQ + g * 256 + hh * 64 - g * 64
                        r = (hl * 2 + g) * 128
                        for k in range(8):
                            c.mm(pb[:, r:r + 128], wA[:, k, c0:c0 + 128], xT[:, k, :], start=(k == 0), stop=(k == 7))
                for g in range(2):
                    ps = slice(g * 64, (g + 1) * 64)
                    src = pb[ps, :].re('p (hl g t) -> p hl g t', hl=2, g=2)[:, :, g, :]
                    c.copy('act', N['qT'][ps, h2_ * 2:h2_ * 2 + 2, :], src)

        def nsa_setup(l, N):
            for half in range(2):
                if os.environ.get('SKIP_W1'):
                    continue
                for a8 in range(8):
                    c.dma('pool', N['w1b'][half * 64:(half + 1) * 64, a8 * 8:(a8 + 1) * 8, :],
                          Dm['cmp_w1'][l].re('a l d e -> d (a l) e')[:, a8 * 8:(a8 + 1) * 8, :])
            c.dma('pool', N['w2b'][:], Dm['cmp_w2'][l].re('a e d -> e a d'))
            c.dma('pool', N['ov'][:], Dm['ov'][:])
            c.dma('sp', N['cq'][:], Dm['cq'][:])
            c.memset('pool', N['w2pad'][:], 0.0)
            c.copy('pool', N['w2pad'][:, 0, 0:64], N['w2b'][:, 0, :])
            c.copy('pool', N['w2pad'][:, 1, 64:128], N['w2b'][:, 0, :])
            for a in range(2):
                c.dma('sp', N['peT'][:, a, :], Dm['cmp_pe'][l, a].re('l d -> d l'), slow=True)
            c.dma('sp', N['b1T'][:], Dm['cmp_b1'][l].re('a e -> e a'), slow=True)
            c.copy('dve', N['peTb'][:], N['peT'][:])
            pb = c.bank()
            for a in range(2):
                for l_ in range(32):
                    c.mm(pb[0:64, a:a + 1], N['w1b'][0:64, a * 32 + l_, :], N['peTb'][:, a, l_:l_ + 1], start=(l_ == 0), stop=(l_ == 31))
            c.tt('dve', N['biasv'][:], pb[0:64, 0:2], N['b1T'][:], ALU.add)
            c.memset('pool', N['kcT'][:], 0.0)
            c.memset('pool', N['hidvT'][:], 0.0)
            c.memset('pool', N['VC1'][:], 1.0)

        def nsa_tile_p(l, i, N, G, wA, kvtm, yT):
            cg = G['cg']
            LOWS, UPI = cg[:, 1, :], cg[:, 2, :]
            proj_q(wA, N)
            for col, dst in ((O_NKV + 256, N['kslcT'][:, i * 128:(i + 1) * 128]), (O_NWIN, N['kwinT'][:, (i % 8) * 128:(i % 8 + 1) * 128]),
                             (O_NKV, N['cmpT'][:, 0, 16:144]), (O_NKV + 128, N['cmpT'][:, 1, 16:144])):
                pb = c.bank()
                proj_chunk(wA, col, 128, pb)
                c.copy('act', dst, pb[:, 0:128])
            c.copy('pool', N['vslc1'][:, i, :, 0:64], kvtm[:, 384:512].re('p (g d) -> p g d', g=2))
            c.copy('pool', N['vwin1'][:, i % 8, :, 0:64], kvtm[:, 640:768].re('p (g d) -> p g d', g=2))
            nb, col0, c0 = (7, 16, 0) if i == 0 else (8, 0, 8 * i - 1)
            hv = lambda t_: t_[0:64, 0:16].re('p (g j) -> p g j', g=2)[:, :, 0:nb]
            hx, h2, hb = N['hx'], N['h2'], N['hb']
            for a in range(2):
                for g in range(2):
                    ps = slice(g * 64, (g + 1) * 64)
                    pbh = c.bank()
                    for l_ in range(32):
                        rhs = N['cmpT'][ps, a, col0 + l_:col0 + l_ + 16 * (nb - 1) + 1:16]
                        c.mm(pbh[0:64, 0:nb], N['w1b'][ps, a * 32 + l_, :], rhs, start=(l_ == 0), stop=(l_ == 31))
                    c.ts('dve', hx[0:64, g * 8:g * 8 + nb], pbh[0:64, 0:nb], N['biasv'][:, a:a + 1], ALU.add)
                c.tt('dve', hv(h2), hv(hx), hv(hx), ALU.mult)
                c.ts('dve', hv(h2), hv(h2), 0.044715, ALU.mult, 1.0, ALU.add)
                c.tt('dve', hv(h2), hv(h2), hv(hx), ALU.mult)
                c.act(hv(h2), hv(h2), AF.Tanh, scale=0.7978845608028654)
                c.ts('dve', hv(h2), hv(h2), 1.0, ALU.add, 0.5, ALU.mult)
                c.tt('dve', hv(hb), hv(h2), hv(hx), ALU.mult)
                if a == 0:
                    pbk = c.bank()
                    for g in range(2):
                        c.mm(pbk[:, g * 8:g * 8 + nb], N['w2pad'][:, g, :], hb[:, g * 8:g * 8 + nb])
                    for g in range(2):
                        c.copy('act', N['kcT'][g * 64:(g + 1) * 64, c0:c0 + nb], pbk[g * 64:(g + 1) * 64, g * 8:g * 8 + nb])
                else:
                    c.copy('act', N['hidvT'][:, :, c0:c0 + nb], hv(hb))
            c.copy('pool', N['cmpT'][:, :, 0:16], N['cmpT'][:, :, 128:144])
            nct = 1 if 8 * i + 6 < 128 else 2
            for ct in range(nct):
                pbv = c.bank()
                for g in range(2):
                    c.mm(pbv[:, g * 64:(g + 1) * 64], N['hidvT'][:, g, ct * 128:(ct + 1) * 128], N['w2b'][:, 1, :])
                c.copy('act', N['VC1'][:, ct, :, 0:64], pbv[:, 0:128].re('p (g d) -> p g d', g=2))
            pbg = c.bank()
            for k in range(8):
                c.mm(pbg[:, 0:24], xT[:, k, :], wA[:, k, O_NG:O_NG + 24], start=(k == 0), stop=(k == 7))
            c.act(N['gate'][:], pbg[:, 0:24], AF.Sigmoid)
            c.dma('sp', N['impm'][:], Dm['impm'][i])
            c.dma('sp', N['impb'][:], Dm['impb'][i])
            Ebufs = [N['E0'], N['E1']]
            bc4 = lambda v_: v_.re('p (o t) -> p o t', o=1).bc([128, 4, 128])
            for g in range(2):
                ps = slice(g * 64, (g + 1) * 64)
                qg = N['qT'][ps, :, :].re('p h t -> p (h t)')

                def attend(kts, lhs_of, v1_of, mask_of, pbO, extra=None):
                    def stage_a(idx):
                        pbS = c.bank()
                        c.mm(pbS[:, :], lhs_of(kts[idx]), qg)
                        Eb_ = Ebufs[idx % 2]
                        c.act(Eb_[:], pbS[:, :].re('p (h t) -> p h t', h=4), AF.Exp, scale=0.125)
                        mask_of(kts[idx], Eb_)
                        return Eb_
                    Enext = stage_a(0)
                    for idx, kt in enumerate(kts):
                        Eb = Enext
                        if idx + 1 < len(kts):
                            Enext = stage_a(idx + 1)
                        for hh in range(4):
                            if extra is not None:
                                extra(kt, idx, hh, Eb)
                            c.mm(pbO[:, hh * 65:(hh + 1) * 65], Eb[:, hh, :], v1_of(kt), start=(idx == 0 and hh == 0), stop=(idx == len(kts) - 1 and hh == 3))

                def finalize(pbO, br, first):
                    o3 = pbO[:, 0:260].re('p (h e) -> p h e', h=4)
                    rs = N['rs']
                    c.ts('dve', rs[:], o3[:, :, 64], 1e-30, ALU.max)
                    c.op('dve', lambda e: e.reciprocal(rs.t[:], rs.t[:]), [rs], [rs])
                    c.tt('dve', rs[:], rs[:], N['gate'][:, br * 8 + g * 4:br * 8 + g * 4 + 4], ALU.mult)
                    dst = N['ytm'][:, g * 256:(g + 1) * 256].re('p (h d) -> p h d', h=4)
                    rb = rs[:].re('p (h o) -> p h o', o=1).bc([128, 4, 64])
                    if first:
                        c.tt('dve', dst, o3[:, :, 0:64], rb, ALU.mult)
                    else:
                        c.tt('dve', N['tmp'][:], o3[:, :, 0:64], rb, ALU.mult)
                        c.tt('pool', dst, dst, N['tmp'][:], ALU.add)

                pbI = c.hold()
                pbOc = c.hold()

                def mask_cmp(ct, Eb):
                    thr = float(128 * i - 31 - 2048 * ct)
                    c.op('pool', lambda e: e.tensor_single_scalar(N['cm'].t[:], N['cq'].t[:], thr, ALU.is_le), [N['cq']], [N['cm']])
                    c.tt('pool', Eb[:], Eb[:], bc4(N['cm'][:]), ALU.mult)

                def extra_imp(ct, idx, hh, Eb):
                    c.mm(pbI[:, hh * 65:(hh + 1) * 65], Eb[:, hh, :], N['ov'][:, ct, :], start=(idx == 0 and hh == 0), stop=(idx == nct - 1 and hh == 3))

                attend(list(range(nct)), lambda ct: N['kcT'][ps, ct * 128:(ct + 1) * 128], lambda ct: N['VC1'][:, ct, g, :], mask_cmp, pbOc, extra_imp)
                i3 = pbI[:, 0:260].re('p (h e) -> p h e', h=4)
                rs2, imp = N['rs2'], N['imp']
                c.ts('dve', rs2[:], i3[:, :, 64], 1e-30, ALU.max)
                c.op('dve', lambda e: e.reciprocal(rs2.t[:], rs2.t[:]), [rs2], [rs2])
                c.ts('dve', imp[:], i3[:, 0, 0:64], rs2[:, 0:1], ALU.mult)
                for hh in range(1, 4):
                    c.stt(imp[:], i3[:, hh, 0:64], rs2[:, hh:hh + 1], imp[:], ALU.mult, ALU.add)
                c.tt('dve', imp[:], imp[:], N['impm'][:], ALU.mult)
                c.tt('dve', imp[:], imp[:], N['impb'][:], ALU.add)
                c.op('dve', lambda e: e.max(N['m8'].t[:], imp.t[:]), [imp], [N['m8']])
                c.ts('dve', N['sel'][:], imp[:], N['m8'][:, 7:8], ALU.is_ge)
                pbT = c.bank()
                c.tr(pbT[0:64, 0:128], N['sel'][:], identf[:])
                c.copy('act', N['selT'][:], pbT[0:64, 0:128])
                finalize(pbOc, 0, True)
                c.release(pbI)
                c.release(pbOc)
                pbO = c.hold()

                def mask_slc(kt, Eb):
                    pbM = c.bank()
                    c.mm(pbM[:, 0:128], N['eall'][:, kt * 128:(kt + 1) * 128], N['selT'][:])
                    c.tt('dve', Eb[:], Eb[:], bc4(pbM[:, 0:128]), ALU.mult)
                    if kt == i:
                        c.tt('pool', Eb[:], Eb[:], bc4(UPI), ALU.mult)

                attend(list(range(0, i + 1)), lambda kt: N['kslcT'][ps, kt * 128:(kt + 1) * 128], lambda kt: N['vslc1'][:, kt, g, :], mask_slc, pbO)
                finalize(pbO, 1, False)
                c.release(pbO)
                pbO = c.hold()

                def mask_win(kt, Eb):
                    if kt == i:
                        c.tt('pool', Eb[:], Eb[:], bc4(UPI), ALU.mult)
                    if kt == i - 4:
                        c.tt('pool', Eb[:], Eb[:], bc4(LOWS), ALU.mult)

                attend(list(range(max(0, i - 4), i + 1)), lambda kt: N['kwinT'][ps, (kt % 8) * 128:(kt % 8 + 1) * 128],
                       lambda kt: N['vwin1'][:, kt % 8, g, :], mask_win, pbO)
                finalize(pbO, 2, False)
                c.release(pbO)
            pb = c.bank()
            for cc in range(4):
                c.tr(pb[:, cc * 128:(cc + 1) * 128], N['ytm'][:, cc * 128:(cc + 1) * 128], identf[:])
            c.copy('act', yT[:, 4:8, :], pb[:].re('p (a t) -> p a t', a=4))


        def nsa_tile_s(l, N, G, wA, kvtm, yT):
            P_ = cfg.PAST
            npg = P_ // 128
            WS = min(512, P_)
            nwt = WS // 128
            nblk = P_ // 16 - 1
            cgS = G['cg']
            c.dma('sp', N['ptb'][:], V(Dm['page_table'], I['page_table'].partition_broadcast(128)).re('p o n -> p (o n)'))
            c.ts('dve', N['idxA'][:], N['ptb'][:], float(l * cfg.NPOOL), ALU.add, 128.0, ALU.mult)
            c.ts('dve', N['idxA'][:], N['idxA'][:], piota[:, 0:1], ALU.add, 2.0, ALU.mult)
            c.ts('dve', N['idxB'][:], N['idxA'][:], 1.0, ALU.add)
            c.dma('pool', N['eall_s'][:], Dm['eall_s'][:])
            c.dma('sp', N['smask'][:], Dm['smask'][:])
            c.dma('sp', N['impm'][:], Dm['impm_s'][:])
            c.dma('sp', N['impb'][:], Dm['impb_s'][:])
            c.memset('pool', N['Ebig'][:], 0.0)
            c.memset('pool', N['VCs'][:], 1.0)
            c.memset('pool', N['vs1'][:], 1.0)
            c.memset('pool', N['vw1'][:], 1.0)
            c.memset('pool', N['v1n'][:], 1.0)
            proj_q(wA, N)
            c.copy('pool', N['qTs'][:].re('p s h t -> p h s t'), N['qT'][:].re('p h (s t) -> p h s t', s=NS))
            for col, dst in ((O_NKV + 256, N['kslcTn']), (O_NWIN, N['kwinTn'])):
                pb = c.bank()
                proj_chunk(wA, col, 128, pb)
                c.copy('act', dst[:], pb[:, 0:128])
            pbg = c.bank()
            for k in range(8):
                c.mm(pbg[:, 0:24], xT[:, k, :], wA[:, k, O_NG:O_NG + 24], start=(k == 0), stop=(k == 7))
            c.act(N['gate'][:], pbg[:, 0:24], AF.Sigmoid)

            def gather(idx, s_, h0, n8):
                for j in range(n8):
                    col = s_ * npg + h0 + j
                    def fn(e, j=j, col=col):
                        return e.indirect_dma_start(out=N['pk'].t[:, j, :], out_offset=None, in_=I['cache_kv'],
                                                    in_offset=bass.IndirectOffsetOnAxis(ap=idx.t[:, col:col + 1], axis=0))
                    c.dma_custom('pool', fn, [Dm['cache_kv'], idx], [N['pk']])

            def transpose_pages(src_of, dst, ntile):
                for j0 in range(0, ntile, 8):
                    n8 = min(8, ntile - j0)
                    pb = c.bank()
                    pbb = V(pb, pb.t[:].bitcast(BF16))
                    for j in range(j0, j0 + n8):
                        c.tr(pbb[:, (j - j0) * 128:(j - j0 + 1) * 128], src_of(j), identb[:], signal=(j == j0 + n8 - 1))
                    c.copy('act', dst[:, j0 * 128:(j0 + n8) * 128], pbb[:, 0:n8 * 128])

            def ocols(pbO, s_):
                return pbO[0:65, s_ * 32:(s_ + 1) * 32]

            def finalize_s(pbO, g, br, first):
                c.copy('act', N['osb'][:].re('p (h s t) -> p s h t', h=4, s=NS), pbO[0:65, :].re('p (s h t) -> p s h t', s=NS, h=4))
                pbt = c.bank()
                for hh in range(4):
                    c.tr(pbt[:, hh * 65:(hh + 1) * 65], N['osb'][:, hh * 128:(hh + 1) * 128], identf[0:65, 0:65])
                o3 = pbt[:, 0:260].re('p (h e) -> p h e', h=4)
                rs = N['rs']
                c.ts('dve', rs[:], o3[:, :, 64], 1e-30, ALU.max)
                c.op('dve', lambda e: e.reciprocal(rs.t[:], rs.t[:]), [rs], [rs])
                c.tt('dve', rs[:], rs[:], N['gate'][:, br * 8 + g * 4:br * 8 + g * 4 + 4], ALU.mult)
                dst = N['ytm'][:, g * 256:(g + 1) * 256].re('p (h d) -> p h d', h=4)
                rb = rs[:].re('p (h o) -> p h o', o=1).bc([128, 4, 64])
                if first:
                    c.tt('dve', dst, o3[:, :, 0:64], rb, ALU.mult)
                else:
                    c.tt('dve', N['tmp'][:], o3[:, :, 0:64], rb, ALU.mult)
                    c.tt('pool', dst, dst, N['tmp'][:], ALU.add)

            hx, h2, hb = N['hxs'], N['h2s'], N['hbs']
            hv = lambda t_: t_[:, :, 0:nblk]
            pbOc = [c.hold(), c.hold()]
            if os.environ.get('SWAPB'):
                pbOc = pbOc[::-1]
            for s_ in range(NS):
                for h0 in range(0, npg, 8):
                    n8 = min(8, npg - h0)
                    gather(N['idxA'], s_, h0, n8)
                    for a in range(2):
                        transpose_pages(lambda j: N['pk'][:, j, a * 128:(a + 1) * 128], N['cmpTs'][:, a, h0 * 128:(h0 + n8) * 128], n8)
                for a in range(2):
                    for g in range(2):
                        ps = slice(g * 64, (g + 1) * 64)
                        pbh = c.bank()
                        for l_ in range(32):
                            rhs = N['cmpTs'][ps, a, l_:l_ + 16 * (nblk - 1) + 1:16]
                            c.mm(pbh[0:64, 0:nblk], N['w1b'][ps, a * 32 + l_, :], rhs, start=(l_ == 0), stop=(l_ == 31))
                        c.ts('dve', hx[:, g, 0:nblk], pbh[0:64, 0:nblk], N['biasv'][:, a:a + 1], ALU.add)
                    c.tt('dve', hv(h2), hv(hx), hv(hx), ALU.mult)
                    c.ts('dve', hv(h2), hv(h2), 0.044715, ALU.mult, 1.0, ALU.add)
                    c.tt('dve', hv(h2), hv(h2), hv(hx), ALU.mult)
                    c.act(hv(h2), hv(h2), AF.Tanh, scale=0.7978845608028654)
                    c.ts('dve', hv(h2), hv(h2), 1.0, ALU.add, 0.5, ALU.mult)
                    c.tt('dve', hv(hb), hv(h2), hv(hx), ALU.mult)
                    if a == 0:
                        pbk = c.bank()
                        for g in range(2):
                            c.mm(pbk[:, g * 128:g * 128 + nblk], N['w2pad'][:, g, :], hb[:, g, 0:nblk])
                        for g in range(2):
                            c.copy('act', N['kcTs'][g * 64:(g + 1) * 64, 0:nblk], pbk[g * 64:(g + 1) * 64, g * 128:g * 128 + nblk])
                    else:
                        pbv = c.bank()
                        for g in range(2):
                            c.mm(pbv[0:nblk, g * 64:(g + 1) * 64], hb[:, g, 0:nblk], N['w2b'][:, 1, :])
                        c.copy('act', N['VCs'][0:nblk, :, 0:64], pbv[0:nblk, 0:128].re('p (g d) -> p g d', g=2))
                if os.environ.get('DBG') and s_ == NS - 1:
                    dst_ = N['dbgst']
                    c.memset('dve', dst_[:], 0.0)
                    c.copy('dve', dst_[:, 0:nblk], N['kcTs'][:, 0:nblk])
                    c.copy('dve', dst_[0:nblk, 128:258], N['VCs'][0:nblk, :, :].re('p g e -> p (g e)'))
                    c.copy('dve', dst_[0:64, 260:260 + 2 * nblk].re('p (g j) -> p g j', g=2), hb[:, :, 0:nblk])
                    c.copy('dve', dst_[:, 300:332], N['qTs'][:, s_, :, :].re('p h t -> p (h t)'))
                    c.copy('dve', dst_[:, 0:512], N['w1b'][:, 0:8, :].re('p a e -> p (a e)'))
                    c.dma('sp', Dm['dbg2'][:, 1280:1280 + 640], dst_[:])
                for g in range(2):
                    ps = slice(g * 64, (g + 1) * 64)
                    qsel = N['qTs'][ps, s_, :, :].re('p h t -> p (h t)')
                    pbS = c.bank()
                    c.mm(pbS[0:nblk, 0:32], N['kcTs'][ps, 0:nblk], qsel)
                    c.act(N['Ecur'][0:nblk, :], pbS[0:nblk, 0:32], AF.Exp, scale=0.125)
                    c.copy('pool', N['Ebig'][0:nblk, g, :, s_ * 8:(s_ + 1) * 8], N['Ecur'][0:nblk, :].re('p (h t) -> p h t', h=4))
                    c.mm(ocols(pbOc[g], s_), N['VCs'][0:nblk, g, :], N['Ecur'][0:nblk, :])
            for g in range(2):
                pbI = c.bank()
                for hh in range(4):
                    c.mm(pbI[:, hh * 65:(hh + 1) * 65], N['Ebig'][:, g, hh, :], N['ov'][:, 0, :], start=(hh == 0), stop=(hh == 3))
                i3 = pbI[:, 0:260].re('p (h e) -> p h e', h=4)
                rs2, imp = N['rs2'], N['imp']
                c.ts('dve', rs2[:], i3[:, :, 64], 1e-30, ALU.max)
                c.op('dve', lambda e: e.reciprocal(rs2.t[:], rs2.t[:]), [rs2], [rs2])
                c.ts('dve', imp[:], i3[:, 0, 0:64], rs2[:, 0:1], ALU.mult)
                for hh in range(1, 4):
                    c.stt(imp[:], i3[:, hh, 0:64], rs2[:, hh:hh + 1], imp[:], ALU.mult, ALU.add)
                c.tt('dve', imp[:], imp[:], N['impm'][:], ALU.mult)
                c.tt('dve', imp[:], imp[:], N['impb'][:], ALU.add)
                c.op('dve', lambda e: e.max(N['m8'].t[:], imp.t[:]), [imp], [N['m8']])
                c.ts('dve', N['sel'][:], imp[:], N['m8'][:, 7:8], ALU.is_ge)
                pbT = c.bank()
                c.tr(pbT[0:64, 0:128], N['sel'][:], identf[:])
                c.copy('act', N['selTs'][:, g, :], pbT[0:64, 0:128])
                finalize_s(pbOc[g], g, 0, True)
                c.release(pbOc[g])
                if os.environ.get('DBG'):
                    c.dma('sp', Dm['dbg2'][:, g * 256:(g + 1) * 256], N['ytm'][:, g * 256:(g + 1) * 256])
                    c.dma('sp', Dm['dbg2'][:, 1024 + g * 64:1024 + (g + 1) * 64], N['sel'][:])
            pbOs = [c.hold(), c.hold()]
            pbOw = [c.hold(), c.hold()]
            kTs = N['cmpTs'][:, 0, :]
            for s_ in range(NS):
                for h0 in range(0, npg, 8):
                    n8 = min(8, npg - h0)
                    gather(N['idxB'], s_, h0, n8)
                    transpose_pages(lambda j: N['pk'][:, j, 0:128], kTs[:, h0 * 128:(h0 + n8) * 128], n8)
                    c.copy('pool', N['vs1'][:, h0:h0 + n8, :, 0:64], N['pk'][:, 0:n8, 128:256].re('p j (g d) -> p j g d', g=2))
                c.dma('pool', N['pw'][:], Dm['cache_win'][l, s_].re('(j r) f -> r j f', r=128))
                transpose_pages(lambda j: N['pw'][:, j, 0:128], N['kwTs'][:, :], nwt)
                c.copy('pool', N['vw1'][:, :, :, 0:64], N['pw'][:, :, 128:256].re('p j (g d) -> p j g d', g=2))
                pbn = c.bank()
                c.mm(pbn[0:8, 0:128], identf[:, s_ * 8:(s_ + 1) * 8], kvtm[:, 384:512])
                c.mm(pbn[0:8, 128:256], identf[:, s_ * 8:(s_ + 1) * 8], kvtm[:, 640:768])
                c.copy('act', N['v1n'][:, :, :, 0:64], pbn[0:8, 0:256].re('p (b g d) -> p b g d', b=2, g=2))
                for g in range(2):
                    ps = slice(g * 64, (g + 1) * 64)
                    qsel = N['qTs'][ps, s_, :, :].re('p h t -> p (h t)')
                    for br, (KT_, V1_, KTn, ntile, pbO) in enumerate(((kTs, N['vs1'], N['kslcTn'], npg, pbOs[g]),
                                                                      (N['kwTs'][:, :], N['vw1'], N['kwinTn'], nwt, pbOw[g]))):
                        first = True
                        for j0 in range(0, ntile, 8):
                            n8 = min(8, ntile - j0)
                            pbS = c.bank()
                            for j in range(j0, j0 + n8):
                                c.mm(pbS[:, (j - j0) * 32:(j - j0 + 1) * 32], KT_[ps, j * 128:(j + 1) * 128], qsel)
                            Es = N['Es']
                            c.act(Es[:, 0:n8 * 32], pbS[:, 0:n8 * 32], AF.Exp, scale=0.125)
                            E4 = Es[:, 0:n8 * 32].re('p (j h t) -> p j h t', j=n8, h=4)
                            if br == 0:
                                pbM = c.bank()
                                for j in range(j0, j0 + n8):
                                    c.mm(pbM[:, (j - j0) * 8:(j - j0 + 1) * 8], N['eall_s'][:, j * 128:(j + 1) * 128], N['selTs'][:, g, s_ * 8:(s_ + 1) * 8])
                                for hh in range(4):
                                    c.tt('dve', E4[:, :, hh, :], E4[:, :, hh, :], pbM[:, 0:n8 * 8].re('p (j t) -> p j t', j=n8), ALU.mult)
                            elif j0 == 0 and WS == 512:
                                c.tt('pool', E4[:, 0, :, :], E4[:, 0, :, :], N['smask'][:, 0:8].re('p (o t) -> p o t', o=1).bc([128, 4, 8]), ALU.mult)
                            for j in range(j0, j0 + n8):
                                c.mm(ocols(pbO, s_), V1_[:, j, g, :], Es[:, (j - j0) * 32:(j - j0 + 1) * 32], start=first, stop=False)
                                first = False
                        pbS = c.bank()
                        c.mm(pbS[0:8, 0:32], KTn[ps, s_ * 8:(s_ + 1) * 8], qsel)
                        En = N['En']
                        c.act(En[:], pbS[0:8, 0:32].re('p (h t) -> p h t', h=4), AF.Exp, scale=0.125)
                        c.tt('pool', En[:], En[:], N['smask'][0:8, 8:16].re('p (o t) -> p o t', o=1).bc([8, 4, 8]), ALU.mult)
                        c.mm(ocols(pbO, s_), N['v1n'][:, br, g, :], En[:].re('p h t -> p (h t)'), start=False, stop=True)
            for g in range(2):
                finalize_s(pbOs[g], g, 1, False)
                c.release(pbOs[g])
                if os.environ.get('DBG'):
                    c.dma('sp', Dm['dbg2'][:, 512 + g * 256:512 + (g + 1) * 256], N['ytm'][:, g * 256:(g + 1) * 256])
                finalize_s(pbOw[g], g, 2, False)
                c.release(pbOw[g])
            pb = c.bank()
            for cc in range(4):
                c.tr(pb[:, cc * 128:(cc + 1) * 128], N['ytm'][:, cc * 128:(cc + 1) * 128], identf[:])
            c.copy('act', yT[:, 4:8, :], pb[:].re('p (a t) -> p a t', a=4))

        c.dma('sp', lnG[:], V(Dm['ln_emb'], I['ln_emb'].partition_broadcast(128)))
        tiles = list(range(NTILES + 1))

        def xsrc(i):
            return Dm['x_p'][i * 128:(i + 1) * 128, :] if i < NTILES else Dm['x_s'][:, :]

        for i in tiles:
            b = xt[i % 2]
            c.dma('sp', b[:], xsrc(i))
            layer_norm(b[:], xr[:], lnG)
            c.dma('sp', XS0[i][:], xr[:])

        for l in range(L):
            with ExitStack() as ph:
                wA = c.sb([128, 8, NIN], BF16, 'wA', ph)
                wO = c.sb([128, 8, D], BF16, 'wO', ph)
                yT = c.sb([128, 8, 128], BF16, 'yT', ph)
                cwa = c.sb([128, 2, 3], F32, 'cwa', ph); cwg = c.sb([128, 6, 4], F32, 'cwg', ph)
                CH = {}
                qkvT = c.sb([128, 6, 128], F32, 'qkvT', ph)
                kvtm = c.sb([128, 768], F32, 'kvtm', ph)
                G = {}
                N = {}
                if cfg.mix_b:
                    for nm, shp in (('cg', [128, 19, 128]), ('rowm', [128, 16]), ('qkv_tm', [128, 768]),
                                    ('ab', [128, 8]), ('gate', [128, 256]), ('t4', [128, 4]), ('g4', [128, 4]), ('beta4', [128, 4]),
                                    ('gc4', [128, 4]), ('egc4', [128, 4]), ('egd4', [128, 4]), ('bgc4', [128, 4]), ('ss', [128, 8]),
                                    ('dtb', [128, 4]), ('negA', [128, 4]), ('gain', [128, 64]), ('sc', [128, 4, 64]),
                                    ('kqT', [64, 3, 128]), ('dg', [128, 2, 128]), ('t1', [128, 128]), ('t2', [128, 128]), ('t3', [128, 128]),
                                    ('A', [128, 128]), ('AT', [128, 128]), ('AqkT', [128, 128]), ('X', [128, 128]), ('XT', [128, 128]),
                                    ('D0', [128, 128]), ('D1', [128, 128]), ('DT0', [128, 128]), ('DT1', [128, 128]),
                                    ('Ys', [128, 128]), ('Y2s', [128, 128]), ('negWT', [64, 128]),
                                    ('Vnew', [128, 64]), ('o_tm', [128, 256]), ('egl', [64, 16]), ('grow', [128, 16]), ('rr', [128, 4])):
                        G[nm] = c.sb(shp, F32, nm, ph)
                    G['hset0'] = {nm: G[nm] for nm in ['kqT', 't1', 't2', 'A', 'AT', 'AqkT', 'X', 'XT', 'D0', 'D1', 'DT0', 'DT1', 'Ys', 'Y2s']}
                    c.dma('sp', G['dtb'][:], V(Dm['gdn_dt_bias'], I['gdn_dt_bias'][l].partition_broadcast(128)))
                    c.dma('sp', G['negA'][:], V(Dm['gdn_a_log'], I['gdn_a_log'][l].partition_broadcast(128)))
                    c.dma('sp', G['gain'][:], V(Dm['gdn_norm_g'], I['gdn_norm_g'][l].partition_broadcast(128)))
                    c.act(G['negA'][:], G['negA'][:], AF.Exp)
                    c.ts('dve', G['negA'][:], G['negA'][:], -1.0, ALU.mult)
                if cfg.mix_c:
                    for nm, shp, dt_ in (('w1b', [128, 64, 64], BF16), ('w2b', [64, 2, 64], BF16), ('w2pad', [64, 2, 128], BF16),
                                         ('peT', [64, 2, 32], F32), ('peTb', [64, 2, 32], BF16), ('b1T', [64, 2], F32), ('biasv', [64, 2], F32),
                                         ('ov', [128, 2, 65], BF16), ('cq', [128, 128], F32), ('qT', [128, 4, 128], BF16),
                                         ('cmpT', [128, 2, 144], BF16), ('hx', [64, 16], F32), ('h2', [64, 16], F32), ('hb', [64, 16], BF16),
                                         ('kcT', [128, 256], BF16), ('hidvT', [64, 2, 256], BF16), ('VC1', [128, 2, 2, 65], BF16),
                                         ('gate', [128, 24], F32), ('impm', [128, 64], F32), ('impb', [128, 64], F32), ('imp', [128, 64], F32),
                                         ('sel', [128, 64], F32), ('m8', [128, 8], F32), ('selT', [64, 128], BF16), ('rs', [128, 4], F32),
                                         ('rs2', [128, 4], F32), ('ytm', [128, 512], F32), ('tmp', [128, 4, 64], F32), ('cm', [128, 128], F32),
                                         ('E0', [128, 4, 128], BF16), ('E1', [128, 4, 128], BF16)):
                        N[nm] = c.sb(shp, dt_, 'n_' + nm, ph)
                    nsa_setup(l, N)
                WP, WS = min(512, T), min(512, cfg.PAST)
                c.dma('sp', Dm['win_s'][l, :, 0:WS - TS, :], Dm['cache_win'][l, :, TS:WS, :])
                for k in range(8):
                    c.dma('pool', wA[:, k, :], Dm['w_in'][l, k * 128:(k + 1) * 128, :])
                    c.dma('pool', wO[:, k, :], Dm['w_out'][l, k * 128:(k + 1) * 128, :])
                c.dma('sp', lnG[:], V(Dm['ln1'], I['ln1'][l].partition_broadcast(128)))
                for tap in range(3):
                    c.dma('sp', cwa[:, :, tap], Dm['conv_a_w'][l, tap].re('(c p) -> p c', p=128), slow=True)
                for tap in range(4):
                    c.dma('sp', cwg[:, :, tap], Dm['gdn_conv_w'][l, tap].re('(c p) -> p c', p=128), slow=True)

                def run_tile(i):
                    samp = (i == NTILES)
                    nseq, TT = (NS, TS) if samp else (1, 128)
                    chA = CH['A']
                    xb = xt[i % 2]
                    c.dma('sp', xb[:], XS0[i][:])
                    make_xT(xb[:])
                    if samp:
                        load_state_T(lambda ch: chA[:, ch, :, 0:2], 2, Dm['st_conv_a'][l], NS * 2)
                    elif i == 0:
                        c.memset('pool', chA[:, :, :, 0:2], 0.0)
                    for ch in range(2):
                        z1 = znext(); z2 = znext()
                        pb = c.bank()
                        proj_chunk(wA, O_AC + ch * 128, 128, pb)
                        c.copy('act', z1[:], pb[:, 0:128])
                        pb2 = c.bank()
                        proj_chunk(wA, O_AH + ch * 128, 128, pb2)
                        c.tt('dve', chA[:, ch, :, 2:2 + TT], v3(z1[:], nseq), v3(pb2[:, 0:128], nseq), ALU.mult)
                        conv_taps(v3(z2[:], nseq), chA, ch, cwa, 3, TT)
                        pb3 = c.bank()
                        proj_chunk(wA, O_AB + ch * 128, 128, pb3)
                        c.tt('dve', yT[:, ch, :], z2[:], pb3[:, 0:128], ALU.mult)
                    if samp:
                        store_state_T(lambda ch: chA[:, ch, :, TT:TT + 2], 2, Dm['ca_s'][l], NS * 2)
                    elif i == NTILES - 1:
                        store_state_T(lambda ch: chA[:, ch, :, TT:TT + 2], 2, Dm['ca_p'][l], 2)
                    if not samp:
                        c.copy('pool', chA[:, :, :, 0:2], chA[:, :, :, TT:TT + 2])
                    pb = c.bank()
                    for k in range(8):
                        c.mm(pb[:, :], xT[:, k, :], wA[:, k, O_NKV:O_NKV + 512], start=(k == 0), stop=(k == 7))
                    c.copy('act', kvtm[:, 0:512], pb[:, :])
                    pb = c.bank()
                    for k in range(8):
                        c.mm(pb[:, 0:256], xT[:, k, :], wA[:, k, O_NWIN:O_NWIN + 256], start=(k == 0), stop=(k == 7))
                    c.copy('act', kvtm[:, 512:768], pb[:, 0:256])
                    if samp:
                        c.dma('sp', Dm['kv_s'][l], kvtm[:, 0:512])
                        for sq in range(NS):
                            c.dma('sp', Dm['win_s'][l, sq, WS - TS:WS, :], kvtm[sq * TS:(sq + 1) * TS, 512:768])
                    else:
                        c.dma('sp', Dm['kv_p'][l, i * 128:(i + 1) * 128, :], kvtm[:, 0:512])
                        if i * 128 >= T - WP:
                            r0 = i * 128 - (T - WP)
                            c.dma('sp', Dm['win_p'][l, r0:r0 + 128, :], kvtm[:, 512:768])
                    chG = CH['G']
                    if samp:
                        load_state_T(lambda ch: chG[:, ch, :, 0:3], 6, Dm['st_gdn_conv'][l], NS * 3)
                    elif i == 0:
                        c.memset('pool', chG[:, :, :, 0:3], 0.0)
                    for ch in range(6):
                        z2 = znext()
                        pb = c.bank()
                        proj_chunk(wA, O_GQ + ch * 128, 128, pb)
                        c.copy('act', chG[:, ch, :, 3:3 + TT], v3(pb[:, 0:128], nseq))
                        conv_taps(v3(z2[:], nseq), chG, ch, cwg, 4, TT)
                        c.act(qkvT[:, ch, :], z2[:], AF.Silu)
                    if samp:
                        store_state_T(lambda ch: chG[:, ch, :, TT:TT + 3], 6, Dm['gc_s'][l], NS * 3)
                    elif i == NTILES - 1:
                        store_state_T(lambda ch: chG[:, ch, :, TT:TT + 3], 6, Dm['gc_p'][l], 3)
                    if not samp:
                        c.copy('pool', chG[:, :, :, 0:3], chG[:, :, :, TT:TT + 3])
                    if cfg.mix_b:
                        gdn_tile(l, i, samp, G, wA, qkvT, yT)
                    else:
                        c.memset('dve', yT[:, 2:4, :], 0.0)
                    if cfg.mix_c == 1 and samp:
                        nsa_tile_s(l, N, G, wA, kvtm, yT)
                    elif cfg.mix_c and not samp and not os.environ.get('NSA_SKIP_TILE'):
                        nsa_tile_p(l, i, N, G, wA, kvtm, yT)
                    else:
                        c.memset('dve', yT[:, 4:8, :], 0.0)
                    if samp and os.environ.get('DBG'):
                        c.dma('sp', Dm['dbg3'][:, :], yT[:].re('p k t -> p (k t)'))
                    for h in range(2):
                        pb = c.bank()
                        for k in range(8):
                            c.mm(pb[:, :], yT[:, k, :], wO[:, k, h * 512:(h + 1) * 512], start=(k == 0), stop=(k == 7))
                        c.stt(xr[:, h * 512:(h + 1) * 512], xb[:, h * 512:(h + 1) * 512], ALPHA, pb[:, :], ALU.mult, ALU.add)
                    if samp and os.environ.get('DBG'):
                        pass
                    layer_norm(xr[:], xr[:], lnG)
                    c.dma('sp', XS1[i][:], xr[:])

                with ExitStack() as sub:
                    CH['A'] = c.sb([128, 2, 1, 2 + 128], F32, 'chA_p', sub); CH['G'] = c.sb([128, 6, 1, 3 + 128], F32, 'chG_p', sub)
                    if cfg.mix_b:
                        G['S'] = c.sb([64, 1, 4, 64], F32, 'S_p', sub)
                        G['hsets'] = [G['hset0']]
                        for k_ in range(1, NHSETS):
                            G['hsets'].append({nm: c.sb([64, 3, 128] if nm == 'kqT' else [128, 128], F32, nm + 'x', sub) for nm in ['kqT', 't1', 't2', 'A', 'AT', 'AqkT', 'X', 'XT', 'D0', 'D1', 'DT0', 'DT1', 'Ys', 'Y2s']})
                    if cfg.mix_c:
                        N['kslcT'] = c.sb([128, T], BF16, 'kslcT', sub)
                        N['vslc1'] = c.sb([128, NTILES, 2, 65], BF16, 'vslc1', sub)
                        N['eall'] = c.sb([64, T], BF16, 'eall', sub)
                        N['kwinT'] = c.sb([128, 1024], BF16, 'kwinT', sub)
                        N['vwin1'] = c.sb([128, 8, 2, 65], BF16, 'vwin1', sub)
                        c.memset('pool', N['vwin1'][:], 1.0)
                        c.dma('pool', N['eall'][:], Dm['eall'][:])
                        c.memset('pool', N['vslc1'][:], 1.0)
                    for i in range(NTILES):
                        run_tile(i)
                    c.barrier()
                with ExitStack() as sub:
                    CH['A'] = c.sb([128, 2, NS, 2 + TS], F32, 'chA_s', sub); CH['G'] = c.sb([128, 6, NS, 3 + TS], F32, 'chG_s', sub)
                    if cfg.mix_b:
                        G['hsets'] = [G['hset0']]
                        G['S'] = c.sb([64, 16, 1, 64], F32, 'S_s', sub)
                        for nm, shp, dt_ in (('colm', [64, 16, 128], BF16), ('negWTm', [64, 4, 128], F32), ('QgTm', [64, 4, 128], F32), ('Kdm', [128, 8, 64], F32)):
                            G[nm] = c.sb(shp, dt_, nm, sub)
                    if cfg.mix_c == 1:
                        NPG = cfg.PAST // 128
                        WS_ = min(512, cfg.PAST)
                        for nm, shp, dt_ in (('ptb', [128, NS * NPG], I32), ('idxA', [128, NS * NPG], I32), ('idxB', [128, NS * NPG], I32),
                                             ('eall_s', [64, cfg.PAST + 128], BF16), ('smask', [128, 16], F32), ('Ebig', [128, 2, 4, 128], BF16),
                                             ('VCs', [128, 2, 65], BF16), ('vs1', [128, NPG, 2, 65], BF16), ('vw1', [128, WS_ // 128, 2, 65], BF16),
                                             ('v1n', [8, 2, 2, 65], BF16), ('kslcTn', [128, 128], BF16), ('qTs', [128, NS, 4, TS], BF16), ('Ecur', [128, 32], BF16), ('kwinTn', [128, 128], BF16),
                                             ('pk', [128, min(NPG, 8), 256], BF16), ('cmpTs', [128, 2, cfg.PAST], BF16), ('pw', [128, WS_ // 128, 256], BF16),
                                             ('kwTs', [128, WS_], BF16), ('hxs', [64, 2, 128], F32), ('h2s', [64, 2, 128], F32), ('hbs', [64, 2, 128], BF16),
                                             ('kcTs', [128, 128], BF16), ('selTs', [64, 2, 128], BF16), ('osb', [65, 512], F32),
                                             ('Es', [128, 256], BF16), ('En', [8, 4, 8], BF16), ('dbgst', [128, 640], F32)):
                            N[nm] = c.sb(shp, dt_, 'ns_' + nm, sub)
                    run_tile(NTILES)
                    c.barrier()

            with ExitStack() as ph:
                wU = c.sb([128, 8, 2 * DFF], BF16, 'wU', ph)
                wD = c.sb([128, 22, D], BF16, 'wD', ph)
                cwf = c.sb([128, 22, 3], F32, 'cwf', ph)
                chFb = c.sb([128, 22, NS * (2 + TS)], F32, 'chF', ph)
                actT = c.sb([128, 22, 128], BF16, 'actT', ph)
                chF_p = chFb[:, :, 0:130].re('p c (s t) -> p c s t', s=1)
                chF_s = chFb[:, :, :].re('p c (s t) -> p c s t', s=NS)
                for k in range(8):
                    c.dma('pool', wU[:, k, :], Dm['w_up'][l, k * 128:(k + 1) * 128, :])
                for k in range(22):
                    c.dma('pool', wD[:, k, :], Dm['w_down'][l, k * 128:(k + 1) * 128, :])
                c.dma('sp', lnG[:], V(Dm['ln2'], I['ln2'][l].partition_broadcast(128)))
                for tap in range(3):
                    c.dma('sp', cwf[:, :, tap], Dm['ffn_conv_w'][l, tap].re('(c p) -> p c', p=128), slow=True)
                for i in tiles:
                    samp = (i == NTILES)
                    nseq, TT = (NS, TS) if samp else (1, 128)
                    chF = chF_s if samp else chF_p
                    xb = xt[i % 2]
                    c.dma('sp', xb[:], XS1[i][:])
                    make_xT(xb[:])
                    if samp:
                        load_state_T(lambda ch: chF[:, ch, :, 0:2], 22, Dm['st_ffn_conv'][l], NS * 2)
                    elif i == 0:
                        c.memset('pool', chF[:, :, :, 0:2], 0.0)
                    for ch in range(22):
                        z1 = znext(); z2 = znext()
                        pb = c.bank()
                        proj_chunk(wU, ch * 128, 128, pb)
                        c.copy('act', chF[:, ch, :, 2:2 + TT], v3(pb[:, 0:128], nseq))
                        conv_taps(v3(z2[:], nseq), chF, ch, cwf, 3, TT)
                        c.act(z1[:], z2[:], AF.Silu)
                        pb2 = c.bank()
                        proj_chunk(wU, DFF + ch * 128, 128, pb2)
                        c.tt('dve', actT[:, ch, :], z1[:], pb2[:, 0:128], ALU.mult)
                    if samp:
                        store_state_T(lambda ch: chF[:, ch, :, TT:TT + 2], 22, Dm['fc_s'][l], NS * 2)
                    elif i == NTILES - 1:
                        store_state_T(lambda ch: chF[:, ch, :, TT:TT + 2], 22, Dm['fc_p'][l], 2)
                    if not samp:
                        c.copy('pool', chF[:, :, :, 0:2], chF[:, :, :, TT:TT + 2])
                    for h in range(2):
                        pb = c.bank()
                        for k in range(22):
                            c.mm(pb[:, :], actT[:, k, :], wD[:, k, h * 512:(h + 1) * 512], start=(k == 0), stop=(k == 21))
                        c.stt(xr[:, h * 512:(h + 1) * 512], xb[:, h * 512:(h + 1) * 512], ALPHA, pb[:, :], ALU.mult, ALU.add)
                    layer_norm(xr[:], xr[:], lnG)
                    if l == L - 1:
                        c.dma('sp', (Dm['y_p'][i * 128:(i + 1) * 128, :] if not samp else Dm['y_s'][:, :]), xr[:])
                    else:
                        c.dma('sp', XS0[i][:], xr[:])
                c.barrier()
        c.finish()
        print("instructions:", c.ninstr)
    return nc


OUT_NAMES = ['y_p', 'y_s', 'kv_p', 'kv_s', 'win_p', 'win_s', 'ca_p', 'ca_s', 'gc_p', 'gc_s', 'gs_p', 'gs_s', 'fc_p', 'fc_s']


def make_in_maps(cfg, inp, ncores=8):
    L, NS = cfg.L, cfg.NS
    nb = inp['x_prompt'].shape[0]
    f = np.ascontiguousarray
    shared = {
        'ln_emb': f(np.stack([inp['ln_emb_g'], inp['ln_emb_b']])),
        'w_in': f(inp['w_in']), 'w_out': f(inp['w_out']), 'w_up': f(inp['w_up']), 'w_down': f(inp['w_down']),
        'conv_a_w': f(inp['conv_a_w']), 'gdn_conv_w': f(inp['gdn_conv_w']), 'ffn_conv_w': f(inp['ffn_conv_w']),
        'gdn_a_log': f(inp['gdn_a_log']), 'gdn_dt_bias': f(inp['gdn_dt_bias']), 'gdn_norm_g': f(inp['gdn_norm_g']),
        'cmp_pe': f(inp['cmp_pe']), 'cmp_w1': f(inp['cmp_w1']), 'cmp_b1': f(inp['cmp_b1']), 'cmp_w2': f(inp['cmp_w2']),
        'ln1': f(np.stack([inp['ln1_g'], inp['ln1_b']], axis=1)), 'ln2': f(np.stack([inp['ln2_g'], inp['ln2_b']], axis=1)),
    }
    for v_, (ns_, ts_) in (('p', (1, 128)), ('s', (cfg.NS, cfg.TS))):
        cg, rowm, colm = gdn_consts(ns_, ts_)
        shared['cg_' + v_], shared['rowm_' + v_], shared['colm_' + v_] = cg, rowm, colm
    shared['ov'], shared['cq'], shared['eall'], shared['impm'], shared['impb'] = nsa_consts(cfg)
    P_ = cfg.PAST
    n_ = np.arange(64)
    shared['eall_s'] = (np.arange(P_ + 128)[None, :] // 64 == n_[:, None]).astype(np.float32)
    qpos = P_ + (np.arange(128) % cfg.TS)
    valid = (n_[None, :] * 64 <= qpos[:, None]); forced = (n_[None, :] == 0) | (n_[None, :] == qpos[:, None] // 64)
    shared['impm_s'] = (valid & ~forced).astype(np.float32)
    shared['impb_s'] = np.where(forced, 1e9, np.where(valid, 0.0, -1e9)).astype(np.float32)
    sm = np.zeros((128, 16), np.float32)
    sm[:, 0:8] = (np.arange(128)[:, None] > np.arange(8)[None, :])
    sm[0:8, 8:16] = (np.arange(8)[:, None] <= np.arange(8)[None, :])
    shared['smask'] = sm
    shared['cache_kv'] = f(inp['cache_nsa_kv']).reshape(-1, 256)
    maps = []
    for i in range(ncores):
        sl = slice(NS * i, NS * (i + 1))
        m = dict(shared)
        m['x_p'] = f(inp['x_prompt'][i % nb])
        m['x_s'] = f(inp['x_sample'][sl].reshape(128, D))
        m['st_conv_a'] = f(inp['state_conv_a'][:, sl].reshape(L, NS * 2, 256))
        m['page_table'] = f(inp['page_table'][sl].astype(np.int32).reshape(1, -1))
        m['cache_win'] = f(inp['cache_nsa_win'][:, sl].reshape(L, NS, -1, 256))
        m['st_gdn_conv'] = f(inp['state_gdn_conv'][:, sl].reshape(L, NS * 3, 768))
        m['st_gdn'] = f(inp['state_gdn'][:, sl])
        m['st_ffn_conv'] = f(inp['state_ffn_conv'][:, sl].reshape(L, NS * 2, DFF))
        maps.append(m)
    return maps


def gather(cfg, res, ncores=8, nb=4):
    L, NS, T = cfg.L, cfg.NS, cfg.T
    WP = min(512, T)
    WS = min(512, cfg.PAST)
    r = res

    def P(name, shp):
        return np.stack([r[i][name].reshape((L,) + shp) for i in range(nb)], axis=1)

    def S(name, shp):
        return np.concatenate([r[i][name].reshape((L, NS) + shp) for i in range(ncores)], axis=1)

    y_p = np.stack([r[i]['y_p'] for i in range(nb)], axis=0)
    y_s = np.concatenate([r[i]['y_s'].reshape(NS, cfg.TS, D) for i in range(ncores)], axis=0)
    return (y_p, y_s,
            P('kv_p', (T, 4, 2, 64)), S('kv_s', (cfg.TS, 4, 2, 64)),
            P('win_p', (WP, 2, 2, 64)), S('win_s', (WS, 2, 2, 64)),
            P('ca_p', (2, 256)), S('ca_s', (2, 256)),
            P('gc_p', (3, 768)), S('gc_s', (3, 768)),
            P('gs_p', (4, 64, 64)), S('gs_s', (4, 64, 64)),
            P('fc_p', (2, DFF)), S('fc_s', (2, DFF)))


def kernel(**inputs):
    inp = {k: np.asarray(v) for k, v in inputs.items()}
    cfg = Cfg()
    nc = build(cfg)
    maps = make_in_maps(cfg, inp, 8)
    res = run_bass_kernel_spmd(nc, maps, core_ids=list(range(8)))
    outs = gather(cfg, res.results, 8, 4)
    return tuple(np.ascontiguousarray(o.astype(np.float32)) for o in outs)
```

```python
import numpy as np
import os
from contextlib import ExitStack
import concourse.bass as bass
import concourse.mybir as mybir
from concourse.bass_utils import run_bass_kernel_spmd

F32 = mybir.dt.float32
BF16 = mybir.dt.bfloat16
I32 = mybir.dt.int32
AF = mybir.ActivationFunctionType
ALU = mybir.AluOpType
AX = mybir.AxisListType

D = 1024
DFF = 2816
NIN = 3104
ALPHA = (2.0 * 4) ** 0.25
NHSETS = int(os.environ.get('NHSETS', '4'))
TRANSITIVE = os.environ.get('TRANSITIVE', '1') == '1'
EPS = 1e-5
O_AB, O_AC, O_AH, O_GQ, O_GK, O_GV, O_GG, O_GA, O_GB, O_NQ, O_NKV, O_NWIN, O_NG = (
    0, 256, 512, 768, 1024, 1280, 1536, 1792, 1796, 1800, 2312, 2824, 3080)


class Cfg:
    def __init__(self, T=4096, L=4, NS=16, TS=8, PAST=2048, NPOOL=2560, mix_b=True, mix_c=True):
        self.T, self.L, self.NS, self.TS, self.PAST, self.NPOOL = T, L, NS, TS, PAST, NPOOL
        self.NT = T // 128
        self.mix_b, self.mix_c = mix_b, mix_c


class Buf:
    def __init__(self, t, name):
        self.t = t
        self.name = name
        self.w = None
        self.r = {}

    def __getitem__(self, key):
        return V(self, self.t[key])


class V:
    def __init__(self, buf, ap):
        self.buf = buf
        self.ap = ap

    def __getitem__(self, key):
        return V(self.buf, self.ap[key])

    def re(self, pat, **kw):
        return V(self.buf, self.ap.rearrange(pat, **kw))

    def bc(self, shape):
        return V(self.buf, self.ap.to_broadcast(list(shape)))


def _aps(x):
    return x.ap if isinstance(x, V) else x


def _chain(gens):
    for g_ in gens:
        yield from g_


class _HeadView:
    def __init__(self, v, h):
        self.v, self.h = v, h

    def __getitem__(self, key):
        k = list(key)
        if isinstance(k[2], int):
            k[2] = 0
        return self.v[tuple(k)]


class Ctx:
    def __init__(self, nc, es, n_dma_sems=30):
        self.nc, self.es = nc, es
        self.eng = {'pe': nc.tensor, 'act': nc.scalar, 'dve': nc.vector, 'pool': nc.gpsimd, 'sp': nc.sync}
        self.sem = {k: es.enter_context(nc.semaphore('s_' + k)) for k in self.eng}
        self.cnt = {k: 0 for k in self.eng}
        self.seen = {k: {} for k in self.eng}
        self.snap = {k: {} for k in self.eng}
        self.dsnap = {}
        self.dsem = [es.enter_context(nc.semaphore('d%d' % i)) for i in range(n_dma_sems)]
        self.dcnt = [0] * n_dma_sems
        self.dnext = {'sp': 0, 'pool': 0, 'act': 0}
        self.dpool = {'sp': list(range(0, 14)), 'pool': list(range(14, n_dma_sems)), 'act': []}
        self.nbuf = 0
        self.banks = []
        self.bnext = 0
        self.ninstr = 0

    def sb(self, shape, dt=F32, name=None, es=None):
        self.nbuf += 1
        name = (name or 'b') + str(self.nbuf)
        esz = 2 if dt == BF16 else 4
        n = 1
        for d_ in shape[1:]:
            n *= d_
        npad = -(-(n * esz) // 64) * 64 // esz
        t = (es or self.es).enter_context(self.nc.sbuf_tensor(name, [shape[0], npad], dt))
        ap = t[:, 0:n]
        if len(shape) > 2:
            names = ['d%d' % i for i in range(len(shape) - 1)]
            pat = 'p (' + ' '.join(names) + ') -> p ' + ' '.join(names)
            ap = ap.rearrange(pat, **{nm: sz for nm, sz in zip(names[:-1], shape[1:-1])})
        return Buf(ap, name)

    def barrier(self):
        for e in self.eng:
            for k in self.eng:
                if k != e and self.cnt[k] > 0:
                    self._wait(e, k, self.cnt[k])
            for j in range(len(self.dsem)):
                if self.dcnt[j] > 0:
                    self._wait(e, j, self.dcnt[j])

    def mkbanks(self):
        for i in range(8):
            t = self.es.enter_context(self.nc.psum_tensor('bank%d' % i, [128, 512], F32))
            self.banks.append(Buf(t, 'bank%d' % i))

    def bank(self):
        if not hasattr(self, 'rot'):
            self.rot = list(self.banks)
        b = self.rot.pop(0)
        self.rot.append(b)
        return b

    def hold(self):
        if not hasattr(self, 'rot'):
            self.rot = list(self.banks)
        return self.rot.pop(0)

    def release(self, b):
        self.rot.append(b)

    def dram(self, ap, name):
        class _T:
            pass
        b = Buf(None, name)
        b.t = ap
        return b

    def _semof(self, key):
        return self.sem[key] if isinstance(key, str) else self.dsem[key]

    def _wait(self, e, key, val):
        if key == e and e == 'pe':
            return
        if self.seen[e].get(key, 0) >= val:
            return
        self.eng[e].wait_ge(self._semof(key), val)
        self.seen[e][key] = val
        self.ninstr += 1
        if TRANSITIVE:
            sn = self.snap[key].get(val) if isinstance(key, str) else self.dsnap.get((key, val))
            if sn:
                se = self.seen[e]
                for k2, v2 in sn.items():
                    if k2 != e and se.get(k2, 0) < v2:
                        se[k2] = v2

    def _deps(self, e, reads, writes):
        for b in reads:
            if b.w is not None:
                self._wait(e, *b.w)
        for b in writes:
            if b.w is not None:
                self._wait(e, *b.w)
            for k, v in b.r.items():
                self._wait(e, k, v)

    def _mark(self, ev, reads, writes):
        for b in reads:
            b.r[ev[0]] = ev[1]
        for b in writes:
            b.w = ev
            b.r = {}

    def op(self, e, fn, reads=(), writes=(), signal=True):
        reads = [x.buf if isinstance(x, V) else x for x in reads]
        writes = [x.buf if isinstance(x, V) else x for x in writes]
        self._deps(e, reads, writes)
        ins = fn(self.eng[e])
        self.ninstr += 1
        if signal or e != 'pe':
            self.cnt[e] += 1
            ins.then_inc(self.sem[e], 1)
            ev = (e, self.cnt[e])
            self.snap[e][self.cnt[e]] = dict(self.seen[e])
        else:
            ev = (e, self.cnt[e] + 1)
        self._mark(ev, reads, writes)
        return ins

    def dma(self, e, out, in_, slow=False):
        pl = self.dpool[e]
        j = pl[self.dnext[e]]
        self.dnext[e] = (self.dnext[e] + 1) % len(pl)
        if self.dcnt[j] > 0:
            self._wait(e, j, self.dcnt[j])
        self._deps(e, [in_.buf], [out.buf])
        self.dcnt[j] += 16
        if slow:
            ins = self.eng[e].dma_start(out=out.ap, in_=in_.ap, allow_slow_non_contiguous=True)
        else:
            ins = self.eng[e].dma_start(out=out.ap, in_=in_.ap)
        ins.then_inc(self.dsem[j], 16)
        self.ninstr += 1
        self.dsnap[(j, self.dcnt[j])] = dict(self.seen[e])
        self._mark((j, self.dcnt[j]), [in_.buf], [out.buf])
        return ins

    def dma_custom(self, e, fn, reads, writes):
        pl = self.dpool[e]
        j = pl[self.dnext[e]]
        self.dnext[e] = (self.dnext[e] + 1) % len(pl)
        if self.dcnt[j] > 0:
            self._wait(e, j, self.dcnt[j])
        self._deps(e, reads, writes)
        self.dcnt[j] += 16
        ins = fn(self.eng[e])
        ins.then_inc(self.dsem[j], 16)
        self.ninstr += 1
        self.dsnap[(j, self.dcnt[j])] = dict(self.seen[e])
        self._mark((j, self.dcnt[j]), reads, writes)
        return ins

    def mm(self, out, lhsT, rhs, start=True, stop=True, signal=None):
        if signal is None:
            signal = stop
        return self.op('pe', lambda e: e.matmul(out.ap, lhsT=lhsT.ap, rhs=rhs.ap, start=start, stop=stop),
                       [lhsT, rhs], [out], signal=signal)

    def tr(self, out, in_, ident, signal=True):
        return self.op('pe', lambda e: e.transpose(out.ap, in_.ap, ident.ap), [in_, ident], [out], signal=signal)

    def act(self, out, in_, func, bias=None, scale=None, accum=None, e='act'):
        kw = {}
        rd = [in_]
        if bias is not None:
            kw['bias'] = _aps(bias)
            if isinstance(bias, V):
                rd.append(bias)
        if scale is not None:
            kw['scale'] = _aps(scale)
            if isinstance(scale, V):
                rd.append(scale)
        wr = [out]
        if accum is not None:
            kw['accum_out'] = accum.ap
            wr.append(accum)
        return self.op('act', lambda e_: e_.activation(out.ap, in_.ap, func, **kw), rd, wr)

    def copy(self, e, out, in_):
        if e == 'act':
            return self.op('act', lambda e_: e_.copy(out.ap, in_.ap), [in_], [out])
        return self.op(e, lambda e_: e_.tensor_copy(out.ap, in_.ap), [in_], [out])

    def tt(self, e, out, in0, in1, op):
        return self.op(e, lambda e_: e_.tensor_tensor(out.ap, in0.ap, in1.ap, op), [in0, in1], [out])

    def ts(self, e, out, in0, s1, op0, s2=None, op1=None, accum=None):
        rd = [in0] + [s for s in (s1, s2) if isinstance(s, V)]
        wr = [out] + ([accum] if accum is not None else [])
        kw = {}
        if accum is not None:
            kw['accum_out'] = accum.ap
        if op1 is None:
            return self.op(e, lambda e_: e_.tensor_scalar(out.ap, in0.ap, _aps(s1), None, op0, **kw), rd, wr)
        return self.op(e, lambda e_: e_.tensor_scalar(out.ap, in0.ap, _aps(s1), _aps(s2), op0, op1, **kw), rd, wr)

    def stt(self, out, in0, s, in1, op0, op1):
        rd = [in0, in1] + ([s] if isinstance(s, V) else [])
        return self.op('dve', lambda e_: e_.scalar_tensor_tensor(out.ap, in0.ap, _aps(s), in1.ap, op0, op1), rd, [out])

    def memset(self, e, out, val):
        return self.op(e, lambda e_: e_.memset(out.ap, val), [], [out])

    def finish(self):
        for j in range(len(self.dsem)):
            if self.dcnt[j] > 0:
                self._wait('sp', j, self.dcnt[j])


def gdn_consts(nseq, TS):
    i = np.arange(128)
    seq = i // TS if nseq > 1 else np.zeros(128, np.int64)
    same = seq[:, None] == seq[None, :]
    lowi = same & (i[:, None] >= i[None, :])
    lows = same & (i[:, None] > i[None, :])
    cg = np.zeros((128, 19, 128), np.float32)
    cg[:, 0], cg[:, 1], cg[:, 2], cg[:, 3], cg[:, 4] = lowi, lows, lowi.T, lows.T, same
    for k in range(7):
        b = 2 << k
        lm = same & (i[:, None] // b == i[None, :] // b) & ((i[:, None] % b) >= b // 2) & ((i[None, :] % b) < b // 2)
        cg[:, 5 + k] = lm
        cg[:, 12 + k] = lm.T
    rowm = np.zeros((128, 16), np.float32)
    rowm[i, seq] = 1.0
    colm = np.ascontiguousarray(np.broadcast_to(rowm.T[None], (64, 16, 128))).astype(np.float32)
    return cg, rowm, colm


def nsa_consts(cfg):
    T, NT = cfg.T, cfg.NT
    c_ = np.arange(128)
    n = np.arange(64)
    ov = np.zeros((128, 2, 65), np.float32)
    for ct in range(2):
        cc = (ct * 128 + c_)[:, None] * 16
        ov[:, ct, :64] = (cc < n[None, :] * 64 + 64) & (cc + 32 > n[None, :] * 64)
        ov[:, ct, 64] = 1.0
    cq = (16.0 * c_[:, None] - c_[None, :]).astype(np.float32)
    eall = (np.arange(T)[None, :] // 64 == n[:, None]).astype(np.float32)
    qpos = np.arange(T)
    valid = (n[None, :] * 64 <= qpos[:, None])
    forced = (n[None, :] == 0) | (n[None, :] == qpos[:, None] // 64)
    mult = (valid & ~forced).astype(np.float32).reshape(NT, 128, 64)
    bias = np.where(forced, 1e9, np.where(valid, 0.0, -1e9)).astype(np.float32).reshape(NT, 128, 64)
    return ov, cq, eall, mult, bias


def build(cfg):
    T, L, NS, TS, NTILES = cfg.T, cfg.L, cfg.NS, cfg.TS, cfg.NT
    nc = bass.Bass("TRN2", target_bir_lowering=False)
    es = ExitStack()

    def din(name, shape, dt=F32):
        return nc.dram_tensor(name, list(shape), dt, kind="ExternalInput").ap()

    def dout(name, shape, dt=F32):
        return nc.dram_tensor(name, list(shape), dt, kind="ExternalOutput").ap()

    def dscr(name, shape, dt=F32):
        return nc.dram_tensor(name, list(shape), dt, kind="Internal").ap()

    I = {}
    I['x_p'] = din('x_p', [T, D]); I['x_s'] = din('x_s', [128, D])
    I['st_conv_a'] = din('st_conv_a', [L, NS * 2, 256])
    I['st_gdn_conv'] = din('st_gdn_conv', [L, NS * 3, 768])
    I['st_gdn'] = din('st_gdn', [L, NS, 4, 64, 64])
    I['st_ffn_conv'] = din('st_ffn_conv', [L, NS * 2, DFF])
    I['cache_win'] = din('cache_win', [L, NS, min(512, cfg.PAST), 256])
    for v_ in ('p', 's'):
        I['cg_' + v_] = din('cg_' + v_, [128, 19, 128]); I['rowm_' + v_] = din('rowm_' + v_, [128, 16]); I['colm_' + v_] = din('colm_' + v_, [64, 16, 128])
    I['gdn_a_log'] = din('gdn_a_log', [L, 4]); I['gdn_dt_bias'] = din('gdn_dt_bias', [L, 4]); I['gdn_norm_g'] = din('gdn_norm_g', [L, 64])
    NPG = cfg.PAST // 128
    I['cache_kv'] = din('cache_kv', [L * cfg.NPOOL * 128 * 2, 256])
    I['page_table'] = din('page_table', [1, NS * NPG], I32)
    I['eall_s'] = din('eall_s', [64, cfg.PAST + 128]); I['impm_s'] = din('impm_s', [128, 64]); I['impb_s'] = din('impb_s', [128, 64])
    I['smask'] = din('smask', [128, 16])
    I['ov'] = din('ov', [128, 2, 65]); I['cq'] = din('cq', [128, 128]); I['eall'] = din('eall', [64, T])
    I['impm'] = din('impm', [NTILES, 128, 64]); I['impb'] = din('impb', [NTILES, 128, 64])
    I['cmp_pe'] = din('cmp_pe', [L, 2, 32, 64]); I['cmp_w1'] = din('cmp_w1', [L, 2, 32, 64, 64]); I['cmp_b1'] = din('cmp_b1', [L, 2, 64])
    I['cmp_w2'] = din('cmp_w2', [L, 2, 64, 64])
    I['ln_emb'] = din('ln_emb', [2, D])
    I['w_in'] = din('w_in', [L, D, NIN]); I['w_out'] = din('w_out', [L, D, D])
    I['w_up'] = din('w_up', [L, D, 2 * DFF]); I['w_down'] = din('w_down', [L, DFF, D])
    I['conv_a_w'] = din('conv_a_w', [L, 3, 256]); I['gdn_conv_w'] = din('gdn_conv_w', [L, 4, 768])
    I['ffn_conv_w'] = din('ffn_conv_w', [L, 3, DFF])
    I['ln1'] = din('ln1', [L, 2, D]); I['ln2'] = din('ln2', [L, 2, D])
    O = {}
    if os.environ.get('DBG'):
        O['dbg2'] = dout('dbg2', [128, 2048]); O['dbg3'] = dout('dbg3', [128, 1024], BF16)
    O['y_p'] = dout('y_p', [T, D]); O['y_s'] = dout('y_s', [128, D])
    O['ca_p'] = dout('ca_p', [L, 2, 256]); O['ca_s'] = dout('ca_s', [L, NS * 2, 256])
    O['gc_p'] = dout('gc_p', [L, 3, 768]); O['gc_s'] = dout('gc_s', [L, NS * 3, 768])
    O['fc_p'] = dout('fc_p', [L, 2, DFF]); O['fc_s'] = dout('fc_s', [L, NS * 2, DFF])
    O['kv_p'] = dout('kv_p', [L, T, 512]); O['kv_s'] = dout('kv_s', [L, 128, 512])
    O["win_p"] = dout("win_p", [L, min(512, T), 256]); O["win_s"] = dout("win_s", [L, NS, min(512, cfg.PAST), 256])
    O['gs_p'] = dout('gs_p', [L, 4, 64, 64]); O['gs_s'] = dout('gs_s', [L, NS, 4, 64, 64])
    xs0 = dscr('xs0', [NTILES + 1, 128, D]); xs1 = (dout if os.environ.get('DBG') else dscr)('xs1', [NTILES + 1, 128, D])

    with es:
        c = Ctx(nc, es)
        c.mkbanks()
        Dm = {k: c.dram(v, k) for k, v in list(I.items()) + list(O.items())}
        XS0 = [c.dram(xs0[i], 'xs0_%d' % i) for i in range(NTILES + 1)]
        XS1 = [c.dram(xs1[i], 'xs1_%d' % i) for i in range(NTILES + 1)]

        identf = c.sb([128, 128], F32, 'identf')
        c.memset('pool', identf[:], 0.0)
        c.op('pool', lambda e: e.affine_select(out=identf.t[:], in_=identf.t[:], pattern=[[-1, 128]],
                                                compare_op=ALU.not_equal, fill=1.0, base=0, channel_multiplier=1),
             [identf], [identf])

        identb = c.sb([128, 128], BF16, 'identb')
        c.copy('dve', identb[:], identf[:])
        piota = c.sb([128, 1], I32, 'piota')
        c.op('pool', lambda e: e.iota(piota.t[:], pattern=[[0, 1]], base=0, channel_multiplier=1), [], [piota])
        ones128 = c.sb([128, 128], F32, 'ones128')
        c.memset('pool', ones128[:], 1.0)
        lnG = c.sb([128, 2, D], F32, 'lnG')
        xt = [c.sb([128, D], F32, 'xt')] * 2
        xr = c.sb([128, D], F32, 'xr')
        xT = c.sb([128, 8, 128], BF16, 'xT')
        st6 = c.sb([128, 2, 6], F32, 'st6'); mv = c.sb([128, 2], F32, 'mv'); rstd = c.sb([128, 1], F32, 'rstd')
        rows = c.sb([48, 512], F32, 'rows')
        zring = [c.sb([128, 128], F32, 'z') for _ in range(6)]
        zstate = [0]

        def znext():
            zstate[0] = (zstate[0] + 1) % len(zring)
            return zring[zstate[0]]
        z1 = zring[0]

        def layer_norm(src, dst, gb):
            for k in range(2):
                c.op('dve', lambda e: e.bn_stats(st6.t[:, k, :], src.ap[:, k * 512:(k + 1) * 512]), [src], [st6])
            c.op('dve', lambda e: e.bn_aggr(mv.t[:], st6.t[:]), [st6], [mv])
            c.ts('dve', rstd[:], mv[:, 1:2], EPS, ALU.add)
            c.act(rstd[:], rstd[:], AF.Sqrt)
            c.op('dve', lambda e: e.reciprocal(rstd.t[:], rstd.t[:]), [rstd], [rstd])
            c.ts('dve', dst, src, mv[:, 0:1], ALU.subtract, rstd[:, 0:1], ALU.mult)
            c.tt('dve', dst[:, 0:512], dst[:, 0:512], gb[:, 0, 0:512], ALU.mult)
            c.tt('pool', dst[:, 512:1024], dst[:, 512:1024], gb[:, 0, 512:1024], ALU.mult)
            c.tt('dve', dst[:, 0:512], dst[:, 0:512], gb[:, 1, 0:512], ALU.add)
            c.tt('pool', dst[:, 512:1024], dst[:, 512:1024], gb[:, 1, 512:1024], ALU.add)

        def make_xT(src):
            for h in range(2):
                pb = c.bank()
                for k in range(4):
                    c.tr(pb[:, k * 128:(k + 1) * 128], src[:, (h * 4 + k) * 128:(h * 4 + k + 1) * 128], identf[:], signal=(k == 3))
                c.copy('act', xT[:, h * 4:(h + 1) * 4, :], pb[:].re('p (k t) -> p k t', k=4))

        def proj_chunk(w, col, m, pb, n0=0):
            for k in range(8):
                c.mm(pb[0:m, n0:n0 + 128], w[:, k, col:col + m], xT[:, k, :], start=(k == 0), stop=(k == 7))

        def load_state_T(dst, nch, dram_rows, R):
            for c0 in range(0, nch, 4):
                n = min(4, nch - c0)
                c.dma('sp', rows[0:R, 0:n * 128], dram_rows[:, c0 * 128:(c0 + n) * 128])
                for ch in range(n):
                    pb = c.bank()
                    c.tr(pb[:, 0:R], rows[0:R, ch * 128:(ch + 1) * 128], identf[0:R, 0:R])
                    d = dst(c0 + ch)
                    c.copy('dve', d, pb[:, 0:R].re('p (s j) -> p s j', s=d.ap.shape[1]))

        def store_state_T(src, nch, dram_rows, R):
            for c0 in range(0, nch, 4):
                n = min(4, nch - c0)
                for ch in range(n):
                    pb = c.bank()
                    sv = src(c0 + ch)
                    zz = znext()
                    c.copy('dve', zz[:, 0:R].re('p (s j) -> p s j', s=sv.ap.shape[1]), sv)
                    c.tr(pb[0:R, 0:128], zz[:, 0:R], identf[:])
                    c.copy('act', rows[0:R, ch * 128:(ch + 1) * 128], pb[0:R, 0:128])
                c.dma('sp', dram_rows[:, c0 * 128:(c0 + n) * 128], rows[0:R, 0:n * 128])

        def conv_taps(dst, ch, cidx, wts, K, TT):
            c.ts('dve', dst, ch[:, cidx, :, 0:TT], wts[:, cidx, 0:1], ALU.mult)
            for i in range(1, K):
                c.stt(dst, ch[:, cidx, :, i:i + TT], wts[:, cidx, i:i + 1], dst, ALU.mult, ALU.add)

        def v3(v, nseq):
            return v.re('p (s t) -> p s t', s=nseq)


        def gdn_tile(l, i, samp, G, wA, qkvT, yT):
            nseq = NS if samp else 1
            nlev = 3 if samp else 7
            vn = 's' if samp else 'p'
            if i == 0 or samp:
                c.dma('sp', G['cg'][:], Dm['cg_' + vn][:]); c.dma('sp', G['rowm'][:], Dm['rowm_' + vn][:])
                if samp:
                    c.dma('pool', G['colm'][:], Dm['colm_' + vn][:])
            cg = G['cg']
            LOWI, LOWS, UPI, UPS, BLK = (cg[:, j, :] for j in range(5))
            S_full = G['S']
            if (not samp) and i == 0:
                c.memset('pool', S_full[:], 0.0)
            qkv = G['qkv_tm']
            for ch in range(6):
                pb = c.bank()
                c.tr(pb[:, 0:128], qkvT[:, ch, :], identf[:])
                c.copy('act', qkv[:, ch * 128:(ch + 1) * 128], pb[:, 0:128])
            pb = c.bank()
            for k in range(8):
                c.mm(pb[:, 0:8], xT[:, k, :], wA[:, k, O_GA:O_GA + 8], start=(k == 0), stop=(k == 7))
            c.copy('act', G['ab'][:], pb[:, 0:8])
            pb = c.bank()
            for k in range(8):
                c.mm(pb[:, 0:256], xT[:, k, :], wA[:, k, O_GG:O_GG + 256], start=(k == 0), stop=(k == 7))
            c.act(G['gate'][:], pb[:, 0:256], AF.Silu)
            t4, g4, beta4, gc4, egc4, egd4, bgc4 = (G[n] for n in ('t4', 'g4', 'beta4', 'gc4', 'egc4', 'egd4', 'bgc4'))
            c.tt('dve', t4[:], G['ab'][:, 0:4], G['dtb'][:], ALU.add)
            c.act(t4[:], t4[:], AF.Exp)
            c.ts('dve', t4[:], t4[:], 1.0, ALU.add)
            c.act(t4[:], t4[:], AF.Ln)
            c.tt('dve', g4[:], t4[:], G['negA'][:], ALU.mult)
            c.act(beta4[:], G['ab'][:, 4:8], AF.Sigmoid)
            pb = c.bank()
            c.mm(pb[:, 0:4], UPI, g4[:])
            c.mm(pb[:, 4:8], BLK, g4[:])
            c.copy('act', gc4[:], pb[:, 0:4])
            c.act(egc4[:], gc4[:], AF.Exp)
            c.tt('dve', egd4[:], pb[:, 4:8], gc4[:], ALU.subtract)
            c.act(egd4[:], egd4[:], AF.Exp)
            c.tt('dve', bgc4[:], beta4[:], egc4[:], ALU.mult)
            sc = G['sc']
            ss = G['ss']
            for hh in range(2):
                c.tt('dve', sc[:], qkv[:, hh * 256:(hh + 1) * 256].re('p (h d) -> p h d', h=4), qkv[:, hh * 256:(hh + 1) * 256].re('p (h d) -> p h d', h=4), ALU.mult)
                c.op('dve', lambda e: e.reduce_sum(ss.t[:, hh * 4:(hh + 1) * 4], sc.t[:], axis=AX.X), [sc], [ss])
            c.ts('dve', ss[:], ss[:], 1e-6, ALU.add)
            c.act(ss[:], ss[:], AF.Sqrt)
            c.op('dve', lambda e: e.reciprocal(ss.t[:], ss.t[:]), [ss], [ss])
            c.ts('dve', ss[:, 0:4], ss[:, 0:4], 0.125, ALU.mult)
            for hh in range(2):
                c.tt('dve', qkv[:, hh * 256:(hh + 1) * 256].re('p (h d) -> p h d', h=4), qkv[:, hh * 256:(hh + 1) * 256].re('p (h d) -> p h d', h=4),
                     ss[:, hh * 4:(hh + 1) * 4].re('p (h o) -> p h o', o=1).bc([128, 4, 64]), ALU.mult)
            def head_gen(h, HB):
                Qh, Kh, Vh = qkv[:, h * 64:(h + 1) * 64], qkv[:, 256 + h * 64:256 + (h + 1) * 64], qkv[:, 512 + h * 64:512 + (h + 1) * 64]
                hs = slice(h, h + 1)
                if samp:
                    c.dma('sp', S_full[:, :, 0, :], Dm['st_gdn'][l, :, h].re('s k v -> k s v'))
                    S = _HeadView(S_full, h)
                else:
                    S = S_full
                Vb, Kbg, Kd, Qg = HB['t1'][:, 0:64], HB['t1'][:, 64:128], HB['t2'][:, 0:64], HB['t2'][:, 64:128]
                c.ts('dve', Qg, Qh, egc4[:, hs], ALU.mult)
                pb = c.bank()
                c.tr(pb[0:64, 0:128], Kh, identf[:])
                c.tr(pb[0:64, 128:256], Qh, identf[:])
                c.tr(pb[0:64, 256:384], Qg, identf[:])
                kqT = HB['kqT']
                c.copy('act', kqT[:], pb[0:64, 0:384].re('p (a t) -> p a t', a=3))
                KT, QT, QgT = kqT[:, 0, :], kqT[:, 1, :], kqT[:, 2, :]
                dg = G['dg']
                c.ts('pool', dg[:, 0, :], identf[:], gc4[:, hs], ALU.mult)
                c.ts('pool', dg[:, 1, :], identf[:], beta4[:, hs], ALU.mult)
                pbG = c.bank()
                c.mm(pbG[:, 0:128], ones128[:], dg[:, 0, :])
                c.mm(pbG[:, 128:256], ones128[:], dg[:, 1, :])
                pbK = c.bank()
                c.mm(pbK[:, 0:128], KT, KT)
                c.mm(pbK[:, 128:256], KT, QT)
                t3 = G['t3']
                A, AT, AqkT = HB['A'], HB['AT'], HB['AqkT']
                c.ts('dve', t3[:], pbG[:, 0:128], gc4[:, hs], ALU.subtract, -1.0, ALU.mult)
                c.tt('pool', t3[:], t3[:], LOWI, ALU.mult)
                c.act(t3[:], t3[:], AF.Exp)
                c.tt('pool', t3[:], t3[:], LOWS, ALU.mult)
                c.stt(A[:], pbK[:, 0:128], beta4[:, hs], t3[:], ALU.mult, ALU.mult)
                c.ts('dve', t3[:], pbG[:, 0:128], gc4[:, hs], ALU.subtract)
                c.tt('pool', t3[:], t3[:], UPI, ALU.mult)
                c.act(t3[:], t3[:], AF.Exp)
                c.tt('pool', t3[:], t3[:], UPI, ALU.mult)
                c.tt('dve', AqkT[:], pbK[:, 128:256], t3[:], ALU.mult)
                c.tt('pool', t3[:], t3[:], UPS, ALU.mult)
                c.tt('dve', t3[:], pbG[:, 128:256], t3[:], ALU.mult)
                c.tt('dve', AT[:], pbK[:, 0:128], t3[:], ALU.mult)
                c.ts('pool', Vb, Vh, beta4[:, hs], ALU.mult)
                c.ts('pool', Kbg, Kh, bgc4[:, hs], ALU.mult)
                c.ts('pool', Kd, Kh, egd4[:, hs], ALU.mult)
                yield
                Db, DTb = [HB['D0'], HB['D1']], [HB['DT0'], HB['DT1']]
                X, XT = HB['X'], HB['XT']
                c.tt('pool', X[:], A[:], cg[:, 5, :], ALU.mult)
                c.tt('pool', Db[0][:], identf[:], X[:], ALU.subtract)
                c.tt('pool', XT[:], AT[:], cg[:, 12, :], ALU.mult)
                c.tt('pool', DTb[0][:], identf[:], XT[:], ALU.subtract)
                cur = 0
                pbI_ = c.hold()
                for k in range(1, nlev):
                    last = (k == nlev - 1)
                    Dc, DTc, Dn, DTn = Db[cur], DTb[cur], Db[1 - cur], DTb[1 - cur]
                    c.tt('pool', X[:], A[:], cg[:, 5 + k, :], ALU.mult)
                    if not last:
                        c.tt('pool', XT[:], AT[:], cg[:, 12 + k, :], ALU.mult)
                        c.mm(pbI_[:, 0:128], XT[:], Dc[:])
                    c.mm(pbI_[:, 256:384], X[:], DTc[:])
                    if not last:
                        c.copy('act', HB['Ys'][:], pbI_[:, 0:128])
                    c.copy('act', HB['Y2s'][:], pbI_[:, 256:384])
                    yield
                    if not last:
                        c.mm(pbI_[:, 128:256], DTc[:], HB['Ys'][:])
                    c.mm(pbI_[:, 384:512], Dc[:], HB['Y2s'][:])
                    if not last:
                        c.tt('dve', Dn[:], Dc[:], pbI_[:, 128:256], ALU.subtract)
                    c.tt('dve', DTn[:], DTc[:], pbI_[:, 384:512], ALU.subtract)
                    cur = 1 - cur
                    yield
                c.release(pbI_)
                TT_ = DTb[cur]
                pb = c.bank()
                c.mm(pb[0:64, 0:128], Kbg, TT_[:])
                negWT = G['negWT']
                c.ts('dve', negWT[:], pb[0:64, 0:128], -1.0, ALU.mult)
                Vnew = G['Vnew']
                bc16 = lambda v_, n_: v_.re('p (o t) -> p o t', o=1).bc([64, n_, 128])
                pbV = c.bank()
                c.mm(pbV[:, 0:64], TT_[:], Vb, start=True, stop=False)
                if nseq > 1:
                    for c4 in range(nseq // 4):
                        c.tt('pool', G['negWTm'][:], bc16(negWT[:], 4), G['colm'][:, c4 * 4:(c4 + 1) * 4, :], ALU.mult)
                        for s4 in range(4):
                            s_ = c4 * 4 + s4
                            c.mm(pbV[:, 0:64], G['negWTm'][:, s4, :], S[:, s_, h, :], start=False, stop=(s_ == nseq - 1), signal=True)
                else:
                    c.mm(pbV[:, 0:64], negWT[:], S[:, 0, h, :], start=False, stop=True)
                c.copy('act', Vnew[:], pbV[:, 0:64])
                pbO = c.bank()
                if nseq > 1:
                    for c4 in range(nseq // 4):
                        c.tt('pool', G['QgTm'][:], bc16(QgT, 4), G['colm'][:, c4 * 4:(c4 + 1) * 4, :], ALU.mult)
                        for s4 in range(4):
                            s_ = c4 * 4 + s4
                            c.mm(pbO[:, 0:64], G['QgTm'][:, s4, :], S[:, s_, h, :], start=(s_ == 0), stop=False, signal=True)
                else:
                    c.mm(pbO[:, 0:64], QgT, S[:, 0, h, :], start=True, stop=False)
                c.mm(pbO[:, 0:64], AqkT[:], Vnew[:], start=False, stop=True)
                c.copy('act', G['o_tm'][:, h * 64:(h + 1) * 64], pbO[:, 0:64])
                c.ts('pool', G['grow'][:], G['rowm'][:], g4[:, hs], ALU.mult)
                pbE = c.bank()
                c.mm(pbE[0:64, 0:16], ones128[:, 0:64], G['grow'][:])
                c.act(G['egl'][:], pbE[0:64, 0:16], AF.Exp)
                for s0 in range(0, nseq, 8):
                    n8 = min(8, nseq - s0)
                    pbS = c.bank()
                    if nseq > 1:
                        c.tt('pool', G['Kdm'][:], Kd.re('p (o d) -> p o d', o=1).bc([128, 8, 64]),
                             G['rowm'][:, s0:s0 + 8].re('p (s o) -> p s o', o=1).bc([128, 8, 64]), ALU.mult)
                    for s_ in range(s0, s0 + n8):
                        kd_ = G['Kdm'][:, s_ - s0, :] if nseq > 1 else Kd
                        c.mm(pbS[0:64, (s_ - s0) * 64:(s_ - s0 + 1) * 64], kd_, Vnew[:])
                    c.tt('dve', S[:, s0:s0 + n8, h, :], S[:, s0:s0 + n8, h, :],
                         G['egl'][:, s0:s0 + n8].re('p (s o) -> p s o', o=1).bc([64, n8, 64]), ALU.mult)
                    c.tt('dve', S[:, s0:s0 + n8, h, :], S[:, s0:s0 + n8, h, :], pbS[0:64, 0:n8 * 64].re('p (s v) -> p s v', s=n8), ALU.add)
                if samp:
                    c.dma('sp', Dm['gs_s'][l, :, h].re('s k v -> k s v'), S_full[:, :, 0, :])
            sets = G['hsets']
            groups = {}
            for h in range(4):
                groups.setdefault(h % len(sets), []).append(h)
            lanes = [iter(_chain([head_gen(h, sets[k_]) for h in hs_])) for k_, hs_ in groups.items()]
            while lanes:
                for ln in list(lanes):
                    try:
                        next(ln)
                    except StopIteration:
                        lanes.remove(ln)
            o_tm, rr = G['o_tm'], G['rr']
            c.tt('dve', sc[:], o_tm[:].re('p (h d) -> p h d', h=4), o_tm[:].re('p (h d) -> p h d', h=4), ALU.mult)
            c.op('dve', lambda e: e.reduce_sum(rr.t[:], sc.t[:], axis=AX.X), [sc], [rr])
            c.ts('dve', rr[:], rr[:], 1.0 / 64, ALU.mult, 1e-6, ALU.add)
            c.act(rr[:], rr[:], AF.Sqrt)
            c.op('dve', lambda e: e.reciprocal(rr.t[:], rr.t[:]), [rr], [rr])
            c.tt('dve', o_tm[:].re('p (h d) -> p h d', h=4), o_tm[:].re('p (h d) -> p h d', h=4),
                 rr[:].re('p (h o) -> p h o', o=1).bc([128, 4, 64]), ALU.mult)
            c.tt('dve', o_tm[:].re('p (h d) -> p h d', h=4), o_tm[:].re('p (h d) -> p h d', h=4),
                 G['gain'][:].re('p (o d) -> p o d', o=1).bc([128, 4, 64]), ALU.mult)
            c.tt('dve', o_tm[:], o_tm[:], G['gate'][:], ALU.mult)
            if samp and os.environ.get('DBG'):
                pass
            pb = c.bank()
            for cc in range(2):
                c.tr(pb[:, cc * 128:(cc + 1) * 128], o_tm[:, cc * 128:(cc + 1) * 128], identf[:])
            c.copy('act', yT[:, 2:4, :], pb[:, 0:256].re('p (a t) -> p a t', a=2))
            if (not samp) and i == NTILES - 1:
                c.dma('sp', Dm['gs_p'][l].re('h k v -> k h v'), S_full[:, 0, :, :])


        def proj_q(wA, N):
            for h2_ in range(2):
                pb = c.bank()
                for hl in range(2):
                    hh = h2_ * 2 + hl
                    for g in range(2):
                        c0 = O_NQ + g * 256 + hh * 64 - g * 64
                        r = (hl * 2 + g) * 128
                        for k in range(8):
                            c.mm(pb[:, r:r + 128], wA[:, k, c0:c0 + 128], xT[:, k, :], start=(k == 0), stop=(k == 7))
                for g in range(2):
                    ps = slice(g * 64, (g + 1) * 64)
                    src = pb[ps, :].re('p (hl g t) -> p hl g t', hl=2, g=2)[:, :, g, :]
                    c.copy('act', N['qT'][ps, h2_ * 2:h2_ * 2 + 2, :], src)

        def nsa_setup(l, N):
            for half in range(2):
                if os.environ.get('SKIP_W1'):
                    continue
                for a8 in range(8):
                    c.dma('pool', N['w1b'][half * 64:(half + 1) * 64, a8 * 8:(a8 + 1) * 8, :],
                          Dm['cmp_w1'][l].re('a l d e -> d (a l) e')[:, a8 * 8:(a8 + 1) * 8, :])
            c.dma('pool', N['w2b'][:], Dm['cmp_w2'][l].re('a e d -> e a d'))
            c.dma('pool', N['ov'][:], Dm['ov'][:])
            c.dma('sp', N['cq'][:], Dm['cq'][:])
            c.memset('pool', N['w2pad'][:], 0.0)
            c.copy('pool', N['w2pad'][:, 0, 0:64], N['w2b'][:, 0, :])
            c.copy('pool', N['w2pad'][:, 1, 64:128], N['w2b'][:, 0, :])
            for a in range(2):
                c.dma('sp', N['peT'][:, a, :], Dm['cmp_pe'][l, a].re('l d -> d l'), slow=True)
            c.dma('sp', N['b1T'][:], Dm['cmp_b1'][l].re('a e -> e a'), slow=True)
            c.copy('dve', N['peTb'][:], N['peT'][:])
            pb = c.bank()
            for a in range(2):
                for l_ in range(32):
                    c.mm(pb[0:64, a:a + 1], N['w1b'][0:64, a * 32 + l_, :], N['peTb'][:, a, l_:l_ + 1], start=(l_ == 0), stop=(l_ == 31))
            c.tt('dve', N['biasv'][:], pb[0:64, 0:2], N['b1T'][:], ALU.add)
            c.memset('pool', N['kcT'][:], 0.0)
            c.memset('pool', N['hidvT'][:], 0.0)
            c.memset('pool', N['VC1'][:], 1.0)

        def nsa_tile_p(l, i, N, G, wA, kvtm, yT):
            cg = G['cg']
            LOWS, UPI = cg[:, 1, :], cg[:, 2, :]
            proj_q(wA, N)
            for col, dst in ((O_NKV + 256, N['kslcT'][:, i * 128:(i + 1) * 128]), (O_NWIN, N['kwinT'][:, (i % 8) * 128:(i % 8 + 1) * 128]),
                             (O_NKV, N['cmpT'][:, 0, 16:144]), (O_NKV + 128, N['cmpT'][:, 1, 16:144])):
                pb = c.bank()
                proj_chunk(wA, col, 128, pb)
                c.copy('act', dst, pb[:, 0:128])
            c.copy('pool', N['vslc1'][:, i, :, 0:64], kvtm[:, 384:512].re('p (g d) -> p g d', g=2))
            c.copy('pool', N['vwin1'][:, i % 8, :, 0:64], kvtm[:, 640:768].re('p (g d) -> p g d', g=2))
            nb, col0, c0 = (7, 16, 0) if i == 0 else (8, 0, 8 * i - 1)
            hv = lambda t_: t_[0:64, 0:16].re('p (g j) -> p g j', g=2)[:, :, 0:nb]
            hx, h2, hb = N['hx'], N['h2'], N['hb']
            for a in range(2):
                for g in range(2):
                    ps = slice(g * 64, (g + 1) * 64)
                    pbh = c.bank()
                    for l_ in range(32):
                        rhs = N['cmpT'][ps, a, col0 + l_:col0 + l_ + 16 * (nb - 1) + 1:16]
                        c.mm(pbh[0:64, 0:nb], N['w1b'][ps, a * 32 + l_, :], rhs, start=(l_ == 0), stop=(l_ == 31))
                    c.ts('dve', hx[0:64, g * 8:g * 8 + nb], pbh[0:64, 0:nb], N['biasv'][:, a:a + 1], ALU.add)
                c.tt('dve', hv(h2), hv(hx), hv(hx), ALU.mult)
                c.ts('dve', hv(h2), hv(h2), 0.044715, ALU.mult, 1.0, ALU.add)
                c.tt('dve', hv(h2), hv(h2), hv(hx), ALU.mult)
                c.act(hv(h2), hv(h2), AF.Tanh, scale=0.7978845608028654)
                c.ts('dve', hv(h2), hv(h2), 1.0, ALU.add, 0.5, ALU.mult)
                c.tt('dve', hv(hb), hv(h2), hv(hx), ALU.mult)
                if a == 0:
                    pbk = c.bank()
                    for g in range(2):
                        c.mm(pbk[:, g * 8:g * 8 + nb], N['w2pad'][:, g, :], hb[:, g * 8:g * 8 + nb])
                    for g in range(2):
                        c.copy('act', N['kcT'][g * 64:(g + 1) * 64, c0:c0 + nb], pbk[g * 64:(g + 1) * 64, g * 8:g * 8 + nb])
                else:
                    c.copy('act', N['hidvT'][:, :, c0:c0 + nb], hv(hb))
            c.copy('pool', N['cmpT'][:, :, 0:16], N['cmpT'][:, :, 128:144])
            nct = 1 if 8 * i + 6 < 128 else 2
            for ct in range(nct):
                pbv = c.bank()
                for g in range(2):
                    c.mm(pbv[:, g * 64:(g + 1) * 64], N['hidvT'][:, g, ct * 128:(ct + 1) * 128], N['w2b'][:, 1, :])
                c.copy('act', N['VC1'][:, ct, :, 0:64], pbv[:, 0:128].re('p (g d) -> p g d', g=2))
            pbg = c.bank()
            for k in range(8):
                c.mm(pbg[:, 0:24], xT[:, k, :], wA[:, k, O_NG:O_NG + 24], start=(k == 0), stop=(k == 7))
            c.act(N['gate'][:], pbg[:, 0:24], AF.Sigmoid)
            c.dma('sp', N['impm'][:], Dm['impm'][i])
            c.dma('sp', N['impb'][:], Dm['impb'][i])
            Ebufs = [N['E0'], N['E1']]
            bc4 = lambda v_: v_.re('p (o t) -> p o t', o=1).bc([128, 4, 128])
            for g in range(2):
                ps = slice(g * 64, (g + 1) * 64)
                qg = N['qT'][ps, :, :].re('p h t -> p (h t)')

                def attend(kts, lhs_of, v1_of, mask_of, pbO, extra=None):
                    def stage_a(idx):
                        pbS = c.bank()
                        c.mm(pbS[:, :], lhs_of(kts[idx]), qg)
                        Eb_ = Ebufs[idx % 2]
                        c.act(Eb_[:], pbS[:, :].re('p (h t) -> p h t', h=4), AF.Exp, scale=0.125)
                        mask_of(kts[idx], Eb_)
                        return Eb_
                    Enext = stage_a(0)
                    for idx, kt in enumerate(kts):
                        Eb = Enext
                        if idx + 1 < len(kts):
                            Enext = stage_a(idx + 1)
                        for hh in range(4):
                            if extra is not None:
                                extra(kt, idx, hh, Eb)
                            c.mm(pbO[:, hh * 65:(hh + 1) * 65], Eb[:, hh, :], v1_of(kt), start=(idx == 0 and hh == 0), stop=(idx == len(kts) - 1 and hh == 3))

                def finalize(pbO, br, first):
                    o3 = pbO[:, 0:260].re('p (h e) -> p h e', h=4)
                    rs = N['rs']
                    c.ts('dve', rs[:], o3[:, :, 64], 1e-30, ALU.max)
                    c.op('dve', lambda e: e.reciprocal(rs.t[:], rs.t[:]), [rs], [rs])
                    c.tt('dve', rs[:], rs[:], N['gate'][:, br * 8 + g * 4:br * 8 + g * 4 + 4], ALU.mult)
                    dst = N['ytm'][:, g * 256:(g + 1) * 256].re('p (h d) -> p h d', h=4)
                    rb = rs[:].re('p (h o) -> p h o', o=1).bc([128, 4, 64])
                    if first:
                        c.tt('dve', dst, o3[:, :, 0:64], rb, ALU.mult)
                    else:
                        c.tt('dve', N['tmp'][:], o3[:, :, 0:64], rb, ALU.mult)
                        c.tt('pool', dst, dst, N['tmp'][:], ALU.add)

                pbI = c.hold()
                pbOc = c.hold()

                def mask_cmp(ct, Eb):
                    thr = float(128 * i - 31 - 2048 * ct)
                    c.op('pool', lambda e: e.tensor_single_scalar(N['cm'].t[:], N['cq'].t[:], thr, ALU.is_le), [N['cq']], [N['cm']])
                    c.tt('pool', Eb[:], Eb[:], bc4(N['cm'][:]), ALU.mult)

                def extra_imp(ct, idx, hh, Eb):
                    c.mm(pbI[:, hh * 65:(hh + 1) * 65], Eb[:, hh, :], N['ov'][:, ct, :], start=(idx == 0 and hh == 0), stop=(idx == nct - 1 and hh == 3))

                attend(list(range(nct)), lambda ct: N['kcT'][ps, ct * 128:(ct + 1) * 128], lambda ct: N['VC1'][:, ct, g, :], mask_cmp, pbOc, extra_imp)
                i3 = pbI[:, 0:260].re('p (h e) -> p h e', h=4)
                rs2, imp = N['rs2'], N['imp']
                c.ts('dve', rs2[:], i3[:, :, 64], 1e-30, ALU.max)
                c.op('dve', lambda e: e.reciprocal(rs2.t[:], rs2.t[:]), [rs2], [rs2])
                c.ts('dve', imp[:], i3[:, 0, 0:64], rs2[:, 0:1], ALU.mult)
                for hh in range(1, 4):
                    c.stt(imp[:], i3[:, hh, 0:64], rs2[:, hh:hh + 1], imp[:], ALU.mult, ALU.add)
                c.tt('dve', imp[:], imp[:], N['impm'][:], ALU.mult)
                c.tt('dve', imp[:], imp[:], N['impb'][:], ALU.add)
                c.op('dve', lambda e: e.max(N['m8'].t[:], imp.t[:]), [imp], [N['m8']])
                c.ts('dve', N['sel'][:], imp[:], N['m8'][:, 7:8], ALU.is_ge)
                pbT = c.bank()
                c.tr(pbT[0:64, 0:128], N['sel'][:], identf[:])
                c.copy('act', N['selT'][:], pbT[0:64, 0:128])
                finalize(pbOc, 0, True)
                c.release(pbI)
                c.release(pbOc)
                pbO = c.hold()

                def mask_slc(kt, Eb):
                    pbM = c.bank()
                    c.mm(pbM[:, 0:128], N['eall'][:, kt * 128:(kt + 1) * 128], N['selT'][:])
                    c.tt('dve', Eb[:], Eb[:], bc4(pbM[:, 0:128]), ALU.mult)
                    if kt == i:
                        c.tt('pool', Eb[:], Eb[:], bc4(UPI), ALU.mult)

                attend(list(range(0, i + 1)), lambda kt: N['kslcT'][ps, kt * 128:(kt + 1) * 128], lambda kt: N['vslc1'][:, kt, g, :], mask_slc, pbO)
                finalize(pbO, 1, False)
                c.release(pbO)
                pbO = c.hold()

                def mask_win(kt, Eb):
                    if kt == i:
                        c.tt('pool', Eb[:], Eb[:], bc4(UPI), ALU.mult)
                    if kt == i - 4:
                        c.tt('pool', Eb[:], Eb[:], bc4(LOWS), ALU.mult)

                attend(list(range(max(0, i - 4), i + 1)), lambda kt: N['kwinT'][ps, (kt % 8) * 128:(kt % 8 + 1) * 128],
                       lambda kt: N['vwin1'][:, kt % 8, g, :], mask_win, pbO)
                finalize(pbO, 2, False)
                c.release(pbO)
            pb = c.bank()
            for cc in range(4):
                c.tr(pb[:, cc * 128:(cc + 1) * 128], N['ytm'][:, cc * 128:(cc + 1) * 128], identf[:])
            c.copy('act', yT[:, 4:8, :], pb[:].re('p (a t) -> p a t', a=4))


        def nsa_tile_s(l, N, G, wA, kvtm, yT):
            P_ = cfg.PAST
            npg = P_ // 128
            WS = min(512, P_)
            nwt = WS // 128
            nblk = P_ // 16 - 1
            cgS = G['cg']
            c.dma('sp', N['ptb'][:], V(Dm['page_table'], I['page_table'].partition_broadcast(128)).re('p o n -> p (o n)'))
            c.ts('dve', N['idxA'][:], N['ptb'][:], float(l * cfg.NPOOL), ALU.add, 128.0, ALU.mult)
            c.ts('dve', N['idxA'][:], N['idxA'][:], piota[:, 0:1], ALU.add, 2.0, ALU.mult)
            c.ts('dve', N['idxB'][:], N['idxA'][:], 1.0, ALU.add)
            c.dma('pool', N['eall_s'][:], Dm['eall_s'][:])
            c.dma('sp', N['smask'][:], Dm['smask'][:])
            c.dma('sp', N['impm'][:], Dm['impm_s'][:])
            c.dma('sp', N['impb'][:], Dm['impb_s'][:])
            c.memset('pool', N['Ebig'][:], 0.0)
            c.memset('pool', N['VCs'][:], 1.0)
            c.memset('pool', N['vs1'][:], 1.0)
            c.memset('pool', N['vw1'][:], 1.0)
            c.memset('pool', N['v1n'][:], 1.0)
            proj_q(wA, N)
            c.copy('pool', N['qTs'][:].re('p s h t -> p h s t'), N['qT'][:].re('p h (s t) -> p h s t', s=NS))
            for col, dst in ((O_NKV + 256, N['kslcTn']), (O_NWIN, N['kwinTn'])):
                pb = c.bank()
                proj_chunk(wA, col, 128, pb)
                c.copy('act', dst[:], pb[:, 0:128])
            pbg = c.bank()
            for k in range(8):
                c.mm(pbg[:, 0:24], xT[:, k, :], wA[:, k, O_NG:O_NG + 24], start=(k == 0), stop=(k == 7))
            c.act(N['gate'][:], pbg[:, 0:24], AF.Sigmoid)

            def gather(idx, s_, h0, n8):
                for j in range(n8):
                    col = s_ * npg + h0 + j
                    def fn(e, j=j, col=col):
                        return e.indirect_dma_start(out=N['pk'].t[:, j, :], out_offset=None, in_=I['cache_kv'],
                                                    in_offset=bass.IndirectOffsetOnAxis(ap=idx.t[:, col:col + 1], axis=0))
                    c.dma_custom('pool', fn, [Dm['cache_kv'], idx], [N['pk']])

            def transpose_pages(src_of, dst, ntile):
                for j0 in range(0, ntile, 8):
                    n8 = min(8, ntile - j0)
                    pb = c.bank()
                    pbb = V(pb, pb.t[:].bitcast(BF16))
                    for j in range(j0, j0 + n8):
                        c.tr(pbb[:, (j - j0) * 128:(j - j0 + 1) * 128], src_of(j), identb[:], signal=(j == j0 + n8 - 1))
                    c.copy('act', dst[:, j0 * 128:(j0 + n8) * 128], pbb[:, 0:n8 * 128])

            def ocols(pbO, s_):
                return pbO[0:65, s_ * 32:(s_ + 1) * 32]

            def finalize_s(pbO, g, br, first):
                c.copy('act', N['osb'][:].re('p (h s t) -> p s h t', h=4, s=NS), pbO[0:65, :].re('p (s h t) -> p s h t', s=NS, h=4))
                pbt = c.bank()
                for hh in range(4):
                    c.tr(pbt[:, hh * 65:(hh + 1) * 65], N['osb'][:, hh * 128:(hh + 1) * 128], identf[0:65, 0:65])
                o3 = pbt[:, 0:260].re('p (h e) -> p h e', h=4)
                rs = N['rs']
                c.ts('dve', rs[:], o3[:, :, 64], 1e-30, ALU.max)
                c.op('dve', lambda e: e.reciprocal(rs.t[:], rs.t[:]), [rs], [rs])
                c.tt('dve', rs[:], rs[:], N['gate'][:, br * 8 + g * 4:br * 8 + g * 4 + 4], ALU.mult)
                dst = N['ytm'][:, g * 256:(g + 1) * 256].re('p (h d) -> p h d', h=4)
                rb = rs[:].re('p (h o) -> p h o', o=1).bc([128, 4, 64])
                if first:
                    c.tt('dve', dst, o3[:, :, 0:64], rb, ALU.mult)
                else:
                    c.tt('dve', N['tmp'][:], o3[:, :, 0:64], rb, ALU.mult)
                    c.tt('pool', dst, dst, N['tmp'][:], ALU.add)

            hx, h2, hb = N['hxs'], N['h2s'], N['hbs']
            hv = lambda t_: t_[:, :, 0:nblk]
            pbOc = [c.hold(), c.hold()]
            if os.environ.get('SWAPB'):
                pbOc = pbOc[::-1]
            for s_ in range(NS):
                for h0 in range(0, npg, 8):
                    n8 = min(8, npg - h0)
                    gather(N['idxA'], s_, h0, n8)
                    for a in range(2):
                        transpose_pages(lambda j: N['pk'][:, j, a * 128:(a + 1) * 128], N['cmpTs'][:, a, h0 * 128:(h0 + n8) * 128], n8)
                for a in range(2):
                    for g in range(2):
                        ps = slice(g * 64, (g + 1) * 64)
                        pbh = c.bank()
                        for l_ in range(32):
                            rhs = N['cmpTs'][ps, a, l_:l_ + 16 * (nblk - 1) + 1:16]
                            c.mm(pbh[0:64, 0:nblk], N['w1b'][ps, a * 32 + l_, :], rhs, start=(l_ == 0), stop=(l_ == 31))
                        c.ts('dve', hx[:, g, 0:nblk], pbh[0:64, 0:nblk], N['biasv'][:, a:a + 1], ALU.add)
                    c.tt('dve', hv(h2), hv(hx), hv(hx), ALU.mult)
                    c.ts('dve', hv(h2), hv(h2), 0.044715, ALU.mult, 1.0, ALU.add)
                    c.tt('dve', hv(h2), hv(h2), hv(hx), ALU.mult)
                    c.act(hv(h2), hv(h2), AF.Tanh, scale=0.7978845608028654)
                    c.ts('dve', hv(h2), hv(h2), 1.0, ALU.add, 0.5, ALU.mult)
                    c.tt('dve', hv(hb), hv(h2), hv(hx), ALU.mult)
                    if a == 0:
                        pbk = c.bank()
                        for g in range(2):
                            c.mm(pbk[:, g * 128:g * 128 + nblk], N['w2pad'][:, g, :], hb[:, g, 0:nblk])
                        for g in range(2):
                            c.copy('act', N['kcTs'][g * 64:(g + 1) * 64, 0:nblk], pbk[g * 64:(g + 1) * 64, g * 128:g * 128 + nblk])
                    else:
                        pbv = c.bank()
                        for g in range(2):
                            c.mm(pbv[0:nblk, g * 64:(g + 1) * 64], hb[:, g, 0:nblk], N['w2b'][:, 1, :])
                        c.copy('act', N['VCs'][0:nblk, :, 0:64], pbv[0:nblk, 0:128].re('p (g d) -> p g d', g=2))
                if os.environ.get('DBG') and s_ == NS - 1:
                    dst_ = N['dbgst']
                    c.memset('dve', dst_[:], 0.0)
                    c.copy('dve', dst_[:, 0:nblk], N['kcTs'][:, 0:nblk])
                    c.copy('dve', dst_[0:nblk, 128:258], N['VCs'][0:nblk, :, :].re('p g e -> p (g e)'))
                    c.copy('dve', dst_[0:64, 260:260 + 2 * nblk].re('p (g j) -> p g j', g=2), hb[:, :, 0:nblk])
                    c.copy('dve', dst_[:, 300:332], N['qTs'][:, s_, :, :].re('p h t -> p (h t)'))
                    c.copy('dve', dst_[:, 0:512], N['w1b'][:, 0:8, :].re('p a e -> p (a e)'))
                    c.dma('sp', Dm['dbg2'][:, 1280:1280 + 640], dst_[:])
                for g in range(2):
                    ps = slice(g * 64, (g + 1) * 64)
                    qsel = N['qTs'][ps, s_, :, :].re('p h t -> p (h t)')
                    pbS = c.bank()
                    c.mm(pbS[0:nblk, 0:32], N['kcTs'][ps, 0:nblk], qsel)
                    c.act(N['Ecur'][0:nblk, :], pbS[0:nblk, 0:32], AF.Exp, scale=0.125)
                    c.copy('pool', N['Ebig'][0:nblk, g, :, s_ * 8:(s_ + 1) * 8], N['Ecur'][0:nblk, :].re('p (h t) -> p h t', h=4))
                    c.mm(ocols(pbOc[g], s_), N['VCs'][0:nblk, g, :], N['Ecur'][0:nblk, :])
            for g in range(2):
                pbI = c.bank()
                for hh in range(4):
                    c.mm(pbI[:, hh * 65:(hh + 1) * 65], N['Ebig'][:, g, hh, :], N['ov'][:, 0, :], start=(hh == 0), stop=(hh == 3))
                i3 = pbI[:, 0:260].re('p (h e) -> p h e', h=4)
                rs2, imp = N['rs2'], N['imp']
                c.ts('dve', rs2[:], i3[:, :, 64], 1e-30, ALU.max)
                c.op('dve', lambda e: e.reciprocal(rs2.t[:], rs2.t[:]), [rs2], [rs2])
                c.ts('dve', imp[:], i3[:, 0, 0:64], rs2[:, 0:1], ALU.mult)
                for hh in range(1, 4):
                    c.stt(imp[:], i3[:, hh, 0:64], rs2[:, hh:hh + 1], imp[:], ALU.mult, ALU.add)
                c.tt('dve', imp[:], imp[:], N['impm'][:], ALU.mult)
                c.tt('dve', imp[:], imp[:], N['impb'][:], ALU.add)
                c.op('dve', lambda e: e.max(N['m8'].t[:], imp.t[:]), [imp], [N['m8']])
                c.ts('dve', N['sel'][:], imp[:], N['m8'][:, 7:8], ALU.is_ge)
                pbT = c.bank()
                c.tr(pbT[0:64, 0:128], N['sel'][:], identf[:])
                c.copy('act', N['selTs'][:, g, :], pbT[0:64, 0:128])
                finalize_s(pbOc[g], g, 0, True)
                c.release(pbOc[g])
                if os.environ.get('DBG'):
                    c.dma('sp', Dm['dbg2'][:, g * 256:(g + 1) * 256], N['ytm'][:, g * 256:(g + 1) * 256])
                    c.dma('sp', Dm['dbg2'][:, 1024 + g * 64:1024 + (g + 1) * 64], N['sel'][:])
            pbOs = [c.hold(), c.hold()]
            pbOw = [c.hold(), c.hold()]
            kTs = N['cmpTs'][:, 0, :]
            for s_ in range(NS):
                for h0 in range(0, npg, 8):
                    n8 = min(8, npg - h0)
                    gather(N['idxB'], s_, h0, n8)
                    transpose_pages(lambda j: N['pk'][:, j, 0:128], kTs[:, h0 * 128:(h0 + n8) * 128], n8)
                    c.copy('pool', N['vs1'][:, h0:h0 + n8, :, 0:64], N['pk'][:, 0:n8, 128:256].re('p j (g d) -> p j g d', g=2))
                c.dma('pool', N['pw'][:], Dm['cache_win'][l, s_].re('(j r) f -> r j f', r=128))
                transpose_pages(lambda j: N['pw'][:, j, 0:128], N['kwTs'][:, :], nwt)
                c.copy('pool', N['vw1'][:, :, :, 0:64], N['pw'][:, :, 128:256].re('p j (g d) -> p j g d', g=2))
                pbn = c.bank()
                c.mm(pbn[0:8, 0:128], identf[:, s_ * 8:(s_ + 1) * 8], kvtm[:, 384:512])
                c.mm(pbn[0:8, 128:256], identf[:, s_ * 8:(s_ + 1) * 8], kvtm[:, 640:768])
                c.copy('act', N['v1n'][:, :, :, 0:64], pbn[0:8, 0:256].re('p (b g d) -> p b g d', b=2, g=2))
                for g in range(2):
                    ps = slice(g * 64, (g + 1) * 64)
                    qsel = N['qTs'][ps, s_, :, :].re('p h t -> p (h t)')
                    for br, (KT_, V1_, KTn, ntile, pbO) in enumerate(((kTs, N['vs1'], N['kslcTn'], npg, pbOs[g]),
                                                                      (N['kwTs'][:, :], N['vw1'], N['kwinTn'], nwt, pbOw[g]))):
                        first = True
                        for j0 in range(0, ntile, 8):
                            n8 = min(8, ntile - j0)
                            pbS = c.bank()
                            for j in range(j0, j0 + n8):
                                c.mm(pbS[:, (j - j0) * 32:(j - j0 + 1) * 32], KT_[ps, j * 128:(j + 1) * 128], qsel)
                            Es = N['Es']
                            c.act(Es[:, 0:n8 * 32], pbS[:, 0:n8 * 32], AF.Exp, scale=0.125)
                            E4 = Es[:, 0:n8 * 32].re('p (j h t) -> p j h t', j=n8, h=4)
                            if br == 0:
                                pbM = c.bank()
                                for j in range(j0, j0 + n8):
                                    c.mm(pbM[:, (j - j0) * 8:(j - j0 + 1) * 8], N['eall_s'][:, j * 128:(j + 1) * 128], N['selTs'][:, g, s_ * 8:(s_ + 1) * 8])
                                for hh in range(4):
                                    c.tt('dve', E4[:, :, hh, :], E4[:, :, hh, :], pbM[:, 0:n8 * 8].re('p (j t) -> p j t', j=n8), ALU.mult)
                            elif j0 == 0 and WS == 512:
                                c.tt('pool', E4[:, 0, :, :], E4[:, 0, :, :], N['smask'][:, 0:8].re('p (o t) -> p o t', o=1).bc([128, 4, 8]), ALU.mult)
                            for j in range(j0, j0 + n8):
                                c.mm(ocols(pbO, s_), V1_[:, j, g, :], Es[:, (j - j0) * 32:(j - j0 + 1) * 32], start=first, stop=False)
                                first = False
                        pbS = c.bank()
                        c.mm(pbS[0:8, 0:32], KTn[ps, s_ * 8:(s_ + 1) * 8], qsel)
                        En = N['En']
                        c.act(En[:], pbS[0:8, 0:32].re('p (h t) -> p h t', h=4), AF.Exp, scale=0.125)
                        c.tt('pool', En[:], En[:], N['smask'][0:8, 8:16].re('p (o t) -> p o t', o=1).bc([8, 4, 8]), ALU.mult)
                        c.mm(ocols(pbO, s_), N['v1n'][:, br, g, :], En[:].re('p h t -> p (h t)'), start=False, stop=True)
            for g in range(2):
                finalize_s(pbOs[g], g, 1, False)
                c.release(pbOs[g])
                if os.environ.get('DBG'):
                    c.dma('sp', Dm['dbg2'][:, 512 + g * 256:512 + (g + 1) * 256], N['ytm'][:, g * 256:(g + 1) * 256])
                finalize_s(pbOw[g], g, 2, False)
                c.release(pbOw[g])
            pb = c.bank()
            for cc in range(4):
                c.tr(pb[:, cc * 128:(cc + 1) * 128], N['ytm'][:, cc * 128:(cc + 1) * 128], identf[:])
            c.copy('act', yT[:, 4:8, :], pb[:].re('p (a t) -> p a t', a=4))

        c.dma('sp', lnG[:], V(Dm['ln_emb'], I['ln_emb'].partition_broadcast(128)))
        tiles = list(range(NTILES + 1))

        def xsrc(i):
            return Dm['x_p'][i * 128:(i + 1) * 128, :] if i < NTILES else Dm['x_s'][:, :]

        for i in tiles:
            b = xt[i % 2]
            c.dma('sp', b[:], xsrc(i))
            layer_norm(b[:], xr[:], lnG)
            c.dma('sp', XS0[i][:], xr[:])

        for l in range(L):
            with ExitStack() as ph:
                wA = c.sb([128, 8, NIN], BF16, 'wA', ph)
                wO = c.sb([128, 8, D], BF16, 'wO', ph)
                yT = c.sb([128, 8, 128], BF16, 'yT', ph)
                cwa = c.sb([128, 2, 3], F32, 'cwa', ph); cwg = c.sb([128, 6, 4], F32, 'cwg', ph)
                CH = {}
                qkvT = c.sb([128, 6, 128], F32, 'qkvT', ph)
                kvtm = c.sb([128, 768], F32, 'kvtm', ph)
                G = {}
                N = {}
                if cfg.mix_b:
                    for nm, shp in (('cg', [128, 19, 128]), ('rowm', [128, 16]), ('qkv_tm', [128, 768]),
                                    ('ab', [128, 8]), ('gate', [128, 256]), ('t4', [128, 4]), ('g4', [128, 4]), ('beta4', [128, 4]),
                                    ('gc4', [128, 4]), ('egc4', [128, 4]), ('egd4', [128, 4]), ('bgc4', [128, 4]), ('ss', [128, 8]),
                                    ('dtb', [128, 4]), ('negA', [128, 4]), ('gain', [128, 64]), ('sc', [128, 4, 64]),
                                    ('kqT', [64, 3, 128]), ('dg', [128, 2, 128]), ('t1', [128, 128]), ('t2', [128, 128]), ('t3', [128, 128]),
                                    ('A', [128, 128]), ('AT', [128, 128]), ('AqkT', [128, 128]), ('X', [128, 128]), ('XT', [128, 128]),
                                    ('D0', [128, 128]), ('D1', [128, 128]), ('DT0', [128, 128]), ('DT1', [128, 128]),
                                    ('Ys', [128, 128]), ('Y2s', [128, 128]), ('negWT', [64, 128]),
                                    ('Vnew', [128, 64]), ('o_tm', [128, 256]), ('egl', [64, 16]), ('grow', [128, 16]), ('rr', [128, 4])):
                        G[nm] = c.sb(shp, F32, nm, ph)
                    G['hset0'] = {nm: G[nm] for nm in ['kqT', 't1', 't2', 'A', 'AT', 'AqkT', 'X', 'XT', 'D0', 'D1', 'DT0', 'DT1', 'Ys', 'Y2s']}
                    c.dma('sp', G['dtb'][:], V(Dm['gdn_dt_bias'], I['gdn_dt_bias'][l].partition_broadcast(128)))
                    c.dma('sp', G['negA'][:], V(Dm['gdn_a_log'], I['gdn_a_log'][l].partition_broadcast(128)))
                    c.dma('sp', G['gain'][:], V(Dm['gdn_norm_g'], I['gdn_norm_g'][l].partition_broadcast(128)))
                    c.act(G['negA'][:], G['negA'][:], AF.Exp)
                    c.ts('dve', G['negA'][:], G['negA'][:], -1.0, ALU.mult)
                if cfg.mix_c:
                    for nm, shp, dt_ in (('w1b', [128, 64, 64], BF16), ('w2b', [64, 2, 64], BF16), ('w2pad', [64, 2, 128], BF16),
                                         ('peT', [64, 2, 32], F32), ('peTb', [64, 2, 32], BF16), ('b1T', [64, 2], F32), ('biasv', [64, 2], F32),
                                         ('ov', [128, 2, 65], BF16), ('cq', [128, 128], F32), ('qT', [128, 4, 128], BF16),
                                         ('cmpT', [128, 2, 144], BF16), ('hx', [64, 16], F32), ('h2', [64, 16], F32), ('hb', [64, 16], BF16),
                                         ('kcT', [128, 256], BF16), ('hidvT', [64, 2, 256], BF16), ('VC1', [128, 2, 2, 65], BF16),
                                         ('gate', [128, 24], F32), ('impm', [128, 64], F32), ('impb', [128, 64], F32), ('imp', [128, 64], F32),
                                         ('sel', [128, 64], F32), ('m8', [128, 8], F32), ('selT', [64, 128], BF16), ('rs', [128, 4], F32),
                                         ('rs2', [128, 4], F32), ('ytm', [128, 512], F32), ('tmp', [128, 4, 64], F32), ('cm', [128, 128], F32),
                                         ('E0', [128, 4, 128], BF16), ('E1', [128, 4, 128], BF16)):
                        N[nm] = c.sb(shp, dt_, 'n_' + nm, ph)
                    nsa_setup(l, N)
                WP, WS = min(512, T), min(512, cfg.PAST)
                c.dma('sp', Dm['win_s'][l, :, 0:WS - TS, :], Dm['cache_win'][l, :, TS:WS, :])
                for k in range(8):
                    c.dma('pool', wA[:, k, :], Dm['w_in'][l, k * 128:(k + 1) * 128, :])
                    c.dma('pool', wO[:, k, :], Dm['w_out'][l, k * 128:(k + 1) * 128, :])
                c.dma('sp', lnG[:], V(Dm['ln1'], I['ln1'][l].partition_broadcast(128)))
                for tap in range(3):
                    c.dma('sp', cwa[:, :, tap], Dm['conv_a_w'][l, tap].re('(c p) -> p c', p=128), slow=True)
                for tap in range(4):
                    c.dma('sp', cwg[:, :, tap], Dm['gdn_conv_w'][l, tap].re('(c p) -> p c', p=128), slow=True)

                def run_tile(i):
                    samp = (i == NTILES)
                    nseq, TT = (NS, TS) if samp else (1, 128)
                    chA = CH['A']
                    xb = xt[i % 2]
                    c.dma('sp', xb[:], XS0[i][:])
                    make_xT(xb[:])
                    if samp:
                        load_state_T(lambda ch: chA[:, ch, :, 0:2], 2, Dm['st_conv_a'][l], NS * 2)
                    elif i == 0:
                        c.memset('pool', chA[:, :, :, 0:2], 0.0)
                    for ch in range(2):
                        z1 = znext(); z2 = znext()
                        pb = c.bank()
                        proj_chunk(wA, O_AC + ch * 128, 128, pb)
                        c.copy('act', z1[:], pb[:, 0:128])
                        pb2 = c.bank()
                        proj_chunk(wA, O_AH + ch * 128, 128, pb2)
                        c.tt('dve', chA[:, ch, :, 2:2 + TT], v3(z1[:], nseq), v3(pb2[:, 0:128], nseq), ALU.mult)
                        conv_taps(v3(z2[:], nseq), chA, ch, cwa, 3, TT)
                        pb3 = c.bank()
                        proj_chunk(wA, O_AB + ch * 128, 128, pb3)
                        c.tt('dve', yT[:, ch, :], z2[:], pb3[:, 0:128], ALU.mult)
                    if samp:
                        store_state_T(lambda ch: chA[:, ch, :, TT:TT + 2], 2, Dm['ca_s'][l], NS * 2)
                    elif i == NTILES - 1:
                        store_state_T(lambda ch: chA[:, ch, :, TT:TT + 2], 2, Dm['ca_p'][l], 2)
                    if not samp:
                        c.copy('pool', chA[:, :, :, 0:2], chA[:, :, :, TT:TT + 2])
                    pb = c.bank()
                    for k in range(8):
                        c.mm(pb[:, :], xT[:, k, :], wA[:, k, O_NKV:O_NKV + 512], start=(k == 0), stop=(k == 7))
                    c.copy('act', kvtm[:, 0:512], pb[:, :])
                    pb = c.bank()
                    for k in range(8):
                        c.mm(pb[:, 0:256], xT[:, k, :], wA[:, k, O_NWIN:O_NWIN + 256], start=(k == 0), stop=(k == 7))
                    c.copy('act', kvtm[:, 512:768], pb[:, 0:256])
                    if samp:
                        c.dma('sp', Dm['kv_s'][l], kvtm[:, 0:512])
                        for sq in range(NS):
                            c.dma('sp', Dm['win_s'][l, sq, WS - TS:WS, :], kvtm[sq * TS:(sq + 1) * TS, 512:768])
                    else:
                        c.dma('sp', Dm['kv_p'][l, i * 128:(i + 1) * 128, :], kvtm[:, 0:512])
                        if i * 128 >= T - WP:
                            r0 = i * 128 - (T - WP)
                            c.dma('sp', Dm['win_p'][l, r0:r0 + 128, :], kvtm[:, 512:768])
                    chG = CH['G']
                    if samp:
                        load_state_T(lambda ch: chG[:, ch, :, 0:3], 6, Dm['st_gdn_conv'][l], NS * 3)
                    elif i == 0:
                        c.memset('pool', chG[:, :, :, 0:3], 0.0)
                    for ch in range(6):
                        z2 = znext()
                        pb = c.bank()
                        proj_chunk(wA, O_GQ + ch * 128, 128, pb)
                        c.copy('act', chG[:, ch, :, 3:3 + TT], v3(pb[:, 0:128], nseq))
                        conv_taps(v3(z2[:], nseq), chG, ch, cwg, 4, TT)
                        c.act(qkvT[:, ch, :], z2[:], AF.Silu)
                    if samp:
                        store_state_T(lambda ch: chG[:, ch, :, TT:TT + 3], 6, Dm['gc_s'][l], NS * 3)
                    elif i == NTILES - 1:
                        store_state_T(lambda ch: chG[:, ch, :, TT:TT + 3], 6, Dm['gc_p'][l], 3)
                    if not samp:
                        c.copy('pool', chG[:, :, :, 0:3], chG[:, :, :, TT:TT + 3])
                    if cfg.mix_b:
                        gdn_tile(l, i, samp, G, wA, qkvT, yT)
                    else:
                        c.memset('dve', yT[:, 2:4, :], 0.0)
                    if cfg.mix_c == 1 and samp:
                        nsa_tile_s(l, N, G, wA, kvtm, yT)
                    elif cfg.mix_c and not samp and not os.environ.get('NSA_SKIP_TILE'):
                        nsa_tile_p(l, i, N, G, wA, kvtm, yT)
                    else:
                        c.memset('dve', yT[:, 4:8, :], 0.0)
                    if samp and os.environ.get('DBG'):
                        c.dma('sp', Dm['dbg3'][:, :], yT[:].re('p k t -> p (k t)'))
                    for h in range(2):
                        pb = c.bank()
                        for k in range(8):
                            c.mm(pb[:, :], yT[:, k, :], wO[:, k, h * 512:(h + 1) * 512], start=(k == 0), stop=(k == 7))
                        c.stt(xr[:, h * 512:(h + 1) * 512], xb[:, h * 512:(h + 1) * 512], ALPHA, pb[:, :], ALU.mult, ALU.add)
                    if samp and os.environ.get('DBG'):
                        pass
                    layer_norm(xr[:], xr[:], lnG)
                    c.dma('sp', XS1[i][:], xr[:])

                with ExitStack() as sub:
                    CH['A'] = c.sb([128, 2, 1, 2 + 128], F32, 'chA_p', sub); CH['G'] = c.sb([128, 6, 1, 3 + 128], F32, 'chG_p', sub)
                    if cfg.mix_b:
                        G['S'] = c.sb([64, 1, 4, 64], F32, 'S_p', sub)
                        G['hsets'] = [G['hset0']]
                        for k_ in range(1, NHSETS):
                            G['hsets'].append({nm: c.sb([64, 3, 128] if nm == 'kqT' else [128, 128], F32, nm + 'x', sub) for nm in ['kqT', 't1', 't2', 'A', 'AT', 'AqkT', 'X', 'XT', 'D0', 'D1', 'DT0', 'DT1', 'Ys', 'Y2s']})
                    if cfg.mix_c:
                        N['kslcT'] = c.sb([128, T], BF16, 'kslcT', sub)
                        N['vslc1'] = c.sb([128, NTILES, 2, 65], BF16, 'vslc1', sub)
                        N['eall'] = c.sb([64, T], BF16, 'eall', sub)
                        N['kwinT'] = c.sb([128, 1024], BF16, 'kwinT', sub)
                        N['vwin1'] = c.sb([128, 8, 2, 65], BF16, 'vwin1', sub)
                        c.memset('pool', N['vwin1'][:], 1.0)
                        c.dma('pool', N['eall'][:], Dm['eall'][:])
                        c.memset('pool', N['vslc1'][:], 1.0)
                    for i in range(NTILES):
                        run_tile(i)
                    c.barrier()
                with ExitStack() as sub:
                    CH['A'] = c.sb([128, 2, NS, 2 + TS], F32, 'chA_s', sub); CH['G'] = c.sb([128, 6, NS, 3 + TS], F32, 'chG_s', sub)
                    if cfg.mix_b:
                        G['hsets'] = [G['hset0']]
                        G['S'] = c.sb([64, 16, 1, 64], F32, 'S_s', sub)
                        for nm, shp, dt_ in (('colm', [64, 16, 128], BF16), ('negWTm', [64, 4, 128], F32), ('QgTm', [64, 4, 128], F32), ('Kdm', [128, 8, 64], F32)):
                            G[nm] = c.sb(shp, dt_, nm, sub)
                    if cfg.mix_c == 1:
                        NPG = cfg.PAST // 128
                        WS_ = min(512, cfg.PAST)
                        for nm, shp, dt_ in (('ptb', [128, NS * NPG], I32), ('idxA', [128, NS * NPG], I32), ('idxB', [128, NS * NPG], I32),
                                             ('eall_s', [64, cfg.PAST + 128], BF16), ('smask', [128, 16], F32), ('Ebig', [128, 2, 4, 128], BF16),
                                             ('VCs', [128, 2, 65], BF16), ('vs1', [128, NPG, 2, 65], BF16), ('vw1', [128, WS_ // 128, 2, 65], BF16),
                                             ('v1n', [8, 2, 2, 65], BF16), ('kslcTn', [128, 128], BF16), ('qTs', [128, NS, 4, TS], BF16), ('Ecur', [128, 32], BF16), ('kwinTn', [128, 128], BF16),
                                             ('pk', [128, min(NPG, 8), 256], BF16), ('cmpTs', [128, 2, cfg.PAST], BF16), ('pw', [128, WS_ // 128, 256], BF16),
                                             ('kwTs', [128, WS_], BF16), ('hxs', [64, 2, 128], F32), ('h2s', [64, 2, 128], F32), ('hbs', [64, 2, 128], BF16),
                                             ('kcTs', [128, 128], BF16), ('selTs', [64, 2, 128], BF16), ('osb', [65, 512], F32),
                                             ('Es', [128, 256], BF16), ('En', [8, 4, 8], BF16), ('dbgst', [128, 640], F32)):
                            N[nm] = c.sb(shp, dt_, 'ns_' + nm, sub)
                    run_tile(NTILES)
                    c.barrier()

            with ExitStack() as ph:
                wU = c.sb([128, 8, 2 * DFF], BF16, 'wU', ph)
                wD = c.sb([128, 22, D], BF16, 'wD', ph)
                cwf = c.sb([128, 22, 3], F32, 'cwf', ph)
                chFb = c.sb([128, 22, NS * (2 + TS)], F32, 'chF', ph)
                actT = c.sb([128, 22, 128], BF16, 'actT', ph)
                chF_p = chFb[:, :, 0:130].re('p c (s t) -> p c s t', s=1)
                chF_s = chFb[:, :, :].re('p c (s t) -> p c s t', s=NS)
                for k in range(8):
                    c.dma('pool', wU[:, k, :], Dm['w_up'][l, k * 128:(k + 1) * 128, :])
                for k in range(22):
                    c.dma('pool', wD[:, k, :], Dm['w_down'][l, k * 128:(k + 1) * 128, :])
                c.dma('sp', lnG[:], V(Dm['ln2'], I['ln2'][l].partition_broadcast(128)))
                for tap in range(3):
                    c.dma('sp', cwf[:, :, tap], Dm['ffn_conv_w'][l, tap].re('(c p) -> p c', p=128), slow=True)
                for i in tiles:
                    samp = (i == NTILES)
                    nseq, TT = (NS, TS) if samp else (1, 128)
                    chF = chF_s if samp else chF_p
                    xb = xt[i % 2]
                    c.dma('sp', xb[:], XS1[i][:])
                    make_xT(xb[:])
                    if samp:
                        load_state_T(lambda ch: chF[:, ch, :, 0:2], 22, Dm['st_ffn_conv'][l], NS * 2)
                    elif i == 0:
                        c.memset('pool', chF[:, :, :, 0:2], 0.0)
                    for ch in range(22):
                        z1 = znext(); z2 = znext()
                        pb = c.bank()
                        proj_chunk(wU, ch * 128, 128, pb)
                        c.copy('act', chF[:, ch, :, 2:2 + TT], v3(pb[:, 0:128], nseq))
                        conv_taps(v3(z2[:], nseq), chF, ch, cwf, 3, TT)
                        c.act(z1[:], z2[:], AF.Silu)
                        pb2 = c.bank()
                        proj_chunk(wU, DFF + ch * 128, 128, pb2)
                        c.tt('dve', actT[:, ch, :], z1[:], pb2[:, 0:128], ALU.mult)
                    if samp:
                        store_state_T(lambda ch: chF[:, ch, :, TT:TT + 2], 22, Dm['fc_s'][l], NS * 2)
                    elif i == NTILES - 1:
                        store_state_T(lambda ch: chF[:, ch, :, TT:TT + 2], 22, Dm['fc_p'][l], 2)
                    if not samp:
                        c.copy('pool', chF[:, :, :, 0:2], chF[:, :, :, TT:TT + 2])
                    for h in range(2):
                        pb = c.bank()
                        for k in range(22):
                            c.mm(pb[:, :], actT[:, k, :], wD[:, k, h * 512:(h + 1) * 512], start=(k == 0), stop=(k == 21))
                        c.stt(xr[:, h * 512:(h + 1) * 512], xb[:, h * 512:(h + 1) * 512], ALPHA, pb[:, :], ALU.mult, ALU.add)
                    layer_norm(xr[:], xr[:], lnG)
                    if l == L - 1:
                        c.dma('sp', (Dm['y_p'][i * 128:(i + 1) * 128, :] if not samp else Dm['y_s'][:, :]), xr[:])
                    else:
                        c.dma('sp', XS0[i][:], xr[:])
                c.barrier()
        c.finish()
        print("instructions:", c.ninstr)
    return nc


OUT_NAMES = ['y_p', 'y_s', 'kv_p', 'kv_s', 'win_p', 'win_s', 'ca_p', 'ca_s', 'gc_p', 'gc_s', 'gs_p', 'gs_s', 'fc_p', 'fc_s']


def make_in_maps(cfg, inp, ncores=8):
    L, NS = cfg.L, cfg.NS
    nb = inp['x_prompt'].shape[0]
    f = np.ascontiguousarray
    shared = {
        'ln_emb': f(np.stack([inp['ln_emb_g'], inp['ln_emb_b']])),
        'w_in': f(inp['w_in']), 'w_out': f(inp['w_out']), 'w_up': f(inp['w_up']), 'w_down': f(inp['w_down']),
        'conv_a_w': f(inp['conv_a_w']), 'gdn_conv_w': f(inp['gdn_conv_w']), 'ffn_conv_w': f(inp['ffn_conv_w']),
        'gdn_a_log': f(inp['gdn_a_log']), 'gdn_dt_bias': f(inp['gdn_dt_bias']), 'gdn_norm_g': f(inp['gdn_norm_g']),
        'cmp_pe': f(inp['cmp_pe']), 'cmp_w1': f(inp['cmp_w1']), 'cmp_b1': f(inp['cmp_b1']), 'cmp_w2': f(inp['cmp_w2']),
        'ln1': f(np.stack([inp['ln1_g'], inp['ln1_b']], axis=1)), 'ln2': f(np.stack([inp['ln2_g'], inp['ln2_b']], axis=1)),
    }
    for v_, (ns_, ts_) in (('p', (1, 128)), ('s', (cfg.NS, cfg.TS))):
        cg, rowm, colm = gdn_consts(ns_, ts_)
        shared['cg_' + v_], shared['rowm_' + v_], shared['colm_' + v_] = cg, rowm, colm
    shared['ov'], shared['cq'], shared['eall'], shared['impm'], shared['impb'] = nsa_consts(cfg)
    P_ = cfg.PAST
    n_ = np.arange(64)
    shared['eall_s'] = (np.arange(P_ + 128)[None, :] // 64 == n_[:, None]).astype(np.float32)
    qpos = P_ + (np.arange(128) % cfg.TS)
    valid = (n_[None, :] * 64 <= qpos[:, None]); forced = (n_[None, :] == 0) | (n_[None, :] == qpos[:, None] // 64)
    shared['impm_s'] = (valid & ~forced).astype(np.float32)
    shared['impb_s'] = np.where(forced, 1e9, np.where(valid, 0.0, -1e9)).astype(np.float32)
    sm = np.zeros((128, 16), np.float32)
    sm[:, 0:8] = (np.arange(128)[:, None] > np.arange(8)[None, :])
    sm[0:8, 8:16] = (np.arange(8)[:, None] <= np.arange(8)[None, :])
    shared['smask'] = sm
    shared['cache_kv'] = f(inp['cache_nsa_kv']).reshape(-1, 256)
    maps = []
    for i in range(ncores):
        sl = slice(NS * i, NS * (i + 1))
        m = dict(shared)
        m['x_p'] = f(inp['x_prompt'][i % nb])
        m['x_s'] = f(inp['x_sample'][sl].reshape(128, D))
        m['st_conv_a'] = f(inp['state_conv_a'][:, sl].reshape(L, NS * 2, 256))
        m['page_table'] = f(inp['page_table'][sl].astype(np.int32).reshape(1, -1))
        m['cache_win'] = f(inp['cache_nsa_win'][:, sl].reshape(L, NS, -1, 256))
        m['st_gdn_conv'] = f(inp['state_gdn_conv'][:, sl].reshape(L, NS * 3, 768))
        m['st_gdn'] = f(inp['state_gdn'][:, sl])
        m['st_ffn_conv'] = f(inp['state_ffn_conv'][:, sl].reshape(L, NS * 2, DFF))
        maps.append(m)
    return maps


def gather(cfg, res, ncores=8, nb=4):
    L, NS, T = cfg.L, cfg.NS, cfg.T
    WP = min(512, T)
    WS = min(512, cfg.PAST)
    r = res

    def P(name, shp):
        return np.stack([r[i][name].reshape((L,) + shp) for i in range(nb)], axis=1)

    def S(name, shp):
        return np.concatenate([r[i][name].reshape((L, NS) + shp) for i in range(ncores)], axis=1)

    y_p = np.stack([r[i]['y_p'] for i in range(nb)], axis=0)
    y_s = np.concatenate([r[i]['y_s'].reshape(NS, cfg.TS, D) for i in range(ncores)], axis=0)
    return (y_p, y_s,
            P('kv_p', (T, 4, 2, 64)), S('kv_s', (cfg.TS, 4, 2, 64)),
            P('win_p', (WP, 2, 2, 64)), S('win_s', (WS, 2, 2, 64)),
            P('ca_p', (2, 256)), S('ca_s', (2, 256)),
            P('gc_p', (3, 768)), S('gc_s', (3, 768)),
            P('gs_p', (4, 64, 64)), S('gs_s', (4, 64, 64)),
            P('fc_p', (2, DFF)), S('fc_s', (2, DFF)))


def kernel(**inputs):
    inp = {k: np.asarray(v) for k, v in inputs.items()}
    cfg = Cfg()
    nc = build(cfg)
    maps = make_in_maps(cfg, inp, 8)
    res = run_bass_kernel_spmd(nc, maps, core_ids=list(range(8)))
    outs = gather(cfg, res.results, 8, 4)
    return tuple(np.ascontiguousarray(o.astype(np.float32)) for o in outs)
```

```python
import numpy as np
import os
from contextlib import ExitStack
import concourse.bass as bass
import concourse.mybir as mybir
from concourse.bass_utils import run_bass_kernel_spmd

F32 = mybir.dt.float32
BF16 = mybir.dt.bfloat16
I32 = mybir.dt.int32
AF = mybir.ActivationFunctionType
ALU = mybir.AluOpType
AX = mybir.AxisListType

D = 1024
DFF = 2816
NIN = 3104
ALPHA = (2.0 * 4) ** 0.25
NHSETS = int(os.environ.get('NHSETS', '4'))
TRANSITIVE = os.environ.get('TRANSITIVE', '1') == '1'
EPS = 1e-5
O_AB, O_AC, O_AH, O_GQ, O_GK, O_GV, O_GG, O_GA, O_GB, O_NQ, O_NKV, O_NWIN, O_NG = (
    0, 256, 512, 768, 1024, 1280, 1536, 1792, 1796, 1800, 2312, 2824, 3080)


class Cfg:
    def __init__(self, T=4096, L=4, NS=16, TS=8, PAST=2048, NPOOL=2560, mix_b=True, mix_c=True):
        self.T, self.L, self.NS, self.TS, self.PAST, self.NPOOL = T, L, NS, TS, PAST, NPOOL
        self.NT = T // 128
        self.mix_b, self.mix_c = mix_b, mix_c


class Buf:
    def __init__(self, t, name):
        self.t = t
        self.name = name
        self.w = None
        self.r = {}

    def __getitem__(self, key):
        return V(self, self.t[key])


class V:
    def __init__(self, buf, ap):
        self.buf = buf
        self.ap = ap

    def __getitem__(self, key):
        return V(self.buf, self.ap[key])

    def re(self, pat, **kw):
        return V(self.buf, self.ap.rearrange(pat, **kw))

    def bc(self, shape):
        return V(self.buf, self.ap.to_broadcast(list(shape)))


def _aps(x):
    return x.ap if isinstance(x, V) else x


def _chain(gens):
    for g_ in gens:
        yield from g_


class _HeadView:
    def __init__(self, v, h):
        self.v, self.h = v, h

    def __getitem__(self, key):
        k = list(key)
        if isinstance(k[2], int):
            k[2] = 0
        return self.v[tuple(k)]


class Ctx:
    def __init__(self, nc, es, n_dma_sems=30):
        self.nc, self.es = nc, es
        self.eng = {'pe': nc.tensor, 'act': nc.scalar, 'dve': nc.vector, 'pool': nc.gpsimd, 'sp': nc.sync}
        self.sem = {k: es.enter_context(nc.semaphore('s_' + k)) for k in self.eng}
        self.cnt = {k: 0 for k in self.eng}
        self.seen = {k: {} for k in self.eng}
        self.snap = {k: {} for k in self.eng}
        self.dsnap = {}
        self.dsem = [es.enter_context(nc.semaphore('d%d' % i)) for i in range(n_dma_sems)]
        self.dcnt = [0] * n_dma_sems
        self.dnext = {'sp': 0, 'pool': 0, 'act': 0}
        self.dpool = {'sp': list(range(0, 14)), 'pool': list(range(14, n_dma_sems)), 'act': []}
        self.nbuf = 0
        self.banks = []
        self.bnext = 0
        self.ninstr = 0

    def sb(self, shape, dt=F32, name=None, es=None):
        self.nbuf += 1
        name = (name or 'b') + str(self.nbuf)
        esz = 2 if dt == BF16 else 4
        n = 1
        for d_ in shape[1:]:
            n *= d_
        npad = -(-(n * esz) // 64) * 64 // esz
        t = (es or self.es).enter_context(self.nc.sbuf_tensor(name, [shape[0], npad], dt))
        ap = t[:, 0:n]
        if len(shape) > 2:
            names = ['d%d' % i for i in range(len(shape) - 1)]
            pat = 'p (' + ' '.join(names) + ') -> p ' + ' '.join(names)
            ap = ap.rearrange(pat, **{nm: sz for nm, sz in zip(names[:-1], shape[1:-1])})
        return Buf(ap, name)

    def barrier(self):
        for e in self.eng:
            for k in self.eng:
                if k != e and self.cnt[k] > 0:
                    self._wait(e, k, self.cnt[k])
            for j in range(len(self.dsem)):
                if self.dcnt[j] > 0:
                    self._wait(e, j, self.dcnt[j])

    def mkbanks(self):
        for i in range(8):
            t = self.es.enter_context(self.nc.psum_tensor('bank%d' % i, [128, 512], F32))
            self.banks.append(Buf(t, 'bank%d' % i))

    def bank(self):
        if not hasattr(self, 'rot'):
            self.rot = list(self.banks)
        b = self.rot.pop(0)
        self.rot.append(b)
        return b

    def hold(self):
        if not hasattr(self, 'rot'):
            self.rot = list(self.banks)
        return self.rot.pop(0)

    def release(self, b):
        self.rot.append(b)

    def dram(self, ap, name):
        class _T:
            pass
        b = Buf(None, name)
        b.t = ap
        return b

    def _semof(self, key):
        return self.sem[key] if isinstance(key, str) else self.dsem[key]

    def _wait(self, e, key, val):
        if key == e and e == 'pe':
            return
        if self.seen[e].get(key, 0) >= val:
            return
        self.eng[e].wait_ge(self._semof(key), val)
        self.seen[e][key] = val
        self.ninstr += 1
        if TRANSITIVE:
            sn = self.snap[key].get(val) if isinstance(key, str) else self.dsnap.get((key, val))
            if sn:
                se = self.seen[e]
                for k2, v2 in sn.items():
                    if k2 != e and se.get(k2, 0) < v2:
                        se[k2] = v2

    def _deps(self, e, reads, writes):
        for b in reads:
            if b.w is not None:
                self._wait(e, *b.w)
        for b in writes:
            if b.w is not None:
                self._wait(e, *b.w)
            for k, v in b.r.items():
                self._wait(e, k, v)

    def _mark(self, ev, reads, writes):
        for b in reads:
            b.r[ev[0]] = ev[1]
        for b in writes:
            b.w = ev
            b.r = {}

    def op(self, e, fn, reads=(), writes=(), signal=True):
        reads = [x.buf if isinstance(x, V) else x for x in reads]
        writes = [x.buf if isinstance(x, V) else x for x in writes]
        self._deps(e, reads, writes)
        ins = fn(self.eng[e])
        self.ninstr += 1
        if signal or e != 'pe':
            self.cnt[e] += 1
            ins.then_inc(self.sem[e], 1)
            ev = (e, self.cnt[e])
            self.snap[e][self.cnt[e]] = dict(self.seen[e])
        else:
            ev = (e, self.cnt[e] + 1)
        self._mark(ev, reads, writes)
        return ins

    def dma(self, e, out, in_, slow=False):
        pl = self.dpool[e]
        j = pl[self.dnext[e]]
        self.dnext[e] = (self.dnext[e] + 1) % len(pl)
        if self.dcnt[j] > 0:
            self._wait(e, j, self.dcnt[j])
        self._deps(e, [in_.buf], [out.buf])
        self.dcnt[j] += 16
        if slow:
            ins = self.eng[e].dma_start(out=out.ap, in_=in_.ap, allow_slow_non_contiguous=True)
        else:
            ins = self.eng[e].dma_start(out=out.ap, in_=in_.ap)
        ins.then_inc(self.dsem[j], 16)
        self.ninstr += 1
        self.dsnap[(j, self.dcnt[j])] = dict(self.seen[e])
        self._mark((j, self.dcnt[j]), [in_.buf], [out.buf])
        return ins

    def dma_custom(self, e, fn, reads, writes):
        pl = self.dpool[e]
        j = pl[self.dnext[e]]
        self.dnext[e] = (self.dnext[e] + 1) % len(pl)
        if self.dcnt[j] > 0:
            self._wait(e, j, self.dcnt[j])
        self._deps(e, reads, writes)
        self.dcnt[j] += 16
        ins = fn(self.eng[e])
        ins.then_inc(self.dsem[j], 16)
        self.ninstr += 1
        self.dsnap[(j, self.dcnt[j])] = dict(self.seen[e])
        self._mark((j, self.dcnt[j]), reads, writes)
        return ins

    def mm(self, out, lhsT, rhs, start=True, stop=True, signal=None):
        if signal is None:
            signal = stop
        return self.op('pe', lambda e: e.matmul(out.ap, lhsT=lhsT.ap, rhs=rhs.ap, start=start, stop=stop),
                       [lhsT, rhs], [out], signal=signal)

    def tr(self, out, in_, ident, signal=True):
        return self.op('pe', lambda e: e.transpose(out.ap, in_.ap, ident.ap), [in_, ident], [out], signal=signal)

    def act(self, out, in_, func, bias=None, scale=None, accum=None, e='act'):
        kw = {}
        rd = [in_]
        if bias is not None:
            kw['bias'] = _aps(bias)
            if isinstance(bias, V):
                rd.append(bias)
        if scale is not None:
            kw['scale'] = _aps(scale)
            if isinstance(scale, V):
                rd.append(scale)
        wr = [out]
        if accum is not None:
            kw['accum_out'] = accum.ap
            wr.append(accum)
        return self.op('act', lambda e_: e_.activation(out.ap, in_.ap, func, **kw), rd, wr)

    def copy(self, e, out, in_):
        if e == 'act':
            return self.op('act', lambda e_: e_.copy(out.ap, in_.ap), [in_], [out])
        return self.op(e, lambda e_: e_.tensor_copy(out.ap, in_.ap), [in_], [out])

    def tt(self, e, out, in0, in1, op):
        return self.op(e, lambda e_: e_.tensor_tensor(out.ap, in0.ap, in1.ap, op), [in0, in1], [out])

    def ts(self, e, out, in0, s1, op0, s2=None, op1=None, accum=None):
        rd = [in0] + [s for s in (s1, s2) if isinstance(s, V)]
        wr = [out] + ([accum] if accum is not None else [])
        kw = {}
        if accum is not None:
            kw['accum_out'] = accum.ap
        if op1 is None:
            return self.op(e, lambda e_: e_.tensor_scalar(out.ap, in0.ap, _aps(s1), None, op0, **kw), rd, wr)
        return self.op(e, lambda e_: e_.tensor_scalar(out.ap, in0.ap, _aps(s1), _aps(s2), op0, op1, **kw), rd, wr)

    def stt(self, out, in0, s, in1, op0, op1):
        rd = [in0, in1] + ([s] if isinstance(s, V) else [])
        return self.op('dve', lambda e_: e_.scalar_tensor_tensor(out.ap, in0.ap, _aps(s), in1.ap, op0, op1), rd, [out])

    def memset(self, e, out, val):
        return self.op(e, lambda e_: e_.memset(out.ap, val), [], [out])

    def finish(self):
        for j in range(len(self.dsem)):
            if self.dcnt[j] > 0:
                self._wait('sp', j, self.dcnt[j])


def gdn_consts(nseq, TS):
    i = np.arange(128)
    seq = i // TS if nseq > 1 else np.zeros(128, np.int64)
    same = seq[:, None] == seq[None, :]
    lowi = same & (i[:, None] >= i[None, :])
    lows = same & (i[:, None] > i[None, :])
    cg = np.zeros((128, 19, 128), np.float32)
    cg[:, 0], cg[:, 1], cg[:, 2], cg[:, 3], cg[:, 4] = lowi, lows, lowi.T, lows.T, same
    for k in range(7):
        b = 2 << k
        lm = same & (i[:, None] // b == i[None, :] // b) & ((i[:, None] % b) >= b // 2) & ((i[None, :] % b) < b // 2)
        cg[:, 5 + k] = lm
        cg[:, 12 + k] = lm.T
    rowm = np.zeros((128, 16), np.float32)
    rowm[i, seq] = 1.0
    colm = np.ascontiguousarray(np.broadcast_to(rowm.T[None], (64, 16, 128))).astype(np.float32)
    return cg, rowm, colm


def nsa_consts(cfg):
    T, NT = cfg.T, cfg.NT
    c_ = np.arange(128)
    n = np.arange(64)
    ov = np.zeros((128, 2, 65), np.float32)
    for ct in range(2):
        cc = (ct * 128 + c_)[:, None] * 16
        ov[:, ct, :64] = (cc < n[None, :] * 64 + 64) & (cc + 32 > n[None, :] * 64)
        ov[:, ct, 64] = 1.0
    cq = (16.0 * c_[:, None] - c_[None, :]).astype(np.float32)
    eall = (np.arange(T)[None, :] // 64 == n[:, None]).astype(np.float32)
    qpos = np.arange(T)
    valid = (n[None, :] * 64 <= qpos[:, None])
    forced = (n[None, :] == 0) | (n[None, :] == qpos[:, None] // 64)
    mult = (valid & ~forced).astype(np.float32).reshape(NT, 128, 64)
    bias = np.where(forced, 1e9, np.where(valid, 0.0, -1e9)).astype(np.float32).reshape(NT, 128, 64)
    return ov, cq, eall, mult, bias


def build(cfg):
    T, L, NS, TS, NTILES = cfg.T, cfg.L, cfg.NS, cfg.TS, cfg.NT
    nc = bass.Bass("TRN2", target_bir_lowering=False)
    es = ExitStack()

    def din(name, shape, dt=F32):
        return nc.dram_tensor(name, list(shape), dt, kind="ExternalInput").ap()

    def dout(name, shape, dt=F32):
        return nc.dram_tensor(name, list(shape), dt, kind="ExternalOutput").ap()

    def dscr(name, shape, dt=F32):
        return nc.dram_tensor(name, list(shape), dt, kind="Internal").ap()

    I = {}
    I['x_p'] = din('x_p', [T, D]); I['x_s'] = din('x_s', [128, D])
    I['st_conv_a'] = din('st_conv_a', [L, NS * 2, 256])
    I['st_gdn_conv'] = din('st_gdn_conv', [L, NS * 3, 768])
    I['st_gdn'] = din('st_gdn', [L, NS, 4, 64, 64])
    I['st_ffn_conv'] = din('st_ffn_conv', [L, NS * 2, DFF])
    I['cache_win'] = din('cache_win', [L, NS, min(512, cfg.PAST), 256])
    for v_ in ('p', 's'):
        I['cg_' + v_] = din('cg_' + v_, [128, 19, 128]); I['rowm_' + v_] = din('rowm_' + v_, [128, 16]); I['colm_' + v_] = din('colm_' + v_, [64, 16, 128])
    I['gdn_a_log'] = din('gdn_a_log', [L, 4]); I['gdn_dt_bias'] = din('gdn_dt_bias', [L, 4]); I['gdn_norm_g'] = din('gdn_norm_g', [L, 64])
    NPG = cfg.PAST // 128
    I['cache_kv'] = din('cache_kv', [L * cfg.NPOOL * 128 * 2, 256])
    I['page_table'] = din('page_table', [1, NS * NPG], I32)
    I['eall_s'] = din('eall_s', [64, cfg.PAST + 128]); I['impm_s'] = din('impm_s', [128, 64]); I['impb_s'] = din('impb_s', [128, 64])
    I['smask'] = din('smask', [128, 16])
    I['ov'] = din('ov', [128, 2, 65]); I['cq'] = din('cq', [128, 128]); I['eall'] = din('eall', [64, T])
    I['impm'] = din('impm', [NTILES, 128, 64]); I['impb'] = din('impb', [NTILES, 128, 64])
    I['cmp_pe'] = din('cmp_pe', [L, 2, 32, 64]); I['cmp_w1'] = din('cmp_w1', [L, 2, 32, 64, 64]); I['cmp_b1'] = din('cmp_b1', [L, 2, 64])
    I['cmp_w2'] = din('cmp_w2', [L, 2, 64, 64])
    I['ln_emb'] = din('ln_emb', [2, D])
    I['w_in'] = din('w_in', [L, D, NIN]); I['w_out'] = din('w_out', [L, D, D])
    I['w_up'] = din('w_up', [L, D, 2 * DFF]); I['w_down'] = din('w_down', [L, DFF, D])
    I['conv_a_w'] = din('conv_a_w', [L, 3, 256]); I['gdn_conv_w'] = din('gdn_conv_w', [L, 4, 768])
    I['ffn_conv_w'] = din('ffn_conv_w', [L, 3, DFF])
    I['ln1'] = din('ln1', [L, 2, D]); I['ln2'] = din('ln2', [L, 2, D])
    O = {}
    if os.environ.get('DBG'):
        O['dbg2'] = dout('dbg2', [128, 2048]); O['dbg3'] = dout('dbg3', [128, 1024], BF16)
    O['y_p'] = dout('y_p', [T, D]); O['y_s'] = dout('y_s', [128, D])
    O['ca_p'] = dout('ca_p', [L, 2, 256]); O['ca_s'] = dout('ca_s', [L, NS * 2, 256])
    O['gc_p'] = dout('gc_p', [L, 3, 768]); O['gc_s'] = dout('gc_s', [L, NS * 3, 768])
    O['fc_p'] = dout('fc_p', [L, 2, DFF]); O['fc_s'] = dout('fc_s', [L, NS * 2, DFF])
    O['kv_p'] = dout('kv_p', [L, T, 512]); O['kv_s'] = dout('kv_s', [L, 128, 512])
    O["win_p"] = dout("win_p", [L, min(512, T), 256]); O["win_s"] = dout("win_s", [L, NS, min(512, cfg.PAST), 256])
    O['gs_p'] = dout('gs_p', [L, 4, 64, 64]); O['gs_s'] = dout('gs_s', [L, NS, 4, 64, 64])
    xs0 = dscr('xs0', [NTILES + 1, 128, D]); xs1 = (dout if os.environ.get('DBG') else dscr)('xs1', [NTILES + 1, 128, D])

    with es:
        c = Ctx(nc, es)
        c.mkbanks()
        Dm = {k: c.dram(v, k) for k, v in list(I.items()) + list(O.items())}
        XS0 = [c.dram(xs0[i], 'xs0_%d' % i) for i in range(NTILES + 1)]
        XS1 = [c.dram(xs1[i], 'xs1_%d' % i) for i in range(NTILES + 1)]

        identf = c.sb([128, 128], F32, 'identf')
        c.memset('pool', identf[:], 0.0)
        c.op('pool', lambda e: e.affine_select(out=identf.t[:], in_=identf.t[:], pattern=[[-1, 128]],
                                                compare_op=ALU.not_equal, fill=1.0, base=0, channel_multiplier=1),
             [identf], [identf])

        identb = c.sb([128, 128], BF16, 'identb')
        c.copy('dve', identb[:], identf[:])
        piota = c.sb([128, 1], I32, 'piota')
        c.op('pool', lambda e: e.iota(piota.t[:], pattern=[[0, 1]], base=0, channel_multiplier=1), [], [piota])
        ones128 = c.sb([128, 128], F32, 'ones128')
        c.memset('pool', ones128[:], 1.0)
        lnG = c.sb([128, 2, D], F32, 'lnG')
        xt = [c.sb([128, D], F32, 'xt')] * 2
        xr = c.sb([128, D], F32, 'xr')
        xT = c.sb([128, 8, 128], BF16, 'xT')
        st6 = c.sb([128, 2, 6], F32, 'st6'); mv = c.sb([128, 2], F32, 'mv'); rstd = c.sb([128, 1], F32, 'rstd')
        rows = c.sb([48, 512], F32, 'rows')
        zring = [c.sb([128, 128], F32, 'z') for _ in range(6)]
        zstate = [0]

        def znext():
            zstate[0] = (zstate[0] + 1) % len(zring)
            return zring[zstate[0]]
        z1 = zring[0]

        def layer_norm(src, dst, gb):
            for k in range(2):
                c.op('dve', lambda e: e.bn_stats(st6.t[:, k, :], src.ap[:, k * 512:(k + 1) * 512]), [src], [st6])
            c.op('dve', lambda e: e.bn_aggr(mv.t[:], st6.t[:]), [st6], [mv])
            c.ts('dve', rstd[:], mv[:, 1:2], EPS, ALU.add)
            c.act(rstd[:], rstd[:], AF.Sqrt)
            c.op('dve', lambda e: e.reciprocal(rstd.t[:], rstd.t[:]), [rstd], [rstd])
            c.ts('dve', dst, src, mv[:, 0:1], ALU.subtract, rstd[:, 0:1], ALU.mult)
            c.tt('dve', dst[:, 0:512], dst[:, 0:512], gb[:, 0, 0:512], ALU.mult)
            c.tt('pool', dst[:, 512:1024], dst[:, 512:1024], gb[:, 0, 512:1024], ALU.mult)
            c.tt('dve', dst[:, 0:512], dst[:, 0:512], gb[:, 1, 0:512], ALU.add)
            c.tt('pool', dst[:, 512:1024], dst[:, 512:1024], gb[:, 1, 512:1024], ALU.add)

        def make_xT(src):
            for h in range(2):
                pb = c.bank()
                for k in range(4):
                    c.tr(pb[:, k * 128:(k + 1) * 128], src[:, (h * 4 + k) * 128:(h * 4 + k + 1) * 128], identf[:], signal=(k == 3))
                c.copy('act', xT[:, h * 4:(h + 1) * 4, :], pb[:].re('p (k t) -> p k t', k=4))

        def proj_chunk(w, col, m, pb, n0=0):
            for k in range(8):
                c.mm(pb[0:m, n0:n0 + 128], w[:, k, col:col + m], xT[:, k, :], start=(k == 0), stop=(k == 7))

        def load_state_T(dst, nch, dram_rows, R):
            for c0 in range(0, nch, 4):
                n = min(4, nch - c0)
                c.dma('sp', rows[0:R, 0:n * 128], dram_rows[:, c0 * 128:(c0 + n) * 128])
                for ch in range(n):
                    pb = c.bank()
                    c.tr(pb[:, 0:R], rows[0:R, ch * 128:(ch + 1) * 128], identf[0:R, 0:R])
                    d = dst(c0 + ch)
                    c.copy('dve', d, pb[:, 0:R].re('p (s j) -> p s j', s=d.ap.shape[1]))

        def store_state_T(src, nch, dram_rows, R):
            for c0 in range(0, nch, 4):
                n = min(4, nch - c0)
                for ch in range(n):
                    pb = c.bank()
                    sv = src(c0 + ch)
                    zz = znext()
                    c.copy('dve', zz[:, 0:R].re('p (s j) -> p s j', s=sv.ap.shape[1]), sv)
                    c.tr(pb[0:R, 0:128], zz[:, 0:R], identf[:])
                    c.copy('act', rows[0:R, ch * 128:(ch + 1) * 128], pb[0:R, 0:128])
                c.dma('sp', dram_rows[:, c0 * 128:(c0 + n) * 128], rows[0:R, 0:n * 128])

        def conv_taps(dst, ch, cidx, wts, K, TT):
            c.ts('dve', dst, ch[:, cidx, :, 0:TT], wts[:, cidx, 0:1], ALU.mult)
            for i in range(1, K):
                c.stt(dst, ch[:, cidx, :, i:i + TT], wts[:, cidx, i:i + 1], dst, ALU.mult, ALU.add)

        def v3(v, nseq):
            return v.re('p (s t) -> p s t', s=nseq)


        def gdn_tile(l, i, samp, G, wA, qkvT, yT):
            nseq = NS if samp else 1
            nlev = 3 if samp else 7
            vn = 's' if samp else 'p'
            if i == 0 or samp:
                c.dma('sp', G['cg'][:], Dm['cg_' + vn][:]); c.dma('sp', G['rowm'][:], Dm['rowm_' + vn][:])
                if samp:
                    c.dma('pool', G['colm'][:], Dm['colm_' + vn][:])
            cg = G['cg']
            LOWI, LOWS, UPI, UPS, BLK = (cg[:, j, :] for j in range(5))
            S_full = G['S']
            if (not samp) and i == 0:
                c.memset('pool', S_full[:], 0.0)
            qkv = G['qkv_tm']
            for ch in range(6):
                pb = c.bank()
                c.tr(pb[:, 0:128], qkvT[:, ch, :], identf[:])
                c.copy('act', qkv[:, ch * 128:(ch + 1) * 128], pb[:, 0:128])
            pb = c.bank()
            for k in range(8):
                c.mm(pb[:, 0:8], xT[:, k, :], wA[:, k, O_GA:O_GA + 8], start=(k == 0), stop=(k == 7))
            c.copy('act', G['ab'][:], pb[:, 0:8])
            pb = c.bank()
            for k in range(8):
                c.mm(pb[:, 0:256], xT[:, k, :], wA[:, k, O_GG:O_GG + 256], start=(k == 0), stop=(k == 7))
            c.act(G['gate'][:], pb[:, 0:256], AF.Silu)
            t4, g4, beta4, gc4, egc4, egd4, bgc4 = (G[n] for n in ('t4', 'g4', 'beta4', 'gc4', 'egc4', 'egd4', 'bgc4'))
            c.tt('dve', t4[:], G['ab'][:, 0:4], G['dtb'][:], ALU.add)
            c.act(t4[:], t4[:], AF.Exp)
            c.ts('dve', t4[:], t4[:], 1.0, ALU.add)
            c.act(t4[:], t4[:], AF.Ln)
            c.tt('dve', g4[:], t4[:], G['negA'][:], ALU.mult)
            c.act(beta4[:], G['ab'][:, 4:8], AF.Sigmoid)
            pb = c.bank()
            c.mm(pb[:, 0:4], UPI, g4[:])
            c.mm(pb[:, 4:8], BLK, g4[:])
            c.copy('act', gc4[:], pb[:, 0:4])
            c.act(egc4[:], gc4[:], AF.Exp)
            c.tt('dve', egd4[:], pb[:, 4:8], gc4[:], ALU.subtract)
            c.act(egd4[:], egd4[:], AF.Exp)
            c.tt('dve', bgc4[:], beta4[:], egc4[:], ALU.mult)
            sc = G['sc']
            ss = G['ss']
            for hh in range(2):
                c.tt('dve', sc[:], qkv[:, hh * 256:(hh + 1) * 256].re('p (h d) -> p h d', h=4), qkv[:, hh * 256:(hh + 1) * 256].re('p (h d) -> p h d', h=4), ALU.mult)
                c.op('dve', lambda e: e.reduce_sum(ss.t[:, hh * 4:(hh + 1) * 4], sc.t[:], axis=AX.X), [sc], [ss])
            c.ts('dve', ss[:], ss[:], 1e-6, ALU.add)
            c.act(ss[:], ss[:], AF.Sqrt)
            c.op('dve', lambda e: e.reciprocal(ss.t[:], ss.t[:]), [ss], [ss])
            c.ts('dve', ss[:, 0:4], ss[:, 0:4], 0.125, ALU.mult)
            for hh in range(2):
                c.tt('dve', qkv[:, hh * 256:(hh + 1) * 256].re('p (h d) -> p h d', h=4), qkv[:, hh * 256:(hh + 1) * 256].re('p (h d) -> p h d', h=4),
                     ss[:, hh * 4:(hh + 1) * 4].re('p (h o) -> p h o', o=1).bc([128, 4, 64]), ALU.mult)
            def head_gen(h, HB):
                Qh, Kh, Vh = qkv[:, h * 64:(h + 1) * 64], qkv[:, 256 + h * 64:256 + (h + 1) * 64], qkv[:, 512 + h * 64:512 + (h + 1) * 64]
                hs = slice(h, h + 1)
                if samp:
                    c.dma('sp', S_full[:, :, 0, :], Dm['st_gdn'][l, :, h].re('s k v -> k s v'))
                    S = _HeadView(S_full, h)
                else:
                    S = S_full
                Vb, Kbg, Kd, Qg = HB['t1'][:, 0:64], HB['t1'][:, 64:128], HB['t2'][:, 0:64], HB['t2'][:, 64:128]
                c.ts('dve', Qg, Qh, egc4[:, hs], ALU.mult)
                pb = c.bank()
                c.tr(pb[0:64, 0:128], Kh, identf[:])
                c.tr(pb[0:64, 128:256], Qh, identf[:])
                c.tr(pb[0:64, 256:384], Qg, identf[:])
                kqT = HB['kqT']
                c.copy('act', kqT[:], pb[0:64, 0:384].re('p (a t) -> p a t', a=3))
                KT, QT, QgT = kqT[:, 0, :], kqT[:, 1, :], kqT[:, 2, :]
                dg = G['dg']
                c.ts('pool', dg[:, 0, :], identf[:], gc4[:, hs], ALU.mult)
                c.ts('pool', dg[:, 1, :], identf[:], beta4[:, hs], ALU.mult)
                pbG = c.bank()
                c.mm(pbG[:, 0:128], ones128[:], dg[:, 0, :])
                c.mm(pbG[:, 128:256], ones128[:], dg[:, 1, :])
                pbK = c.bank()
                c.mm(pbK[:, 0:128], KT, KT)
                c.mm(pbK[:, 128:256], KT, QT)
                t3 = G['t3']
                A, AT, AqkT = HB['A'], HB['AT'], HB['AqkT']
                c.ts('dve', t3[:], pbG[:, 0:128], gc4[:, hs], ALU.subtract, -1.0, ALU.mult)
                c.tt('pool', t3[:], t3[:], LOWI, ALU.mult)
                c.act(t3[:], t3[:], AF.Exp)
                c.tt('pool', t3[:], t3[:], LOWS, ALU.mult)
                c.stt(A[:], pbK[:, 0:128], beta4[:, hs], t3[:], ALU.mult, ALU.mult)
                c.ts('dve', t3[:], pbG[:, 0:128], gc4[:, hs], ALU.subtract)
                c.tt('pool', t3[:], t3[:], UPI, ALU.mult)
                c.act(t3[:], t3[:], AF.Exp)
                c.tt('pool', t3[:], t3[:], UPI, ALU.mult)
                c.tt('dve', AqkT[:], pbK[:, 128:256], t3[:], ALU.mult)
                c.tt('pool', t3[:], t3[:], UPS, ALU.mult)
                c.tt('dve', t3[:], pbG[:, 128:256], t3[:], ALU.mult)
                c.tt('dve', AT[:], pbK[:, 0:128], t3[:], ALU.mult)
                c.ts('pool', Vb, Vh, beta4[:, hs], ALU.mult)
                c.ts('pool', Kbg, Kh, bgc4[:, hs], ALU.mult)
                c.ts('pool', Kd, Kh, egd4[:, hs], ALU.mult)
                yield
                Db, DTb = [HB['D0'], HB['D1']], [HB['DT0'], HB['DT1']]
                X, XT = HB['X'], HB['XT']
                c.tt('pool', X[:], A[:], cg[:, 5, :], ALU.mult)
                c.tt('pool', Db[0][:], identf[:], X[:], ALU.subtract)
                c.tt('pool', XT[:], AT[:], cg[:, 12, :], ALU.mult)
                c.tt('pool', DTb[0][:], identf[:], XT[:], ALU.subtract)
                cur = 0
                pbI_ = c.hold()
                for k in range(1, nlev):
                    last = (k == nlev - 1)
                    Dc, DTc, Dn, DTn = Db[cur], DTb[cur], Db[1 - cur], DTb[1 - cur]
                    c.tt('pool', X[:], A[:], cg[:, 5 + k, :], ALU.mult)
                    if not last:
                        c.tt('pool', XT[:], AT[:], cg[:, 12 + k, :], ALU.mult)
                        c.mm(pbI_[:, 0:128], XT[:], Dc[:])
                    c.mm(pbI_[:, 256:384], X[:], DTc[:])
                    if not last:
                        c.copy('act', HB['Ys'][:], pbI_[:, 0:128])
                    c.copy('act', HB['Y2s'][:], pbI_[:, 256:384])
                    yield
                    if not last:
                        c.mm(pbI_[:, 128:256], DTc[:], HB['Ys'][:])
                    c.mm(pbI_[:, 384:512], Dc[:], HB['Y2s'][:])
                    if not last:
                        c.tt('dve', Dn[:], Dc[:], pbI_[:, 128:256], ALU.subtract)
                    c.tt('dve', DTn[:], DTc[:], pbI_[:, 384:512], ALU.subtract)
                    cur = 1 - cur
                    yield
                c.release(pbI_)
                TT_ = DTb[cur]
                pb = c.bank()
                c.mm(pb[0:64, 0:128], Kbg, TT_[:])
                negWT = G['negWT']
                c.ts('dve', negWT[:], pb[0:64, 0:128], -1.0, ALU.mult)
                Vnew = G['Vnew']
                bc16 = lambda v_, n_: v_.re('p (o t) -> p o t', o=1).bc([64, n_, 128])
                pbV = c.bank()
                c.mm(pbV[:, 0:64], TT_[:], Vb, start=True, stop=False)
                if nseq > 1:
                    for c4 in range(nseq // 4):
                        c.tt('pool', G['negWTm'][:], bc16(negWT[:], 4), G['colm'][:, c4 * 4:(c4 + 1) * 4, :], ALU.mult)
                        for s4 in range(4):
                            s_ = c4 * 4 + s4
                            c.mm(pbV[:, 0:64], G['negWTm'][:, s4, :], S[:, s_, h, :], start=False, stop=(s_ == nseq - 1), signal=True)
                else:
                    c.mm(pbV[:, 0:64], negWT[:], S[:, 0, h, :], start=False, stop=True)
                c.copy('act', Vnew[:], pbV[:, 0:64])
                pbO = c.bank()
                if nseq > 1:
                    for c4 in range(nseq // 4):
                        c.tt('pool', G['QgTm'][:], bc16(QgT, 4), G['colm'][:, c4 * 4:(c4 + 1) * 4, :], ALU.mult)
                        for s4 in range(4):
                            s_ = c4 * 4 + s4
                            c.mm(pbO[:, 0:64], G['QgTm'][:, s4, :], S[:, s_, h, :], start=(s_ == 0), stop=False, signal=True)
                else:
                    c.mm(pbO[:, 0:64], QgT, S[:, 0, h, :], start=True, stop=False)
                c.mm(pbO[:, 0:64], AqkT[:], Vnew[:], start=False, stop=True)
                c.copy('act', G['o_tm'][:, h * 64:(h + 1) * 64], pbO[:, 0:64])
                c.ts('pool', G['grow'][:], G['rowm'][:], g4[:, hs], ALU.mult)
                pbE = c.bank()
                c.mm(pbE[0:64, 0:16], ones128[:, 0:64], G['grow'][:])
                c.act(G['egl'][:], pbE[0:64, 0:16], AF.Exp)
                for s0 in range(0, nseq, 8):
                    n8 = min(8, nseq - s0)
                    pbS = c.bank()
                    if nseq > 1:
                        c.tt('pool', G['Kdm'][:], Kd.re('p (o d) -> p o d', o=1).bc([128, 8, 64]),
                             G['rowm'][:, s0:s0 + 8].re('p (s o) -> p s o', o=1).bc([128, 8, 64]), ALU.mult)
                    for s_ in range(s0, s0 + n8):
                        kd_ = G['Kdm'][:, s_ - s0, :] if nseq > 1 else Kd
                        c.mm(pbS[0:64, (s_ - s0) * 64:(s_ - s0 + 1) * 64], kd_, Vnew[:])
                    c.tt('dve', S[:, s0:s0 + n8, h, :], S[:, s0:s0 + n8, h, :],
                         G['egl'][:, s0:s0 + n8].re('p (s o) -> p s o', o=1).bc([64, n8, 64]), ALU.mult)
                    c.tt('dve', S[:, s0:s0 + n8, h, :], S[:, s0:s0 + n8, h, :], pbS[0:64, 0:n8 * 64].re('p (s v) -> p s v', s=n8), ALU.add)
                if samp:
                    c.dma('sp', Dm['gs_s'][l, :, h].re('s k v -> k s v'), S_full[:, :, 0, :])
            sets = G['hsets']
            groups = {}
            for h in range(4):
                groups.setdefault(h % len(sets), []).append(h)
            lanes = [iter(_chain([head_gen(h, sets[k_]) for h in hs_])) for k_, hs_ in groups.items()]
            while lanes:
                for ln in list(lanes):
                    try:
                        next(ln)
                    except StopIteration:
                        lanes.remove(ln)
            o_tm, rr = G['o_tm'], G['rr']
            c.tt('dve', sc[:], o_tm[:].re('p (h d) -> p h d', h=4), o_tm[:].re('p (h d) -> p h d', h=4), ALU.mult)
            c.op('dve', lambda e: e.reduce_sum(rr.t[:], sc.t[:], axis=AX.X), [sc], [rr])
            c.ts('dve', rr[:], rr[:], 1.0 / 64, ALU.mult, 1e-6, ALU.add)
            c.act(rr[:], rr[:], AF.Sqrt)
            c.op('dve', lambda e: e.reciprocal(rr.t[:], rr.t[:]), [rr], [rr])
            c.tt('dve', o_tm[:].re('p (h d) -> p h d', h=4), o_tm[:].re('p (h d) -> p h d', h=4),
                 rr[:].re('p (h o) -> p h o', o=1).bc([128, 4, 64]), ALU.mult)
            c.tt('dve', o_tm[:].re('p (h d) -> p h d', h=4), o_tm[:].re('p (h d) -> p h d', h=4),
                 G['gain'][:].re('p (o d) -> p o d', o=1).bc([128, 4, 64]), ALU.mult)
            c.tt('dve', o_tm[:], o_tm[:], G['gate'][:], ALU.mult)
            if samp and os.environ.get('DBG'):
                pass
            pb = c.bank()
            for cc in range(2):
                c.tr(pb[:, cc * 128:(cc + 1) * 128], o_tm[:, cc * 128:(cc + 1) * 128], identf[:])
            c.copy('act', yT[:, 2:4, :], pb[:, 0:256].re('p (a t) -> p a t', a=2))
            if (not samp) and i == NTILES - 1:
                c.dma('sp', Dm['gs_p'][l].re('h k v -> k h v'), S_full[:, 0, :, :])


        def proj_q(wA, N):
            for h2_ in range(2):
                pb = c.bank()
                for hl in range(2):
                    hh = h2_ * 2 + hl
                    for g in range(2):
                        c0 = O_NQ + g * 256 + hh * 64 - g * 64
                        r = (hl * 2 + g) * 128
                        for k in range(8):
                            c.mm(pb[:, r:r + 128], wA[:, k, c0:c0 + 128], xT[:, k, :], start=(k == 0), stop=(k == 7))
                for g in range(2):
                    ps = slice(g * 64, (g + 1) * 64)
                    src = pb[ps, :].re('p (hl g t) -> p hl g t', hl=2, g=2)[:, :, g, :]
                    c.copy('act', N['qT'][ps, h2_ * 2:h2_ * 2 + 2, :], src)

        def nsa_setup(l, N):
            for half in range(2):
                if os.environ.get('SKIP_W1'):
                    continue
                for a8 in range(8):
                    c.dma('pool', N['w1b'][half * 64:(half + 1) * 64, a8 * 8:(a8 + 1) * 8, :],
                          Dm['cmp_w1'][l].re('a l d e -> d (a l) e')[:, a8 * 8:(a8 + 1) * 8, :])
            c.dma('pool', N['w2b'][:], Dm['cmp_w2'][l].re('a e d -> e a d'))
            c.dma('pool', N['ov'][:], Dm['ov'][:])
            c.dma('sp', N['cq'][:], Dm['cq'][:])
            c.memset('pool', N['w2pad'][:], 0.0)
            c.copy('pool', N['w2pad'][:, 0, 0:64], N['w2b'][:, 0, :])
            c.copy('pool', N['w2pad'][:, 1, 64:128], N['w2b'][:, 0, :])
            for a in range(2):
                c.dma('sp', N['peT'][:, a, :], Dm['cmp_pe'][l, a].re('l d -> d l'), slow=True)
            c.dma('sp', N['b1T'][:], Dm['cmp_b1'][l].re('a e -> e a'), slow=True)
            c.copy('dve', N['peTb'][:], N['peT'][:])
            pb = c.bank()
            for a in range(2):
                for l_ in range(32):
                    c.mm(pb[0:64, a:a + 1], N['w1b'][0:64, a * 32 + l_, :], N['peTb'][:, a, l_:l_ + 1], start=(l_ == 0), stop=(l_ == 31))
            c.tt('dve', N['biasv'][:], pb[0:64, 0:2], N['b1T'][:], ALU.add)
            c.memset('pool', N['kcT'][:], 0.0)
            c.memset('pool', N['hidvT'][:], 0.0)
            c.memset('pool', N['VC1'][:], 1.0)

        def nsa_tile_p(l, i, N, G, wA, kvtm, yT):
            cg = G['cg']
            LOWS, UPI = cg[:, 1, :], cg[:, 2, :]
            proj_q(wA, N)
            for col, dst in ((O_NKV + 256, N['kslcT'][:, i * 128:(i + 1) * 128]), (O_NWIN, N['kwinT'][:, (i % 8) * 128:(i % 8 + 1) * 128]),
                             (O_NKV, N['cmpT'][:, 0, 16:144]), (O_NKV + 128, N['cmpT'][:, 1, 16:144])):
                pb = c.bank()
                proj_chunk(wA, col, 128, pb)
                c.copy('act', dst, pb[:, 0:128])
            c.copy('pool', N['vslc1'][:, i, :, 0:64], kvtm[:, 384:512].re('p (g d) -> p g d', g=2))
            c.copy('pool', N['vwin1'][:, i % 8, :, 0:64], kvtm[:, 640:768].re('p (g d) -> p g d', g=2))
            nb, col0, c0 = (7, 16, 0) if i == 0 else (8, 0, 8 * i - 1)
            hv = lambda t_: t_[0:64, 0:16].re('p (g j) -> p g j', g=2)[:, :, 0:nb]
            hx, h2, hb = N['hx'], N['h2'], N['hb']
            for a in range(2):
                for g in range(2):
                    ps = slice(g * 64, (g + 1) * 64)
                    pbh = c.bank()
                    for l_ in range(32):
                        rhs = N['cmpT'][ps, a, col0 + l_:col0 + l_ + 16 * (nb - 1) + 1:16]
                        c.mm(pbh[0:64, 0:nb], N['w1b'][ps, a * 32 + l_, :], rhs, start=(l_ == 0), stop=(l_ == 31))
                    c.ts('dve', hx[0:64, g * 8:g * 8 + nb], pbh[0:64, 0:nb], N['biasv'][:, a:a + 1], ALU.add)
                c.tt('dve', hv(h2), hv(hx), hv(hx), ALU.mult)
                c.ts('dve', hv(h2), hv(h2), 0.044715, ALU.mult, 1.0, ALU.add)
                c.tt('dve', hv(h2), hv(h2), hv(hx), ALU.mult)
                c.act(hv(h2), hv(h2), AF.Tanh, scale=0.7978845608028654)
                c.ts('dve', hv(h2), hv(h2), 1.0, ALU.add, 0.5, ALU.mult)
                c.tt('dve', hv(hb), hv(h2), hv(hx), ALU.mult)
                if a == 0:
                    pbk = c.bank()
                    for g in range(2):
                        c.mm(pbk[:, g * 8:g * 8 + nb], N['w2pad'][:, g, :], hb[:, g * 8:g * 8 + nb])
                    for g in range(2):
                        c.copy('act', N['kcT'][g * 64:(g + 1) * 64, c0:c0 + nb], pbk[g * 64:(g + 1) * 64, g * 8:g * 8 + nb])
                else:
                    c.copy('act', N['hidvT'][:, :, c0:c0 + nb], hv(hb))
            c.copy('pool', N['cmpT'][:, :, 0:16], N['cmpT'][:, :, 128:144])
            nct = 1 if 8 * i + 6 < 128 else 2
            for ct in range(nct):
                pbv = c.bank()
                for g in range(2):
                    c.mm(pbv[:, g * 64:(g + 1) * 64], N['hidvT'][:, g, ct * 128:(ct + 1) * 128], N['w2b'][:, 1, :])
                c.copy('act', N['VC1'][:, ct, :, 0:64], pbv[:, 0:128].re('p (g d) -> p g d', g=2))
            pbg = c.bank()
            for k in range(8):
                c.mm(pbg[:, 0:24], xT[:, k, :], wA[:, k, O_NG:O_NG + 24], start=(k == 0), stop=(k == 7))
            c.act(N['gate'][:], pbg[:, 0:24], AF.Sigmoid)
            c.dma('sp', N['impm'][:], Dm['impm'][i])
            c.dma('sp', N['impb'][:], Dm['impb'][i])
            Ebufs = [N['E0'], N['E1']]
            bc4 = lambda v_: v_.re('p (o t) -> p o t', o=1).bc([128, 4, 128])
            for g in range(2):
                ps = slice(g * 64, (g + 1) * 64)
                qg = N['qT'][ps, :, :].re('p h t -> p (h t)')

                def attend(kts, lhs_of, v1_of, mask_of, pbO, extra=None):
                    def stage_a(idx):
                        pbS = c.bank()
                        c.mm(pbS[:, :], lhs_of(kts[idx]), qg)
                        Eb_ = Ebufs[idx % 2]
                        c.act(Eb_[:], pbS[:, :].re('p (h t) -> p h t', h=4), AF.Exp, scale=0.125)
                        mask_of(kts[idx], Eb_)
                        return Eb_
                    Enext = stage_a(0)
                    for idx, kt in enumerate(kts):
                        Eb = Enext
                        if idx + 1 < len(kts):
                            Enext = stage_a(idx + 1)
                        for hh in range(4):
                            if extra is not None:
                                extra(kt, idx, hh, Eb)
                            c.mm(pbO[:, hh * 65:(hh + 1) * 65], Eb[:, hh, :], v1_of(kt), start=(idx == 0 and hh == 0), stop=(idx == len(kts) - 1 and hh == 3))

                def finalize(pbO, br, first):
                    o3 = pbO[:, 0:260].re('p (h e) -> p h e', h=4)
                    rs = N['rs']
                    c.ts('dve', rs[:], o3[:, :, 64], 1e-30, ALU.max)
                    c.op('dve', lambda e: e.reciprocal(rs.t[:], rs.t[:]), [rs], [rs])
                    c.tt('dve', rs[:], rs[:], N['gate'][:, br * 8 + g * 4:br * 8 + g * 4 + 4], ALU.mult)
                    dst = N['ytm'][:, g * 256:(g + 1) * 256].re('p (h d) -> p h d', h=4)
                    rb = rs[:].re('p (h o) -> p h o', o=1).bc([128, 4, 64])
                    if first:
                        c.tt('dve', dst, o3[:, :, 0:64], rb, ALU.mult)
                    else:
                        c.tt('dve', N['tmp'][:], o3[:, :, 0:64], rb, ALU.mult)
                        c.tt('pool', dst, dst, N['tmp'][:], ALU.add)

                pbI = c.hold()
                pbOc = c.hold()

                def mask_cmp(ct, Eb):
                    thr = float(128 * i - 31 - 2048 * ct)
                    c.op('pool', lambda e: e.tensor_single_scalar(N['cm'].t[:], N['cq'].t[:], thr, ALU.is_le), [N['cq']], [N['cm']])
                    c.tt('pool', Eb[:], Eb[:], bc4(N['cm'][:]), ALU.mult)

                def extra_imp(ct, idx, hh, Eb):
                    c.mm(pbI[:, hh * 65:(hh + 1) * 65], Eb[:, hh, :], N['ov'][:, ct, :], start=(idx == 0 and hh == 0), stop=(idx == nct - 1 and hh == 3))

                attend(list(range(nct)), lambda ct: N['kcT'][ps, ct * 128:(ct + 1) * 128], lambda ct: N['VC1'][:, ct, g, :], mask_cmp, pbOc, extra_imp)
                i3 = pbI[:, 0:260].re('p (h e) -> p h e', h=4)
                rs2, imp = N['rs2'], N['imp']
                c.ts('dve', rs2[:], i3[:, :, 64], 1e-30, ALU.max)
                c.op('dve', lambda e: e.reciprocal(rs2.t[:], rs2.t[:]), [rs2], [rs2])
                c.ts('dve', imp[:], i3[:, 0, 0:64], rs2[:, 0:1], ALU.mult)
                for hh in range(1, 4):
                    c.stt(imp[:], i3[:, hh, 0:64], rs2[:, hh:hh + 1], imp[:], ALU.mult, ALU.add)
                c.tt('dve', imp[:], imp[:], N['impm'][:], ALU.mult)
                c.tt('dve', imp[:], imp[:], N['impb'][:], ALU.add)
                c.op('dve', lambda e: e.max(N['m8'].t[:], imp.t[:]), [imp], [N['m8']])
                c.ts('dve', N['sel'][:], imp[:], N['m8'][:, 7:8], ALU.is_ge)
                pbT = c.bank()
                c.tr(pbT[0:64, 0:128], N['sel'][:], identf[:])
                c.copy('act', N['selT'][:], pbT[0:64, 0:128])
                finalize(pbOc, 0, True)
                c.release(pbI)
                c.release(pbOc)
                pbO = c.hold()

                def mask_slc(kt, Eb):
                    pbM = c.bank()
                    c.mm(pbM[:, 0:128], N['eall'][:, kt * 128:(kt + 1) * 128], N['selT'][:])
                    c.tt('dve', Eb[:], Eb[:], bc4(pbM[:, 0:128]), ALU.mult)
                    if kt == i:
                        c.tt('pool', Eb[:], Eb[:], bc4(UPI), ALU.mult)

                attend(list(range(0, i + 1)), lambda kt: N['kslcT'][ps, kt * 128:(kt + 1) * 128], lambda kt: N['vslc1'][:, kt, g, :], mask_slc, pbO)
                finalize(pbO, 1, False)
                c.release(pbO)
                pbO = c.hold()

                def mask_win(kt, Eb):
                    if kt == i:
                        c.tt('pool', Eb[:], Eb[:], bc4(UPI), ALU.mult)
                    if kt == i - 4:
                        c.tt('pool', Eb[:], Eb[:], bc4(LOWS), ALU.mult)

                attend(list(range(max(0, i - 4), i + 1)), lambda kt: N['kwinT'][ps, (kt % 8) * 128:(kt % 8 + 1) * 128],
                       lambda kt: N['vwin1'][:, kt % 8, g, :], mask_win, pbO)
                finalize(pbO, 2, False)
                c.release(pbO)
            pb = c.bank()
            for cc in range(4):
                c.tr(pb[:, cc * 128:(cc + 1) * 128], N['ytm'][:, cc * 128:(cc + 1) * 128], identf[:])
            c.copy('act', yT[:, 4:8, :], pb[:].re('p (a t) -> p a t', a=4))


        def nsa_tile_s(l, N, G, wA, kvtm, yT):
            P_ = cfg.PAST
            npg = P_ // 128
            WS = min(512, P_)
            nwt = WS // 128
            nblk = P_ // 16 - 1
            cgS = G['cg']
            c.dma('sp', N['ptb'][:], V(Dm['page_table'], I['page_table'].partition_broadcast(128)).re('p o n -> p (o n)'))
            c.ts('dve', N['idxA'][:], N['ptb'][:], float(l * cfg.NPOOL), ALU.add, 128.0, ALU.mult)
            c.ts('dve', N['idxA'][:], N['idxA'][:], piota[:, 0:1], ALU.add, 2.0, ALU.mult)
            c.ts('dve', N['idxB'][:], N['idxA'][:], 1.0, ALU.add)
            c.dma('pool', N['eall_s'][:], Dm['eall_s'][:])
            c.dma('sp', N['smask'][:], Dm['smask'][:])
            c.dma('sp', N['impm'][:], Dm['impm_s'][:])
            c.dma('sp', N['impb'][:], Dm['impb_s'][:])
            c.memset('pool', N['Ebig'][:], 0.0)
            c.memset('pool', N['VCs'][:], 1.0)
            c.memset('pool', N['vs1'][:], 1.0)
            c.memset('pool', N['vw1'][:], 1.0)
            c.memset('pool', N['v1n'][:], 1.0)
            proj_q(wA, N)
            c.copy('pool', N['qTs'][:].re('p s h t -> p h s t'), N['qT'][:].re('p h (s t) -> p h s t', s=NS))
            for col, dst in ((O_NKV + 256, N['kslcTn']), (O_NWIN, N['kwinTn'])):
                pb = c.bank()
                proj_chunk(wA, col, 128, pb)
                c.copy('act', dst[:], pb[:, 0:128])
            pbg = c.bank()
            for k in range(8):
                c.mm(pbg[:, 0:24], xT[:, k, :], wA[:, k, O_NG:O_NG + 24], start=(k == 0), stop=(k == 7))
            c.act(N['gate'][:], pbg[:, 0:24], AF.Sigmoid)

            def gather(idx, s_, h0, n8):
                for j in range(n8):
                    col = s_ * npg + h0 + j
                    def fn(e, j=j, col=col):
                        return e.indirect_dma_start(out=N['pk'].t[:, j, :], out_offset=None, in_=I['cache_kv'],
                                                    in_offset=bass.IndirectOffsetOnAxis(ap=idx.t[:, col:col + 1], axis=0))
                    c.dma_custom('pool', fn, [Dm['cache_kv'], idx], [N['pk']])

            def transpose_pages(src_of, dst, ntile):
                for j0 in range(0, ntile, 8):
                    n8 = min(8, ntile - j0)
                    pb = c.bank()
                    pbb = V(pb, pb.t[:].bitcast(BF16))
                    for j in range(j0, j0 + n8):
                        c.tr(pbb[:, (j - j0) * 128:(j - j0 + 1) * 128], src_of(j), identb[:], signal=(j == j0 + n8 - 1))
                    c.copy('act', dst[:, j0 * 128:(j0 + n8) * 128], pbb[:, 0:n8 * 128])

            def ocols(pbO, s_):
                return pbO[0:65, s_ * 32:(s_ + 1) * 32]

            def finalize_s(pbO, g, br, first):
                c.copy('act', N['osb'][:].re('p (h s t) -> p s h t', h=4, s=NS), pbO[0:65, :].re('p (s h t) -> p s h t', s=NS, h=4))
                pbt = c.bank()
                for hh in range(4):
                    c.tr(pbt[:, hh * 65:(hh + 1) * 65], N['osb'][:, hh * 128:(hh + 1) * 128], identf[0:65, 0:65])
                o3 = pbt[:, 0:260].re('p (h e) -> p h e', h=4)
                rs = N['rs']
                c.ts('dve', rs[:], o3[:, :, 64], 1e-30, ALU.max)
                c.op('dve', lambda e: e.reciprocal(rs.t[:], rs.t[:]), [rs], [rs])
                c.tt('dve', rs[:], rs[:], N['gate'][:, br * 8 + g * 4:br * 8 + g * 4 + 4], ALU.mult)
                dst = N['ytm'][:, g * 256:(g + 1) * 256].re('p (h d) -> p h d', h=4)
                rb = rs[:].re('p (h o) -> p h o', o=1).bc([128, 4, 64])
                if first:
                    c.tt('dve', dst, o3[:, :, 0:64], rb, ALU.mult)
                else:
                    c.tt('dve', N['tmp'][:], o3[:, :, 0:64], rb, ALU.mult)
                    c.tt('pool', dst, dst, N['tmp'][:], ALU.add)

            hx, h2, hb = N['hxs'], N['h2s'], N['hbs']
            hv = lambda t_: t_[:, :, 0:nblk]
            pbOc = [c.hold(), c.hold()]
            if os.environ.get('SWAPB'):
                pbOc = pbOc[::-1]
            for s_ in range(NS):
                for h0 in range(0, npg, 8):
                    n8 = min(8, npg - h0)
                    gather(N['idxA'], s_, h0, n8)
                    for a in range(2):
                        transpose_pages(lambda j: N['pk'][:, j, a * 128:(a + 1) * 128], N['cmpTs'][:, a, h0 * 128:(h0 + n8) * 128], n8)
                for a in range(2):
                    for g in range(2):
                        ps = slice(g * 64, (g + 1) * 64)
                        pbh = c.bank()
                        for l_ in range(32):
                            rhs = N['cmpTs'][ps, a, l_:l_ + 16 * (nblk - 1) + 1:16]
                            c.mm(pbh[0:64, 0:nblk], N['w1b'][ps, a * 32 + l_, :], rhs, start=(l_ == 0), stop=(l_ == 31))
                        c.ts('dve', hx[:, g, 0:nblk], pbh[0:64, 0:nblk], N['biasv'][:, a:a + 1], ALU.add)
                    c.tt('dve', hv(h2), hv(hx), hv(hx), ALU.mult)
                    c.ts('dve', hv(h2), hv(h2), 0.044715, ALU.mult, 1.0, ALU.add)
                    c.tt('dve', hv(h2), hv(h2), hv(hx), ALU.mult)
                    c.act(hv(h2), hv(h2), AF.Tanh, scale=0.7978845608028654)
                    c.ts('dve', hv(h2), hv(h2), 1.0, ALU.add, 0.5, ALU.mult)
                    c.tt('dve', hv(hb), hv(h2), hv(hx), ALU.mult)
                    if a == 0:
                        pbk = c.bank()
                        for g in range(2):
                            c.mm(pbk[:, g * 128:g * 128 + nblk], N['w2pad'][:, g, :], hb[:, g, 0:nblk])
                        for g in range(2):
                            c.copy('act', N['kcTs'][g * 64:(g + 1) * 64, 0:nblk], pbk[g * 64:(g + 1) * 64, g * 128:g * 128 + nblk])
                    else:
                        pbv = c.bank()
                        for g in range(2):
                            c.mm(pbv[0:nblk, g * 64:(g + 1) * 64], hb[:, g, 0:nblk], N['w2b'][:, 1, :])
                        c.copy('act', N['VCs'][0:nblk, :, 0:64], pbv[0:nblk, 0:128].re('p (g d) -> p g d', g=2))
                if os.environ.get('DBG') and s_ == NS - 1:
                    dst_ = N['dbgst']
                    c.memset('dve', dst_[:], 0.0)
                    c.copy('dve', dst_[:, 0:nblk], N['kcTs'][:, 0:nblk])
                    c.copy('dve', dst_[0:nblk, 128:258], N['VCs'][0:nblk, :, :].re('p g e -> p (g e)'))
                    c.copy('dve', dst_[0:64, 260:260 + 2 * nblk].re('p (g j) -> p g j', g=2), hb[:, :, 0:nblk])
                    c.copy('dve', dst_[:, 300:332], N['qTs'][:, s_, :, :].re('p h t -> p (h t)'))
                    c.copy('dve', dst_[:, 0:512], N['w1b'][:, 0:8, :].re('p a e -> p (a e)'))
                    c.dma('sp', Dm['dbg2'][:, 1280:1280 + 640], dst_[:])
                for g in range(2):
                    ps = slice(g * 64, (g + 1) * 64)
                    qsel = N['qTs'][ps, s_, :, :].re('p h t -> p (h t)')
                    pbS = c.bank()
                    c.mm(pbS[0:nblk, 0:32], N['kcTs'][ps, 0:nblk], qsel)
                    c.act(N['Ecur'][0:nblk, :], pbS[0:nblk, 0:32], AF.Exp, scale=0.125)
                    c.copy('pool', N['Ebig'][0:nblk, g, :, s_ * 8:(s_ + 1) * 8], N['Ecur'][0:nblk, :].re('p (h t) -> p h t', h=4))
                    c.mm(ocols(pbOc[g], s_), N['VCs'][0:nblk, g, :], N['Ecur'][0:nblk, :])
            for g in range(2):
                pbI = c.bank()
                for hh in range(4):
                    c.mm(pbI[:, hh * 65:(hh + 1) * 65], N['Ebig'][:, g, hh, :], N['ov'][:, 0, :], start=(hh == 0), stop=(hh == 3))
                i3 = pbI[:, 0:260].re('p (h e) -> p h e', h=4)
                rs2, imp = N['rs2'], N['imp']
                c.ts('dve', rs2[:], i3[:, :, 64], 1e-30, ALU.max)
                c.op('dve', lambda e: e.reciprocal(rs2.t[:], rs2.t[:]), [rs2], [rs2])
                c.ts('dve', imp[:], i3[:, 0, 0:64], rs2[:, 0:1], ALU.mult)
                for hh in range(1, 4):
                    c.stt(imp[:], i3[:, hh, 0:64], rs2[:, hh:hh + 1], imp[:], ALU.mult, ALU.add)
                c.tt('dve', imp[:], imp[:], N['impm'][:], ALU.mult)
                c.tt('dve', imp[:], imp[:], N['impb'][:], ALU.add)
                c.op('dve', lambda e: e.max(N['m8'].t[:], imp.t[:]), [imp], [N['m8']])
                c.ts('dve', N['sel'][:], imp[:], N['m8'][:, 7:8], ALU.is_ge)
                pbT = c.bank()
                c.tr(pbT[0:64, 0:128], N['sel'][:], identf[:])
                c.copy('act', N['selTs'][:, g, :], pbT[0:64, 0:128])
                finalize_s(pbOc[g], g, 0, True)
                c.release(pbOc[g])
                if os.environ.get('DBG'):
                    c.dma('sp', Dm['dbg2'][:, g * 256:(g + 1) * 256], N['ytm'][:, g * 256:(g + 1) * 256])
                    c.dma('sp', Dm['dbg2'][:, 1024 + g * 64:1024 + (g + 1) * 64], N['sel'][:])
            pbOs = [c.hold(), c.hold()]
            pbOw = [c.hold(), c.hold()]
            kTs = N['cmpTs'][:, 0, :]
            for s_ in range(NS):
                for h0 in range(0, npg, 8):
                    n8 = min(8, npg - h0)
                    gather(N['idxB'], s_, h0, n8)
                    transpose_pages(lambda j: N['pk'][:, j, 0:128], kTs[:, h0 * 128:(h0 + n8) * 128], n8)
                    c.copy('pool', N['vs1'][:, h0:h0 + n8, :, 0:64], N['pk'][:, 0:n8, 128:256].re('p j (g d) -> p j g d', g=2))
                c.dma('pool', N['pw'][:], Dm['cache_win'][l, s_].re('(j r) f -> r j f', r=128))
                transpose_pages(lambda j: N['pw'][:, j, 0:128], N['kwTs'][:, :], nwt)
                c.copy('pool', N['vw1'][:, :, :, 0:64], N['pw'][:, :, 128:256].re('p j (g d) -> p j g d', g=2))
                pbn = c.bank()
                c.mm(pbn[0:8, 0:128], identf[:, s_ * 8:(s_ + 1) * 8], kvtm[:, 384:512])
                c.mm(pbn[0:8, 128:256], identf[:, s_ * 8:(s_ + 1) * 8], kvtm[:, 640:768])
                c.copy('act', N['v1n'][:, :, :, 0:64], pbn[0:8, 0:256].re('p (b g d) -> p b g d', b=2, g=2))
                for g in range(2):
                    ps = slice(g * 64, (g + 1) * 64)
                    qsel = N['qTs'][ps, s_, :, :].re('p h t -> p (h t)')
                    for br, (KT_, V1_, KTn, ntile, pbO) in enumerate(((kTs, N['vs1'], N['kslcTn'], npg, pbOs[g]),
                                                                      (N['kwTs'][:, :], N['vw1'], N['kwinTn'], nwt, pbOw[g]))):
                        first = True
                        for j0 in range(0, ntile, 8):
                            n8 = min(8, ntile - j0)
                            pbS = c.bank()
                            for j in range(j0, j0 + n8):
                                c.mm(pbS[:, (j - j0) * 32:(j - j0 + 1) * 32], KT_[ps, j * 128:(j + 1) * 128], qsel)
                            Es = N['Es']
                            c.act(Es[:, 0:n8 * 32], pbS[:, 0:n8 * 32], AF.Exp, scale=0.125)
                            E4 = Es[:, 0:n8 * 32].re('p (j h t) -> p j h t', j=n8, h=4)
                            if br == 0:
                                pbM = c.bank()
                                for j in range(j0, j0 + n8):
                                    c.mm(pbM[:, (j - j0) * 8:(j - j0 + 1) * 8], N['eall_s'][:, j * 128:(j + 1) * 128], N['selTs'][:, g, s_ * 8:(s_ + 1) * 8])
                                c.tt('dve', E4, E4, pbM[:, 0:n8 * 8].re('p (j o t) -> p j o t', j=n8, o=1).bc([128, n8, 4, 8]), ALU.mult)
                            elif j0 == 0 and WS == 512:
                                c.tt('pool', E4[:, 0, :, :], E4[:, 0, :, :], N['smask'][:, 0:8].re('p (o t) -> p o t', o=1).bc([128, 4, 8]), ALU.mult)
                            for j in range(j0, j0 + n8):
                                c.mm(ocols(pbO, s_), V1_[:, j, g, :], Es[:, (j - j0) * 32:(j - j0 + 1) * 32], start=first, stop=False)
                                first = False
                        pbS = c.bank()
                        c.mm(pbS[0:8, 0:32], KTn[ps, s_ * 8:(s_ + 1) * 8], qsel)
                        En = N['En']
                        c.act(En[:], pbS[0:8, 0:32].re('p (h t) -> p h t', h=4), AF.Exp, scale=0.125)
                        c.tt('pool', En[:], En[:], N['smask'][0:8, 8:16].re('p (o t) -> p o t', o=1).bc([8, 4, 8]), ALU.mult)
                        c.mm(ocols(pbO, s_), N['v1n'][:, br, g, :], En[:].re('p h t -> p (h t)'), start=False, stop=True)
            for g in range(2):
                finalize_s(pbOs[g], g, 1, False)
                c.release(pbOs[g])
                if os.environ.get('DBG'):
                    c.dma('sp', Dm['dbg2'][:, 512 + g * 256:512 + (g + 1) * 256], N['ytm'][:, g * 256:(g + 1) * 256])
                finalize_s(pbOw[g], g, 2, False)
                c.release(pbOw[g])
            pb = c.bank()
            for cc in range(4):
                c.tr(pb[:, cc * 128:(cc + 1) * 128], N['ytm'][:, cc * 128:(cc + 1) * 128], identf[:])
            c.copy('act', yT[:, 4:8, :], pb[:].re('p (a t) -> p a t', a=4))

        c.dma('sp', lnG[:], V(Dm['ln_emb'], I['ln_emb'].partition_broadcast(128)))
        tiles = list(range(NTILES + 1))

        def xsrc(i):
            return Dm['x_p'][i * 128:(i + 1) * 128, :] if i < NTILES else Dm['x_s'][:, :]

        for i in tiles:
            b = xt[i % 2]
            c.dma('sp', b[:], xsrc(i))
            layer_norm(b[:], xr[:], lnG)
            c.dma('sp', XS0[i][:], xr[:])

        for l in range(L):
            with ExitStack() as ph:
                wA = c.sb([128, 8, NIN], BF16, 'wA', ph)
                wO = c.sb([128, 8, D], BF16, 'wO', ph)
                yT = c.sb([128, 8, 128], BF16, 'yT', ph)
                cwa = c.sb([128, 2, 3], F32, 'cwa', ph); cwg = c.sb([128, 6, 4], F32, 'cwg', ph)
                CH = {}
                qkvT = c.sb([128, 6, 128], F32, 'qkvT', ph)
                kvtm = c.sb([128, 768], F32, 'kvtm', ph)
                G = {}
                N = {}
                if cfg.mix_b:
                    for nm, shp in (('cg', [128, 19, 128]), ('rowm', [128, 16]), ('qkv_tm', [128, 768]),
                                    ('ab', [128, 8]), ('gate', [128, 256]), ('t4', [128, 4]), ('g4', [128, 4]), ('beta4', [128, 4]),
                                    ('gc4', [128, 4]), ('egc4', [128, 4]), ('egd4', [128, 4]), ('bgc4', [128, 4]), ('ss', [128, 8]),
                                    ('dtb', [128, 4]), ('negA', [128, 4]), ('gain', [128, 64]), ('sc', [128, 4, 64]),
                                    ('kqT', [64, 3, 128]), ('dg', [128, 2, 128]), ('t1', [128, 128]), ('t2', [128, 128]), ('t3', [128, 128]),
                                    ('A', [128, 128]), ('AT', [128, 128]), ('AqkT', [128, 128]), ('X', [128, 128]), ('XT', [128, 128]),
                                    ('D0', [128, 128]), ('D1', [128, 128]), ('DT0', [128, 128]), ('DT1', [128, 128]),
                                    ('Ys', [128, 128]), ('Y2s', [128, 128]), ('negWT', [64, 128]),
                                    ('Vnew', [128, 64]), ('o_tm', [128, 256]), ('egl', [64, 16]), ('grow', [128, 16]), ('rr', [128, 4])):
                        G[nm] = c.sb(shp, F32, nm, ph)
                    G['hset0'] = {nm: G[nm] for nm in ['kqT', 't1', 't2', 'A', 'AT', 'AqkT', 'X', 'XT', 'D0', 'D1', 'DT0', 'DT1', 'Ys', 'Y2s']}
                    c.dma('sp', G['dtb'][:], V(Dm['gdn_dt_bias'], I['gdn_dt_bias'][l].partition_broadcast(128)))
                    c.dma('sp', G['negA'][:], V(Dm['gdn_a_log'], I['gdn_a_log'][l].partition_broadcast(128)))
                    c.dma('sp', G['gain'][:], V(Dm['gdn_norm_g'], I['gdn_norm_g'][l].partition_broadcast(128)))
                    c.act(G['negA'][:], G['negA'][:], AF.Exp)
                    c.ts('dve', G['negA'][:], G['negA'][:], -1.0, ALU.mult)
                if cfg.mix_c:
                    for nm, shp, dt_ in (('w1b', [128, 64, 64], BF16), ('w2b', [64, 2, 64], BF16), ('w2pad', [64, 2, 128], BF16),
                                         ('peT', [64, 2, 32], F32), ('peTb', [64, 2, 32], BF16), ('b1T', [64, 2], F32), ('biasv', [64, 2], F32),
                                         ('ov', [128, 2, 65], BF16), ('cq', [128, 128], F32), ('qT', [128, 4, 128], BF16),
                                         ('cmpT', [128, 2, 144], BF16), ('hx', [64, 16], F32), ('h2', [64, 16], F32), ('hb', [64, 16], BF16),
                                         ('kcT', [128, 256], BF16), ('hidvT', [64, 2, 256], BF16), ('VC1', [128, 2, 2, 65], BF16),
                                         ('gate', [128, 24], F32), ('impm', [128, 64], F32), ('impb', [128, 64], F32), ('imp', [128, 64], F32),
                                         ('sel', [128, 64], F32), ('m8', [128, 8], F32), ('selT', [64, 128], BF16), ('rs', [128, 4], F32),
                                         ('rs2', [128, 4], F32), ('ytm', [128, 512], F32), ('tmp', [128, 4, 64], F32), ('cm', [128, 128], F32),
                                         ('E0', [128, 4, 128], BF16), ('E1', [128, 4, 128], BF16)):
                        N[nm] = c.sb(shp, dt_, 'n_' + nm, ph)
                    nsa_setup(l, N)
                WP, WS = min(512, T), min(512, cfg.PAST)
                c.dma('sp', Dm['win_s'][l, :, 0:WS - TS, :], Dm['cache_win'][l, :, TS:WS, :])
                for k in range(8):
                    c.dma('pool', wA[:, k, :], Dm['w_in'][l, k * 128:(k + 1) * 128, :])
                    c.dma('pool', wO[:, k, :], Dm['w_out'][l, k * 128:(k + 1) * 128, :])
                c.dma('sp', lnG[:], V(Dm['ln1'], I['ln1'][l].partition_broadcast(128)))
                for tap in range(3):
                    c.dma('sp', cwa[:, :, tap], Dm['conv_a_w'][l, tap].re('(c p) -> p c', p=128), slow=True)
                for tap in range(4):
                    c.dma('sp', cwg[:, :, tap], Dm['gdn_conv_w'][l, tap].re('(c p) -> p c', p=128), slow=True)

                def run_tile(i):
                    samp = (i == NTILES)
                    nseq, TT = (NS, TS) if samp else (1, 128)
                    chA = CH['A']
                    xb = xt[i % 2]
                    c.dma('sp', xb[:], XS0[i][:])
                    make_xT(xb[:])
                    if samp:
                        load_state_T(lambda ch: chA[:, ch, :, 0:2], 2, Dm['st_conv_a'][l], NS * 2)
                    elif i == 0:
                        c.memset('pool', chA[:, :, :, 0:2], 0.0)
                    for ch in range(2):
                        z1 = znext(); z2 = znext()
                        pb = c.bank()
                        proj_chunk(wA, O_AC + ch * 128, 128, pb)
                        c.copy('act', z1[:], pb[:, 0:128])
                        pb2 = c.bank()
                        proj_chunk(wA, O_AH + ch * 128, 128, pb2)
                        c.tt('dve', chA[:, ch, :, 2:2 + TT], v3(z1[:], nseq), v3(pb2[:, 0:128], nseq), ALU.mult)
                        conv_taps(v3(z2[:], nseq), chA, ch, cwa, 3, TT)
                        pb3 = c.bank()
                        proj_chunk(wA, O_AB + ch * 128, 128, pb3)
                        c.tt('dve', yT[:, ch, :], z2[:], pb3[:, 0:128], ALU.mult)
                    if samp:
                        store_state_T(lambda ch: chA[:, ch, :, TT:TT + 2], 2, Dm['ca_s'][l], NS * 2)
                    elif i == NTILES - 1:
                        store_state_T(lambda ch: chA[:, ch, :, TT:TT + 2], 2, Dm['ca_p'][l], 2)
                    if not samp:
                        c.copy('pool', chA[:, :, :, 0:2], chA[:, :, :, TT:TT + 2])
                    pb = c.bank()
                    for k in range(8):
                        c.mm(pb[:, :], xT[:, k, :], wA[:, k, O_NKV:O_NKV + 512], start=(k == 0), stop=(k == 7))
                    c.copy('act', kvtm[:, 0:512], pb[:, :])
                    pb = c.bank()
                    for k in range(8):
                        c.mm(pb[:, 0:256], xT[:, k, :], wA[:, k, O_NWIN:O_NWIN + 256], start=(k == 0), stop=(k == 7))
                    c.copy('act', kvtm[:, 512:768], pb[:, 0:256])
                    if samp:
                        c.dma('sp', Dm['kv_s'][l], kvtm[:, 0:512])
                        for sq in range(NS):
                            c.dma('sp', Dm['win_s'][l, sq, WS - TS:WS, :], kvtm[sq * TS:(sq + 1) * TS, 512:768])
                    else:
                        c.dma('sp', Dm['kv_p'][l, i * 128:(i + 1) * 128, :], kvtm[:, 0:512])
                        if i * 128 >= T - WP:
                            r0 = i * 128 - (T - WP)
                            c.dma('sp', Dm['win_p'][l, r0:r0 + 128, :], kvtm[:, 512:768])
                    chG = CH['G']
                    if samp:
                        load_state_T(lambda ch: chG[:, ch, :, 0:3], 6, Dm['st_gdn_conv'][l], NS * 3)
                    elif i == 0:
                        c.memset('pool', chG[:, :, :, 0:3], 0.0)
                    for ch in range(6):
                        z2 = znext()
                        pb = c.bank()
                        proj_chunk(wA, O_GQ + ch * 128, 128, pb)
                        c.copy('act', chG[:, ch, :, 3:3 + TT], v3(pb[:, 0:128], nseq))
                        conv_taps(v3(z2[:], nseq), chG, ch, cwg, 4, TT)
                        c.act(qkvT[:, ch, :], z2[:], AF.Silu)
                    if samp:
                        store_state_T(lambda ch: chG[:, ch, :, TT:TT + 3], 6, Dm['gc_s'][l], NS * 3)
                    elif i == NTILES - 1:
                        store_state_T(lambda ch: chG[:, ch, :, TT:TT + 3], 6, Dm['gc_p'][l], 3)
                    if not samp:
                        c.copy('pool', chG[:, :, :, 0:3], chG[:, :, :, TT:TT + 3])
                    if cfg.mix_b:
                        gdn_tile(l, i, samp, G, wA, qkvT, yT)
                    else:
                        c.memset('dve', yT[:, 2:4, :], 0.0)
                    if cfg.mix_c == 1 and samp:
                        nsa_tile_s(l, N, G, wA, kvtm, yT)
                    elif cfg.mix_c and not samp and not os.environ.get('NSA_SKIP_TILE'):
                        nsa_tile_p(l, i, N, G, wA, kvtm, yT)
                    else:
                        c.memset('dve', yT[:, 4:8, :], 0.0)
                    if samp and os.environ.get('DBG'):
                        c.dma('sp', Dm['dbg3'][:, :], yT[:].re('p k t -> p (k t)'))
                    for h in range(2):
                        pb = c.bank()
                        for k in range(8):
                            c.mm(pb[:, :], yT[:, k, :], wO[:, k, h * 512:(h + 1) * 512], start=(k == 0), stop=(k == 7))
                        c.stt(xr[:, h * 512:(h + 1) * 512], xb[:, h * 512:(h + 1) * 512], ALPHA, pb[:, :], ALU.mult, ALU.add)
                    if samp and os.environ.get('DBG'):
                        pass
                    layer_norm(xr[:], xr[:], lnG)
                    c.dma('sp', XS1[i][:], xr[:])

                with ExitStack() as sub:
                    CH['A'] = c.sb([128, 2, 1, 2 + 128], F32, 'chA_p', sub); CH['G'] = c.sb([128, 6, 1, 3 + 128], F32, 'chG_p', sub)
                    if cfg.mix_b:
                        G['S'] = c.sb([64, 1, 4, 64], F32, 'S_p', sub)
                        G['hsets'] = [G['hset0']]
                        for k_ in range(1, NHSETS):
                            G['hsets'].append({nm: c.sb([64, 3, 128] if nm == 'kqT' else [128, 128], F32, nm + 'x', sub) for nm in ['kqT', 't1', 't2', 'A', 'AT', 'AqkT', 'X', 'XT', 'D0', 'D1', 'DT0', 'DT1', 'Ys', 'Y2s']})
                    if cfg.mix_c:
                        N['kslcT'] = c.sb([128, T], BF16, 'kslcT', sub)
                        N['vslc1'] = c.sb([128, NTILES, 2, 65], BF16, 'vslc1', sub)
                        N['eall'] = c.sb([64, T], BF16, 'eall', sub)
                        N['kwinT'] = c.sb([128, 1024], BF16, 'kwinT', sub)
                        N['vwin1'] = c.sb([128, 8, 2, 65], BF16, 'vwin1', sub)
                        c.memset('pool', N['vwin1'][:], 1.0)
                        c.dma('pool', N['eall'][:], Dm['eall'][:])
                        c.memset('pool', N['vslc1'][:], 1.0)
                    for i in range(NTILES):
                        run_tile(i)
                    c.barrier()
                with ExitStack() as sub:
                    CH['A'] = c.sb([128, 2, NS, 2 + TS], F32, 'chA_s', sub); CH['G'] = c.sb([128, 6, NS, 3 + TS], F32, 'chG_s', sub)
                    if cfg.mix_b:
                        G['hsets'] = [G['hset0']]
                        G['S'] = c.sb([64, 16, 1, 64], F32, 'S_s', sub)
                        for nm, shp, dt_ in (('colm', [64, 16, 128], BF16), ('negWTm', [64, 4, 128], F32), ('QgTm', [64, 4, 128], F32), ('Kdm', [128, 8, 64], F32)):
                            G[nm] = c.sb(shp, dt_, nm, sub)
                    if cfg.mix_c == 1:
                        NPG = cfg.PAST // 128
                        WS_ = min(512, cfg.PAST)
                        for nm, shp, dt_ in (('ptb', [128, NS * NPG], I32), ('idxA', [128, NS * NPG], I32), ('idxB', [128, NS * NPG], I32),
                                             ('eall_s', [64, cfg.PAST + 128], BF16), ('smask', [128, 16], F32), ('Ebig', [128, 2, 4, 128], BF16),
                                             ('VCs', [128, 2, 65], BF16), ('vs1', [128, NPG, 2, 65], BF16), ('vw1', [128, WS_ // 128, 2, 65], BF16),
                                             ('v1n', [8, 2, 2, 65], BF16), ('kslcTn', [128, 128], BF16), ('qTs', [128, NS, 4, TS], BF16), ('Ecur', [128, 32], BF16), ('kwinTn', [128, 128], BF16),
                                             ('pk', [128, min(NPG, 8), 256], BF16), ('cmpTs', [128, 2, cfg.PAST], BF16), ('pw', [128, WS_ // 128, 256], BF16),
                                             ('kwTs', [128, WS_], BF16), ('hxs', [64, 2, 128], F32), ('h2s', [64, 2, 128], F32), ('hbs', [64, 2, 128], BF16),
                                             ('kcTs', [128, 128], BF16), ('selTs', [64, 2, 128], BF16), ('osb', [65, 512], F32),
                                             ('Es', [128, 256], BF16), ('En', [8, 4, 8], BF16), ('dbgst', [128, 640], F32)):
                            N[nm] = c.sb(shp, dt_, 'ns_' + nm, sub)
                    run_tile(NTILES)
                    c.barrier()

            with ExitStack() as ph:
                wU = c.sb([128, 8, 2 * DFF], BF16, 'wU', ph)
                wD = c.sb([128, 22, D], BF16, 'wD', ph)
                cwf = c.sb([128, 22, 3], F32, 'cwf', ph)
                chFb = c.sb([128, 22, NS * (2 + TS)], F32, 'chF', ph)
                actT = c.sb([128, 22, 128], BF16, 'actT', ph)
                chF_p = chFb[:, :, 0:130].re('p c (s t) -> p c s t', s=1)
                chF_s = chFb[:, :, :].re('p c (s t) -> p c s t', s=NS)
                for k in range(8):
                    c.dma('pool', wU[:, k, :], Dm['w_up'][l, k * 128:(k + 1) * 128, :])
                for k in range(22):
                    c.dma('pool', wD[:, k, :], Dm['w_down'][l, k * 128:(k + 1) * 128, :])
                c.dma('sp', lnG[:], V(Dm['ln2'], I['ln2'][l].partition_broadcast(128)))
                for tap in range(3):
                    c.dma('sp', cwf[:, :, tap], Dm['ffn_conv_w'][l, tap].re('(c p) -> p c', p=128), slow=True)
                for i in tiles:
                    samp = (i == NTILES)
                    nseq, TT = (NS, TS) if samp else (1, 128)
                    chF = chF_s if samp else chF_p
                    xb = xt[i % 2]
                    c.dma('sp', xb[:], XS1[i][:])
                    make_xT(xb[:])
                    if samp:
                        load_state_T(lambda ch: chF[:, ch, :, 0:2], 22, Dm['st_ffn_conv'][l], NS * 2)
                    elif i == 0:
                        c.memset('pool', chF[:, :, :, 0:2], 0.0)
                    for ch in range(22):
                        z1 = znext(); z2 = znext()
                        pb = c.bank()
                        proj_chunk(wU, ch * 128, 128, pb)
                        c.copy('act', chF[:, ch, :, 2:2 + TT], v3(pb[:, 0:128], nseq))
                        conv_taps(v3(z2[:], nseq), chF, ch, cwf, 3, TT)
                        c.act(z1[:], z2[:], AF.Silu)
                        pb2 = c.bank()
                        proj_chunk(wU, DFF + ch * 128, 128, pb2)
                        c.tt('dve', actT[:, ch, :], z1[:], pb2[:, 0:128], ALU.mult)
                    if samp:
                        store_state_T(lambda ch: chF[:, ch, :, TT:TT + 2], 22, Dm['fc_s'][l], NS * 2)
                    elif i == NTILES - 1:
                        store_state_T(lambda ch: chF[:, ch, :, TT:TT + 2], 22, Dm['fc_p'][l], 2)
                    if not samp:
                        c.copy('pool', chF[:, :, :, 0:2], chF[:, :, :, TT:TT + 2])
                    for h in range(2):
                        pb = c.bank()
                        for k in range(22):
                            c.mm(pb[:, :], actT[:, k, :], wD[:, k, h * 512:(h + 1) * 512], start=(k == 0), stop=(k == 21))
                        c.stt(xr[:, h * 512:(h + 1) * 512], xb[:, h * 512:(h + 1) * 512], ALPHA, pb[:, :], ALU.mult, ALU.add)
                    layer_norm(xr[:], xr[:], lnG)
                    if l == L - 1:
                        c.dma('sp', (Dm['y_p'][i * 128:(i + 1) * 128, :] if not samp else Dm['y_s'][:, :]), xr[:])
                    else:
                        c.dma('sp', XS0[i][:], xr[:])
                c.barrier()
        c.finish()
        print("instructions:", c.ninstr)
    return nc


OUT_NAMES = ['y_p', 'y_s', 'kv_p', 'kv_s', 'win_p', 'win_s', 'ca_p', 'ca_s', 'gc_p', 'gc_s', 'gs_p', 'gs_s', 'fc_p', 'fc_s']


def make_in_maps(cfg, inp, ncores=8):
    L, NS = cfg.L, cfg.NS
    nb = inp['x_prompt'].shape[0]
    f = np.ascontiguousarray
    shared = {
        'ln_emb': f(np.stack([inp['ln_emb_g'], inp['ln_emb_b']])),
        'w_in': f(inp['w_in']), 'w_out': f(inp['w_out']), 'w_up': f(inp['w_up']), 'w_down': f(inp['w_down']),
        'conv_a_w': f(inp['conv_a_w']), 'gdn_conv_w': f(inp['gdn_conv_w']), 'ffn_conv_w': f(inp['ffn_conv_w']),
        'gdn_a_log': f(inp['gdn_a_log']), 'gdn_dt_bias': f(inp['gdn_dt_bias']), 'gdn_norm_g': f(inp['gdn_norm_g']),
        'cmp_pe': f(inp['cmp_pe']), 'cmp_w1': f(inp['cmp_w1']), 'cmp_b1': f(inp['cmp_b1']), 'cmp_w2': f(inp['cmp_w2']),
        'ln1': f(np.stack([inp['ln1_g'], inp['ln1_b']], axis=1)), 'ln2': f(np.stack([inp['ln2_g'], inp['ln2_b']], axis=1)),
    }
    for v_, (ns_, ts_) in (('p', (1, 128)), ('s', (cfg.NS, cfg.TS))):
        cg, rowm, colm = gdn_consts(ns_, ts_)
        shared['cg_' + v_], shared['rowm_' + v_], shared['colm_' + v_] = cg, rowm, colm
    shared['ov'], shared['cq'], shared['eall'], shared['impm'], shared['impb'] = nsa_consts(cfg)
    P_ = cfg.PAST
    n_ = np.arange(64)
    shared['eall_s'] = (np.arange(P_ + 128)[None, :] // 64 == n_[:, None]).astype(np.float32)
    qpos = P_ + (np.arange(128) % cfg.TS)
    valid = (n_[None, :] * 64 <= qpos[:, None]); forced = (n_[None, :] == 0) | (n_[None, :] == qpos[:, None] // 64)
    shared['impm_s'] = (valid & ~forced).astype(np.float32)
    shared['impb_s'] = np.where(forced, 1e9, np.where(valid, 0.0, -1e9)).astype(np.float32)
    sm = np.zeros((128, 16), np.float32)
    sm[:, 0:8] = (np.arange(128)[:, None] > np.arange(8)[None, :])
    sm[0:8, 8:16] = (np.arange(8)[:, None] <= np.arange(8)[None, :])
    shared['smask'] = sm
    shared['cache_kv'] = f(inp['cache_nsa_kv']).reshape(-1, 256)
    maps = []
    for i in range(ncores):
        sl = slice(NS * i, NS * (i + 1))
        m = dict(shared)
        m['x_p'] = f(inp['x_prompt'][i % nb])
        m['x_s'] = f(inp['x_sample'][sl].reshape(128, D))
        m['st_conv_a'] = f(inp['state_conv_a'][:, sl].reshape(L, NS * 2, 256))
        m['page_table'] = f(inp['page_table'][sl].astype(np.int32).reshape(1, -1))
        m['cache_win'] = f(inp['cache_nsa_win'][:, sl].reshape(L, NS, -1, 256))
        m['st_gdn_conv'] = f(inp['state_gdn_conv'][:, sl].reshape(L, NS * 3, 768))
        m['st_gdn'] = f(inp['state_gdn'][:, sl])
        m['st_ffn_conv'] = f(inp['state_ffn_conv'][:, sl].reshape(L, NS * 2, DFF))
        maps.append(m)
    return maps


def gather(cfg, res, ncores=8, nb=4):
    L, NS, T = cfg.L, cfg.NS, cfg.T
    WP = min(512, T)
    WS = min(512, cfg.PAST)
    r = res

    def P(name, shp):
        return np.stack([r[i][name].reshape((L,) + shp) for i in range(nb)], axis=1)

    def S(name, shp):
        return np.concatenate([r[i][name].reshape((L, NS) + shp) for i in range(ncores)], axis=1)

    y_p = np.stack([r[i]['y_p'] for i in range(nb)], axis=0)
    y_s = np.concatenate([r[i]['y_s'].reshape(NS, cfg.TS, D) for i in range(ncores)], axis=0)
    return (y_p, y_s,
            P('kv_p', (T, 4, 2, 64)), S('kv_s', (cfg.TS, 4, 2, 64)),
            P('win_p', (WP, 2, 2, 64)), S('win_s', (WS, 2, 2, 64)),
            P('ca_p', (2, 256)), S('ca_s', (2, 256)),
            P('gc_p', (3, 768)), S('gc_s', (3, 768)),
            P('gs_p', (4, 64, 64)), S('gs_s', (4, 64, 64)),
            P('fc_p', (2, DFF)), S('fc_s', (2, DFF)))


def kernel(**inputs):
    inp = {k: np.asarray(v) for k, v in inputs.items()}
    cfg = Cfg()
    nc = build(cfg)
    maps = make_in_maps(cfg, inp, 8)
    res = run_bass_kernel_spmd(nc, maps, core_ids=list(range(8)))
    outs = gather(cfg, res.results, 8, 4)
    return tuple(np.ascontiguousarray(o.astype(np.float32)) for o in outs)
```
